# Optimizing a Trainium2 kernel written in Bass

```python
import jax, jax.numpy as jnp
from jax import lax
import numpy as np

D_MODEL = 1024
BATCH = 4
SEQ = 8192
DEPTH = 1

HEAD_DIM = 64
ATTN_PATTERNS = ((128, 1), (512, 4), (2048, 16))
N_PATTERNS = 3
HEADS_PER_PATTERN = 4
N_ATTN_HEADS = N_PATTERNS * HEADS_PER_PATTERN
ATTN_WIDTH = N_ATTN_HEADS * HEAD_DIM
ATTN_OUT_WIDTH = HEADS_PER_PATTERN * HEAD_DIM
N_KEYS = ATTN_PATTERNS[0][0] // ATTN_PATTERNS[0][1] + 1
Q_BLOCK = 128
REL_BUCKETS = 32
REL_MAX_DIST = 2048
SSM_GROUP = 16
SSM_STATE = 64
SSM_WIDTH = 512
SSM_GROUPS = SSM_WIDTH // SSM_GROUP
DT_MIN = 1e-3
DT_MAX = 1e-1
MEM_LEN = 256
XATTN_HEADS = 4
XATTN_HEAD_DIM = 128
XATTN_WIDTH = XATTN_HEADS * XATTN_HEAD_DIM
N_BRANCHES = 3
IN_WIDTH = SSM_WIDTH + 3 * ATTN_WIDTH + XATTN_WIDTH + N_BRANCHES * D_MODEL
D_FF = 2816
EPS = 1e-6

kernel_name = "hybrid_s5_dilated_attn_gated_block"


def rmsnorm(x, g):
    xf = x.astype(jnp.float32)
    y = xf * lax.rsqrt(jnp.mean(xf * xf, axis=-1, keepdims=True) + EPS)
    return (y * g.astype(jnp.float32)).astype(x.dtype)


def swiglu_ffn(x, w_in, w_down):
    a, b = jnp.split(x @ w_in, 2, axis=-1)
    return (jax.nn.silu(a) * b) @ w_down


def t5_causal_buckets(dist):
    dist = np.asarray(dist, np.int32)
    max_exact = REL_BUCKETS // 2
    safe = np.maximum(dist, 1).astype(np.float32)
    large = max_exact + (np.log(safe / max_exact) / np.log(REL_MAX_DIST / max_exact)
                         * (REL_BUCKETS - max_exact)).astype(np.int32)
    large = np.minimum(large, REL_BUCKETS - 1)
    return np.where(dist < max_exact, dist, large).astype(np.int32)


def pattern_offsets():
    return np.stack([np.arange(N_KEYS, dtype=np.int32) * d for (_, d) in ATTN_PATTERNS])


def s5_scan(u, a_re, a_im, log_dt, b_re, b_im, c_re, c_im, d_skip):
    bsz, seqlen, _ = u.shape
    f32 = jnp.float32
    uf = u.astype(f32).reshape(bsz, seqlen, SSM_GROUPS, SSM_GROUP)
    a_re = a_re.astype(f32); a_im = a_im.astype(f32)
    dt = jnp.exp(log_dt.astype(f32))[:, None]
    mag = jnp.exp(a_re * dt)
    ang = a_im * dt
    abar_re = mag * jnp.cos(ang)
    abar_im = mag * jnp.sin(ang)
    nr = abar_re - 1.0
    ni = abar_im
    den = a_re * a_re + a_im * a_im
    coef_re = (nr * a_re + ni * a_im) / den
    coef_im = (ni * a_re - nr * a_im) / den
    b_re = b_re.astype(f32); b_im = b_im.astype(f32)
    bbar_re = coef_re[..., None] * b_re - coef_im[..., None] * b_im
    bbar_im = coef_re[..., None] * b_im + coef_im[..., None] * b_re
    bu_re = jnp.einsum('blgh,gph->blgp', uf, bbar_re)
    bu_im = jnp.einsum('blgh,gph->blgp', uf, bbar_im)
    a_seq_re = jnp.broadcast_to(abar_re[None, None], (1, seqlen, SSM_GROUPS, SSM_STATE))
    a_seq_im = jnp.broadcast_to(abar_im[None, None], (1, seqlen, SSM_GROUPS, SSM_STATE))

    def combine(e1, e2):
        a1r, a1i, b1r, b1i = e1
        a2r, a2i, b2r, b2i = e2
        return (a2r * a1r - a2i * a1i,
                a2r * a1i + a2i * a1r,
                a2r * b1r - a2i * b1i + b2r,
                a2r * b1i + a2i * b1r + b2i)

    _, _, x_re, x_im = lax.associative_scan(combine, (a_seq_re, a_seq_im, bu_re, bu_im), axis=1)
    y = (jnp.einsum('blgp,ghp->blgh', x_re, c_re.astype(f32))
         - jnp.einsum('blgp,ghp->blgh', x_im, c_im.astype(f32))
         + uf * d_skip.astype(f32))
    return y.reshape(bsz, seqlen, SSM_WIDTH).astype(u.dtype)


def dilated_mixture_attention(q, k, v, rel_table):
    bsz, seqlen = q.shape[0], q.shape[1]
    f32 = jnp.float32
    offsets = pattern_offsets()
    buckets = t5_causal_buckets(offsets)
    table = rel_table.astype(f32).reshape(REL_BUCKETS, N_PATTERNS, HEADS_PER_PATTERN)
    grp = np.arange(N_PATTERNS)[:, None]
    bias = jnp.transpose(table[buckets, grp], (0, 2, 1))
    grp3 = np.arange(N_PATTERNS)[:, None, None]
    scale = HEAD_DIM ** -0.5

    def block(start):
        qb = lax.dynamic_slice_in_dim(q, start, Q_BLOCK, axis=1).astype(f32)
        pos = start + jnp.arange(Q_BLOCK, dtype=jnp.int32)
        idx = pos[None, :, None] - offsets[:, None, :]
        valid = idx >= 0
        idx = jnp.maximum(idx, 0)
        kb = k[:, idx, grp3].astype(f32)
        vb = v[:, idx, grp3].astype(f32)
        logits = jnp.einsum('bqghd,bgqkhd->bghqk', qb, kb) * scale + bias[None, :, :, None, :]
        logits = jnp.where(valid[None, :, None], logits, -jnp.inf)
        lse = jax.nn.logsumexp(logits, axis=-1)
        probs = jnp.exp(logits - lse[..., None])
        out = jnp.einsum('bghqk,bgqkhd->bqghd', probs, vb)
        mix = jax.nn.softmax(lse, axis=1)
        return jnp.einsum('bghq,bqghd->bqhd', mix, out).astype(q.dtype)

    starts = jnp.arange(seqlen // Q_BLOCK, dtype=jnp.int32) * Q_BLOCK
    out = lax.map(block, starts)
    return out.transpose(1, 0, 2, 3, 4).reshape(bsz, seqlen, ATTN_OUT_WIDTH)


def memory_cross_attention(xq, mem_n, w_kv):
    bsz, seqlen, _ = xq.shape
    f32 = jnp.float32
    mk, mv = jnp.split(mem_n @ w_kv, 2, axis=-1)
    mk = mk.reshape(bsz, -1, XATTN_HEADS, XATTN_HEAD_DIM).astype(f32)
    mv = mv.reshape(bsz, -1, XATTN_HEADS, XATTN_HEAD_DIM).astype(f32)
    qh = xq.reshape(bsz, seqlen, XATTN_HEADS, XATTN_HEAD_DIM).astype(f32)
    logits = jnp.einsum('bqhd,bkhd->bhqk', qh, mk) * (XATTN_HEAD_DIM ** -0.5)
    probs = jax.nn.softmax(logits, axis=-1)
    out = jnp.einsum('bhqk,bkhd->bqhd', probs, mv)
    return out.reshape(bsz, seqlen, XATTN_WIDTH).astype(xq.dtype)


def hybrid_layer(x, mem, rel_table, ffn1_norm, ffn1_w_in, ffn1_w_down, mix_norm, w_in,
                 ssm_a_re, ssm_a_im, ssm_log_dt, ssm_b_re, ssm_b_im, ssm_c_re, ssm_c_im, ssm_d, ssm_w_glu,
                 attn_w_up, mem_norm, xattn_w_kv, xattn_w_up, w_out, ffn2_norm, ffn2_w_in, ffn2_w_down):
    bsz, seqlen, _ = x.shape
    x = x + 0.5 * swiglu_ffn(rmsnorm(x, ffn1_norm), ffn1_w_in, ffn1_w_down)
    u = rmsnorm(x, mix_norm)
    proj = u @ w_in
    cuts = np.cumsum([SSM_WIDTH, ATTN_WIDTH, ATTN_WIDTH, ATTN_WIDTH, XATTN_WIDTH, D_MODEL, D_MODEL]).tolist()
    u_ssm, q, k, v, xq, g_ssm, g_attn, g_mem = jnp.split(proj, cuts, axis=-1)
    y_ssm = jax.nn.gelu(s5_scan(u_ssm, ssm_a_re, ssm_a_im, ssm_log_dt, ssm_b_re, ssm_b_im,
                                ssm_c_re, ssm_c_im, ssm_d))
    glu_a, glu_b = jnp.split(y_ssm @ ssm_w_glu, 2, axis=-1)
    br_ssm = glu_a * jax.nn.sigmoid(glu_b)
    hshape = (bsz, seqlen, N_PATTERNS, HEADS_PER_PATTERN, HEAD_DIM)
    br_attn = dilated_mixture_attention(q.reshape(hshape), k.reshape(hshape), v.reshape(hshape),
                                        rel_table) @ attn_w_up
    br_mem = memory_cross_attention(xq, rmsnorm(mem, mem_norm), xattn_w_kv) @ xattn_w_up
    merged = (jax.nn.sigmoid(g_ssm) * br_ssm + jax.nn.sigmoid(g_attn) * br_attn
              + jax.nn.sigmoid(g_mem) * br_mem)
    x = x + merged @ w_out
    x = x + 0.5 * swiglu_ffn(rmsnorm(x, ffn2_norm), ffn2_w_in, ffn2_w_down)
    return x


def setup_inputs(seed: int = 0) -> dict:
    key = jax.random.key(seed)
    ks = jax.random.split(key, 32)
    f32 = jnp.float32

    def nrm(k, shape, fan_in):
        return jax.random.normal(k, shape, f32) * (fan_in ** -0.5)

    def gain(k, shape):
        return 1.0 + 0.02 * jax.random.normal(k, shape, f32)

    L = DEPTH
    a_im_init = jnp.pi * jnp.arange(SSM_STATE, dtype=f32)
    return {
        "x": jax.random.normal(ks[0], (BATCH, SEQ, D_MODEL), f32),
        "mem": jax.random.normal(ks[1], (BATCH, MEM_LEN, D_MODEL), f32),
        "ffn1_norm": gain(ks[2], (L, D_MODEL)),
        "ffn1_w_in": nrm(ks[3], (L, D_MODEL, 2 * D_FF), D_MODEL),
        "ffn1_w_down": nrm(ks[4], (L, D_FF, D_MODEL), D_FF),
        "mix_norm": gain(ks[5], (L, D_MODEL)),
        "w_in": nrm(ks[6], (L, D_MODEL, IN_WIDTH), D_MODEL),
        "ssm_a_re": -0.5 + 0.01 * jax.random.normal(ks[7], (L, SSM_GROUPS, SSM_STATE), f32),
        "ssm_a_im": a_im_init + 0.01 * jax.random.normal(ks[8], (L, SSM_GROUPS, SSM_STATE), f32),
        "ssm_log_dt": jax.random.uniform(ks[9], (L, SSM_GROUPS), f32,
                                         float(np.log(DT_MIN)), float(np.log(DT_MAX))),
        "ssm_b_re": nrm(ks[10], (L, SSM_GROUPS, SSM_STATE, SSM_GROUP), 2 * SSM_GROUP),
        "ssm_b_im": nrm(ks[11], (L, SSM_GROUPS, SSM_STATE, SSM_GROUP), 2 * SSM_GROUP),
        "ssm_c_re": nrm(ks[12], (L, SSM_GROUPS, SSM_GROUP, SSM_STATE), SSM_STATE),
        "ssm_c_im": nrm(ks[13], (L, SSM_GROUPS, SSM_GROUP, SSM_STATE), SSM_STATE),
        "ssm_d": jax.random.normal(ks[14], (L, SSM_GROUPS, SSM_GROUP), f32),
        "ssm_w_glu": nrm(ks[15], (L, SSM_WIDTH, 2 * D_MODEL), SSM_WIDTH),
        "rel_table": 0.5 * jax.random.normal(ks[16], (REL_BUCKETS, N_ATTN_HEADS), f32),
        "attn_w_up": nrm(ks[17], (L, ATTN_OUT_WIDTH, D_MODEL), ATTN_OUT_WIDTH),
        "mem_norm": gain(ks[18], (L, D_MODEL)),
        "xattn_w_kv": nrm(ks[19], (L, D_MODEL, 2 * XATTN_WIDTH), D_MODEL),
        "xattn_w_up": nrm(ks[20], (L, XATTN_WIDTH, D_MODEL), XATTN_WIDTH),
        "w_out": nrm(ks[21], (L, D_MODEL, D_MODEL), D_MODEL),
        "ffn2_norm": gain(ks[22], (L, D_MODEL)),
        "ffn2_w_in": nrm(ks[23], (L, D_MODEL, 2 * D_FF), D_MODEL),
        "ffn2_w_down": nrm(ks[24], (L, D_FF, D_MODEL), D_FF),
        "final_norm": gain(ks[25], (D_MODEL,)),
    }


def reference(x, mem, ffn1_norm, ffn1_w_in, ffn1_w_down, mix_norm, w_in,
              ssm_a_re, ssm_a_im, ssm_log_dt, ssm_b_re, ssm_b_im, ssm_c_re, ssm_c_im, ssm_d, ssm_w_glu,
              rel_table, attn_w_up, mem_norm, xattn_w_kv, xattn_w_up, w_out,
              ffn2_norm, ffn2_w_in, ffn2_w_down, final_norm):
    h = x
    for l in range(DEPTH):
        h = hybrid_layer(h, mem, rel_table, ffn1_norm[l], ffn1_w_in[l], ffn1_w_down[l], mix_norm[l], w_in[l],
                         ssm_a_re[l], ssm_a_im[l], ssm_log_dt[l], ssm_b_re[l], ssm_b_im[l],
                         ssm_c_re[l], ssm_c_im[l], ssm_d[l], ssm_w_glu[l],
                         attn_w_up[l], mem_norm[l], xattn_w_kv[l], xattn_w_up[l], w_out[l],
                         ffn2_norm[l], ffn2_w_in[l], ffn2_w_down[l])
    return rmsnorm(h, final_norm)
```

```python
import numpy as np
import math
from contextlib import ExitStack
import concourse.bass as bass
import concourse.mybir as mybir
from concourse.bass_utils import run_bass_kernel_spmd

F32 = mybir.dt.float32
BF16 = mybir.dt.bfloat16
I32 = mybir.dt.int32
AF = mybir.ActivationFunctionType
ALU = mybir.AluOpType
AX = mybir.AxisListType

D = 1024; DFF = 2816; NJ = 22; T = 512; NCH = 8
SEQ = 8192; HALF = 4096
EPS = 1e-6
NEG = -30000.0
RING = (2, 2, 5)
LMAX = (1, 1, 4)
STRIPW = (256, 256, 640)
STRIP_OFF = (0, 1024, 2048)
STRIP_TOT = 4 * (256 + 256 + 640)

CFG = dict(n_prev=8, n_own=8, stage="full")


class Buf:
    __slots__ = ("name", "w", "r")

    def __init__(self, name=""):
        self.name = name
        self.w = None
        self.r = {}


ENGS = ("pe", "act", "dve", "pool", "sp")


class Sched:
    def __init__(self):
        self.ops = {e: [] for e in ENGS}
        self.known = {e: {} for e in ENGS}
        self.dma_cnt = {}
        self.emitted = {e: 0 for e in ENGS}
        self.sigbase = {e: 0 for e in ENGS}
        self.snap = {e: {} for e in ENGS}
        self.collect = False

    def _waits(self, eng, idx, reads, writes):
        deps = {}
        for b in reads:
            if b.w is not None:
                k, v = b.w
                if deps.get(k, -1) < v:
                    deps[k] = v
        for b in writes:
            if b.w is not None:
                k, v = b.w
                if deps.get(k, -1) < v:
                    deps[k] = v
            for k, v in b.r.items():
                if deps.get(k, -1) < v:
                    deps[k] = v
        waits = []
        kn = self.known[eng]
        for k, v in deps.items():
            if k == eng and eng == "pe":
                continue
            if kn.get(k, -1) >= v:
                continue
            kn[k] = v
            waits.append((k, v))
            if not isinstance(k, tuple):
                self.ops[k][v][2] = True
                for k3, v3 in self.ops[k][v][3].items():
                    if kn.get(k3, -1) < v3:
                        kn[k3] = v3
        if waits:
            self.snap[eng] = dict(kn)
        return waits

    def op(self, eng, fn, reads=(), writes=()):
        if self.collect:
            return
        idx = len(self.ops[eng])
        waits = self._waits(eng, idx, reads, writes)
        self.ops[eng].append([fn, waits, False, self.snap[eng]])
        for b in reads:
            if b.r.get(eng, -1) < idx:
                b.r[eng] = idx
        for b in writes:
            b.w = (eng, idx)
            b.r = {}

    def dma(self, q, out_ap, in_ap, sem, reads=(), writes=(), **kw):
        if self.collect:
            return
        val = self.dma_cnt.get(sem, 0) + 16
        self.dma_cnt[sem] = val
        key = ("dma", sem)
        idx = len(self.ops[q])
        waits = self._waits(q, idx, reads, writes)
        self.ops[q].append([("dma", out_ap, in_ap, sem, kw), waits, False, self.snap[q]])
        for b in reads:
            b.r[key] = val
        for b in writes:
            b.w = (key, val)
            b.r = {}

    def barrier(self, final=False):
        if self.collect:
            return
        last = {e: len(self.ops[e]) - 1 for e in ENGS if len(self.ops[e]) > 0}
        for e in ENGS:
            waits = []
            for e2, i2 in last.items():
                if self.known[e].get(e2, -1) >= i2:
                    continue
                self.known[e][e2] = i2
                self.ops[e2][i2][2] = True
                waits.append((e2, i2))
            for sem, val in self.dma_cnt.items():
                k = ("dma", sem)
                if sem.startswith("cv") and not final:
                    continue
                if self.known[e].get(k, -1) >= val:
                    continue
                self.known[e][k] = val
                waits.append((k, val))
            if waits:
                self.snap[e] = dict(self.known[e])
            self.ops[e].append([None, waits, False, self.snap[e]])

    def emit(self, nc, esem, dsem):
        sigcount = {}
        for e in ENGS:
            c = 0
            lst = []
            for o in self.ops[e]:
                if o[2]:
                    c += 1
                lst.append(c)
            sigcount[e] = lst
        ops = self.ops
        emitted = self.emitted

        def run(e_name, eng):
            lst = ops[e_name]
            for i in range(emitted[e_name], len(lst)):
                fn, waits, sig = lst[i][0], lst[i][1], lst[i][2]
                for k, v in waits:
                    if isinstance(k, tuple):
                        eng.wait_ge(dsem[k[1]], v)
                    else:
                        eng.wait_ge(esem[k], sigcount[k][v])
                if fn is None:
                    if sig:
                        eng.nop().then_inc(esem[e_name], 1)
                    continue
                if isinstance(fn, tuple):
                    _, o_ap, i_ap, sem, kw = fn
                    ins = eng.dma_start(out=o_ap, in_=i_ap, **kw).then_inc(dsem[sem], 16)
                    if sig:
                        eng.nop().then_inc(esem[e_name], 1)
                else:
                    ins = fn(eng)
                    if sig:
                        ins.then_inc(esem[e_name], 1)
            emitted[e_name] = len(lst)

        with nc.Block() as block:
            @block.tensor
            def _(e):
                run("pe", e)

            @block.scalar
            def _(e):
                run("act", e)

            @block.vector
            def _(e):
                run("dve", e)

            @block.gpsimd
            def _(e):
                run("pool", e)

            @block.sync
            def _(e):
                run("sp", e)


def t5_causal_buckets(dist):
    dist = np.asarray(dist, np.int32)
    max_exact = 16
    safe = np.maximum(dist, 1).astype(np.float32)
    large = max_exact + (np.log(safe / max_exact) / np.log(2048 / max_exact) * (32 - max_exact)).astype(np.int32)
    large = np.minimum(large, 31)
    return np.where(dist < max_exact, dist, large).astype(np.int32)


class Wall:
    def __init__(self):
        self.parts = []
        self.off = {}
        self.total = 0

    def add(self, name, arr):
        arr = np.ascontiguousarray(arr, dtype=np.float32).reshape(128, -1)
        self.off[name] = (self.total, arr.shape[1])
        self.parts.append(arr)
        self.total += arr.shape[1]

    def build(self, pad_to):
        tot = ((self.total + pad_to - 1) // pad_to) * pad_to
        if tot > self.total:
            self.parts.append(np.zeros((128, tot - self.total), np.float32))
        self.total = tot
        return np.concatenate(self.parts, axis=1)


def kchunks(w):
    k = w.shape[0] // 128
    return w.reshape(k, 128, w.shape[1]).transpose(1, 0, 2)


def ssm_feat_index():
    idx = -np.ones((6, 128), np.int64)
    for c6 in range(6):
        for p in range(96):
            G = 3 * c6 + p // 32
            if G < 16:
                idx[c6, p] = G * 32 + p % 32
    return idx


def build_wall(inp, names_only=False):
    W = Wall()
    uidx = ssm_feat_index()
    z = lambda *s: np.zeros(s, np.float32)
    rel = inp["rel_table"]
    strips = np.full((128, STRIP_TOT), NEG, np.float32)
    kk = np.arange(128)[:, None]
    for g, (dil, cs) in enumerate(((1, 1), (4, 4), (16, 4))):
        Wd = STRIPW[g]
        X = np.arange(Wd)[None, :]
        dl = X - kk
        dist = dl * cs
        valid = (dl >= 0) & (dist % dil == 0) & (dist // dil <= 128)
        bk = t5_causal_buckets(np.maximum(dist, 0))
        for h in range(4):
            vals = rel[bk, 4 * g + h]
            o = 4 * STRIP_OFF[g] // 4 * 0 + sum(4 * STRIPW[i] for i in range(g)) + h * Wd
            strips[:, o:o + Wd] = np.where(valid, vals, NEG)
    W.add("strips", strips)
    kv = kchunks(inp["xattn_w_kv"][0])
    for h in range(4):
        W.add("xk%d" % h, kv[:, :, h * 128:(h + 1) * 128])
    W.add("xv0", kv[:, 0:4, 512:1024])
    W.add("xv1", kv[:, 4:8, 512:1024])
    FF = (1,)
    ffn_w = {1: (inp["ffn1_w_in"], inp["ffn1_w_down"]), 2: (inp["ffn2_w_in"], inp["ffn2_w_down"])}
    for f in FF:
        w_in = ffn_w[f][0][0]
        w_dn = ffn_w[f][1][0]
        kin = kchunks(w_in)
        for j in range(NJ):
            a = kin[:, :, j * 128:(j + 1) * 128]
            b = kin[:, :, DFF + j * 128:DFF + (j + 1) * 128]
            W.add("ffn%d_in%d" % (f, j), np.stack([a, b], axis=1))
        kdn = kchunks(w_dn)
        for m in range(8):
            W.add("ffn%d_dn%d_0" % (f, m), kdn[:, 0:11, m * 128:(m + 1) * 128])
            W.add("ffn%d_dn%d_1" % (f, m), kdn[:, 11:22, m * 128:(m + 1) * 128])
    w_in = inp["w_in"][0]
    kw = kchunks(w_in)
    for c6 in range(6):
        a = z(128, 8, 128)
        for col in range(128):
            fi = uidx[c6, col]
            if fi >= 0:
                a[:, :, col] = kw[:, :, fi]
        W.add("pu%d" % c6, a)
    for c in range(6):
        W.add("pk%d" % c, kw[:, :, 1280 + c * 128:1280 + (c + 1) * 128])
    for g in range(3):
        W.add("pv%d" % g, kw[:, :, 2048 + g * 256:2048 + (g + 1) * 256])
    W.first_end = W.total
    for c in range(6):
        W.add("pq%d" % c, kw[:, :, 512 + c * 128:512 + (c + 1) * 128])
    for c in range(4):
        W.add("pxq%d" % c, kw[:, :, 2816 + c * 128:2816 + (c + 1) * 128])
    for b in range(3):
        for c in range(8):
            o = 3328 + b * 1024 + c * 128
            W.add("pg%d_%d" % (b, c), kw[:, :, o:o + 128])
    glu = inp["ssm_w_glu"][0]
    gk = z(128, 6, 2048)
    for c6 in range(6):
        for p in range(128):
            fi = uidx[c6, p]
            if fi >= 0:
                gk[p, c6, :] = glu[fi, :]
    for c in range(8):
        a = gk[:, :, c * 128:(c + 1) * 128]
        b = gk[:, :, 1024 + c * 128:1024 + (c + 1) * 128]
        W.add("glu%d" % c, np.stack([a, b], axis=1))
    aup = inp["attn_w_up"][0]
    ak = z(128, 4, 1024)
    ak[:64] = aup.reshape(4, 64, 1024).transpose(1, 0, 2)
    for c in range(8):
        W.add("aup%d" % c, ak[:, :, c * 128:(c + 1) * 128])
    xk = kchunks(inp["xattn_w_up"][0])
    for c in range(8):
        W.add("xup%d" % c, xk[:, :, c * 128:(c + 1) * 128])
    wo = kchunks(inp["w_out"][0])
    for c in range(8):
        W.add("wo%d" % c, wo[:, :, c * 128:(c + 1) * 128])
    FF = (2,)
    for f in FF:
        w_in = ffn_w[f][0][0]
        w_dn = ffn_w[f][1][0]
        kin = kchunks(w_in)
        for j in range(NJ):
            a = kin[:, :, j * 128:(j + 1) * 128]
            b = kin[:, :, DFF + j * 128:DFF + (j + 1) * 128]
            W.add("ffn%d_in%d" % (f, j), np.stack([a, b], axis=1))
        kdn = kchunks(w_dn)
        for m in range(8):
            W.add("ffn%d_dn%d_0" % (f, m), kdn[:, 0:11, m * 128:(m + 1) * 128])
            W.add("ffn%d_dn%d_1" % (f, m), kdn[:, 11:22, m * 128:(m + 1) * 128])
    return W


def strip_off(g, h):
    return sum(4 * STRIPW[i] for i in range(g)) + h * STRIPW[g]


def pairs_layout(a):
    s = a.shape
    a = a.reshape((16, 2, 64) + s[2:])
    perm = (1, 2, 0) + tuple(range(3, a.ndim))
    return np.ascontiguousarray(a.transpose(perm)).reshape((128, 16) + s[2:])


def build_small(inp):
    cols = {}
    parts = []
    tot = [0]

    def add(name, arr):
        arr = np.ascontiguousarray(arr, np.float32).reshape(128, -1)
        cols[name] = (tot[0], arr.shape[1])
        parts.append(arr)
        tot[0] += arr.shape[1]

    for nm in ("ffn1_norm", "mix_norm", "ffn2_norm", "mem_norm"):
        add(nm, inp[nm][0].reshape(8, 128).T)
    add("final_norm", inp["final_norm"].reshape(8, 128).T)
    add("ident", np.eye(128, dtype=np.float32))
    add("nidx", np.tile(np.arange(1, 65, dtype=np.float32)[None, :], (128, 1)))
    add("kidx", np.tile(np.arange(0, 9, dtype=np.float32)[None, :], (128, 1)))
    are = inp["ssm_a_re"][0]; aim = inp["ssm_a_im"][0]
    add("are", pairs_layout(are))
    add("aim", pairs_layout(aim))
    ldt = np.repeat(inp["ssm_log_dt"][0][:, None], 64, axis=1)
    add("ldt", pairs_layout(ldt))
    for nm in ("ssm_b_re", "ssm_b_im"):
        b = pairs_layout(inp[nm][0])
        bp = np.zeros((128, 16, 32), np.float32)
        bp[:64, :, 0:16] = b[:64]
        bp[64:, :, 16:32] = b[64:]
        add(nm, bp)
    for nm in ("ssm_c_re", "ssm_c_im"):
        c = inp[nm][0].transpose(0, 2, 1)
        c = pairs_layout(c)
        cp = np.zeros((128, 16, 32), np.float32)
        cp[:64, :, 0:16] = c[:64]
        cp[64:, :, 16:32] = c[64:]
        add(nm, cp)
    dsk = inp["ssm_d"][0].reshape(512)
    uidx = ssm_feat_index()
    dv = np.zeros((128, 6), np.float32)
    for c6 in range(6):
        for p in range(128):
            if uidx[c6, p] >= 0:
                dv[p, c6] = dsk[uidx[c6, p]]
    add("dvec", dv)
    P_NAMES = ("ffn1_norm", "mix_norm", "ffn2_norm", "mem_norm", "final_norm", "ident", "dvec")
    pa, pc, sa, sc_ = [], {}, [], {}
    po = so = 0
    for (name, (o, n)), arr in zip(cols.items(), parts):
        if name in P_NAMES:
            pa.append(arr); pc[name] = (po, n); po += n
        else:
            sa.append(arr); sc_[name] = (so, n); so += n
    return np.concatenate(pa, axis=1), pc, np.concatenate(sa, axis=1), sc_


class TileB:
    def __init__(self, t, n=1):
        self.t = t
        self.b = [Buf() for _ in range(n)]


def build_program(woff, wtotal, scols, nsmall, pcols, nssmp, first_end):
    n_prev = CFG["n_prev"]; n_own = CFG["n_own"]; stage = CFG["stage"]
    NTOK = (n_prev + n_own) * T
    nc = bass.Bass("TRN2", target_bir_lowering=False)
    xin_d = nc.dram_tensor("xin", [NTOK, D], F32, kind="ExternalInput").ap()
    wall_d = nc.dram_tensor("wall", [128, wtotal], F32, kind="ExternalInput").ap()
    small_d = nc.dram_tensor("small", [128, nsmall], F32, kind="ExternalInput").ap()
    ssmp_d = nc.dram_tensor("ssmp", [128, nssmp], F32, kind="ExternalInput").ap()
    flag_d = nc.dram_tensor("flag", [128, 64], F32, kind="ExternalInput").ap()
    mem_d = nc.dram_tensor("mem", [256, D], F32, kind="ExternalInput").ap()
    y_d = nc.dram_tensor("y", [n_own * T, D], F32, kind="ExternalOutput").ap()
    scr_d = nc.dram_tensor("scr", [128, wtotal], BF16).ap()
    SSMW = 6 * 2048 + 4 * 2048
    sscr_d = nc.dram_tensor("sscr", [128, SSMW], BF16).ap()

    S = Sched()
    CB = 8192
    ncb = wtotal // CB
    with ExitStack() as st:
        esem = {e: st.enter_context(nc.semaphore("e_" + e)) for e in ENGS}
        dsem = {}

        def getsem(name):
            if name not in dsem:
                dsem[name] = st.enter_context(nc.semaphore("d_" + name))
            return name

        uniq = [0]

        def sb(name, shape, dt, stack=st):
            uniq[0] += 1
            return stack.enter_context(nc.sbuf_tensor("s%d_%s" % (uniq[0], name), shape, dt))

        cvb = [Buf("cv%d" % i) for i in range(ncb)]
        cv_next = [0]

        def conv_issue(upto):
            upto = min(upto, ncb)
            while cv_next[0] < upto:
                i = cv_next[0]
                S.dma("pool", scr_d[:, i * CB:(i + 1) * CB], wall_d[:, i * CB:(i + 1) * CB],
                      getsem("cv%d" % i), writes=[cvb[i]], max_dma_last_dim=8192)
                cv_next[0] += 1
        ncb_first = (first_end + CB - 1) // CB
        conv_issue(ncb_first)

        small = sb("small", [128, nsmall], F32)
        smallb = Buf("small")
        S.dma("sp", small[:], small_d[:, :], getsem("small"), writes=[smallb])

        def sc(name, a=None, b=None):
            o, n = scols[name]
            if a is None:
                return small[:, o:o + n]
            return small[:, o + a:o + b]

        ident = sc("ident")
        ones_bf = sb("ones_bf", [128, 128], BF16)
        onesb = Buf("ones")
        S.op("dve", lambda e: e.memset(ones_bf[:], 1.0), writes=[onesb])
        ident_bf = sb("ident_bf", [128, 128], BF16)
        identbb = Buf("identbf")
        S.op("dve", lambda e: e.tensor_copy(out=ident_bf[:], in_=ident), reads=[smallb], writes=[identbb])
        flag_f = sb("flag_f", [128, 64], F32)
        flag_bf = sb("flag_bf", [128, 64], BF16)
        flagb = Buf("flag")
        S.dma("sp", flag_f[:], flag_d[:, :], getsem("flag"), writes=[flagb])
        S.op("dve", lambda e: e.tensor_copy(out=flag_bf[:], in_=flag_f[:]), reads=[flagb], writes=[flagb])
        epsb = sb("epsb", [128, 1], F32)
        epsbb = Buf("eps")
        S.op("dve", lambda e: e.memset(epsb[:], EPS), writes=[epsbb])

        strips = sb("strips", [128, STRIP_TOT], BF16)
        stripb = Buf("strips")
        so_, sn_ = woff["strips"]
        S.dma("sp", strips[:], scr_d[:, so_:so_ + sn_], getsem("strips"),
              reads=[cvb[i] for i in range(so_ // CB, (so_ + sn_ - 1) // CB + 1)], writes=[stripb])

        NBANK = 8
        banks = [st.enter_context(nc.psum_tensor("bank%d" % i, [128, 512], F32)) for i in range(NBANK)]
        bankb = [Buf("bank%d" % i) for i in range(NBANK)]
        bank_free = list(range(NBANK))

        def palloc():
            assert bank_free, "psum pool exhausted"
            return bank_free.pop(0)

        def pfree(i):
            bank_free.append(i)

        NSLOT = 4
        SLOTF = 2048
        slots = [sb("slot%d" % i, [128, SLOTF], BF16) for i in range(NSLOT)]
        slotb = [Buf("slot%d" % i) for i in range(NSLOT)]
        for i in range(NSLOT):
            getsem("slot%d" % i)

        class Stream:
            def __init__(self):
                self.plan = []
                self.pos = 0
                self.issued = 0

            def get(self, src, off, n):
                if S.collect:
                    self.plan.append((src, off, n))
                    return None, None
                r = self.pos
                assert self.plan[r] == (src, off, n), (r, self.plan[r], (src, off, n))
                self.pos += 1
                while self.issued < len(self.plan) and self.issued <= r + (NSLOT - 1):
                    q = self.issued
                    s2, o2, n2 = self.plan[q]
                    sl = q % NSLOT
                    if s2 == "w":
                        rd = [cvb[i] for i in range(o2 // CB, (o2 + n2 - 1) // CB + 1)]
                        S.dma("sp", slots[sl][:, 0:n2], scr_d[:, o2:o2 + n2], "slot%d" % sl, reads=rd, writes=[slotb[sl]])
                    else:
                        S.dma("sp", slots[sl][:, 0:n2], sscr_d[:, o2:o2 + n2], "slot%d" % sl, reads=[sscrb], writes=[slotb[sl]])
                    self.issued += 1
                sl = r % NSLOT
                return slots[sl], slotb[sl]

        stream = Stream()
        sscrb = Buf("sscr")

        def wget(name):
            o, n = woff[name]
            return stream.get("w", o, n)

        xres = xn = gbuf = xin_t = yout_t = tmpf = ud = qT = xqT = kT = vtm = accN = accD = PT = zt = wt = ta = tb_ = macc = None
        tmpi = [0]

        def tmp():
            tmpi[0] = (tmpi[0] + 1) % 3
            return tmpf[tmpi[0]]

        def alloc_main():
            nonlocal xres, xn, gbuf, xin_t, yout_t, tmpf, ud, qT, xqT, kT, vtm, accN, accD, PT, zt, wt, ta, tb_, macc
            xres = TileB(sb("xres", [128, 8, T], F32), 8)
            xn = TileB(sb("xn", [128, 8, T], BF16), 8)
            gbuf = TileB(sb("gbuf", [128, NJ, T], BF16), NJ)
            xin_t = [TileB(sb("xin%d" % i, [128, D], F32)) for i in range(2)]
            yout_t = xin_t
            tmpf = [TileB(sb("tmpf%d" % i, [128, T], F32)) for i in range(3)]
            ud = TileB(sb("ud", [128, 6, T], BF16), 6)
            qT = TileB(sb("qT", [128, 12, T], BF16), 12)
            S.op("pool", lambda e: e.memset(qT.t[:], 0.0), writes=qT.b)
            xqT = TileB(sb("xqT", [128, 4, T], BF16), 4)
            kT = [TileB(sb("kT%d" % g, [128, 2, RING[g] * T], BF16), RING[g]) for g in range(3)]
            vtm = [TileB(sb("vtm%d" % g, [128, RING[g], 4, 256], BF16), RING[g]) for g in range(3)]
            accN = TileB(sb("accN", [64, T], F32)); accD = TileB(sb("accD", [64, T], F32))
            PT = [TileB(sb("PT%d" % i, [128, 640], BF16)) for i in range(3)]
            zt = TileB(sb("zt", [128, 2, 16, 64], F32)); wt = TileB(sb("wt", [128, 2, 16, 64], F32))
            ta = TileB(sb("ta", [128, 16, 64], F32)); tb_ = TileB(sb("tb", [128, 16, 64], F32))
            macc = TileB(sb("macc", [128, T], F32))
            for i in range(2):
                getsem("xin%d" % i)

        gain = {nm: sc(nm) for nm in ("ffn1_norm", "mix_norm", "ffn2_norm", "mem_norm", "final_norm")}

        def mm(bank, ps_ap, lhsT, rhs, start, stop, reads):
            S.op("pe", lambda e: e.matmul(ps_ap, lhsT, rhs, start=start, stop=stop), reads=reads, writes=[bankb[bank]])

        def rmsnorm(gname):
            g = gain[gname]
            for c in range(8):
                if c % 2 == 0:
                    S.op("act", lambda e, c=c: e.activation(out=xn.t[:, c, :], in_=xres.t[:, c, :], func=AF.Square),
                         reads=[xres.b[c]], writes=[xn.b[c]])
                else:
                    S.op("dve", lambda e, c=c: e.tensor_tensor(out=xn.t[:, c, :], in0=xres.t[:, c, :], in1=xres.t[:, c, :], op=ALU.mult),
                         reads=[xres.b[c]], writes=[xn.b[c]])
            bk = palloc()
            for c in range(8):
                mm(bk, banks[bk][:, :], ones_bf[:], xn.t[:, c, :], c == 0, c == 7, [onesb, xn.b[c]])
            t1 = tmp(); t2 = tmp()
            S.op("act", lambda e: e.activation(out=t1.t[:], in_=banks[bk][:, :], func=AF.Ln, scale=1.0 / D, bias=epsb[:]),
                 reads=[bankb[bk], epsbb], writes=[t1.b[0]])
            pfree(bk)
            S.op("act", lambda e: e.activation(out=t2.t[:], in_=t1.t[:], func=AF.Exp, scale=-0.5),
                 reads=[t1.b[0]], writes=[t2.b[0]])
            for c in range(8):
                S.op("dve", lambda e, c=c: e.scalar_tensor_tensor(out=xn.t[:, c, :], in0=xres.t[:, c, :], scalar=g[:, c:c + 1],
                                                                  in1=t2.t[:], op0=ALU.mult, op1=ALU.mult),
                     reads=[xres.b[c], t2.b[0], smallb], writes=[xn.b[c]])
            return t2

        def ffn(f):
            for j in range(NJ):
                sl, slb = wget("ffn%d_in%d" % (f, j))
                if S.collect:
                    continue
                w = sl[:, 0:2048].rearrange("p (a k c) -> p a k c", a=2, k=8)
                ba = palloc(); bb = palloc()
                if j == 0:
                    for k in range(8):
                        mm(ba, banks[ba][:, :], w[:, 0, k, :], xn.t[:, k, :], k == 0, k == 7, [slb, xn.b[k]])
                        mm(bb, banks[bb][:, :], w[:, 1, k, :], xn.t[:, k, :], k == 0, k == 7, [slb, xn.b[k]])
                else:
                    for k in range(8):
                        mm(ba, banks[ba][:, :], w[:, 0, k, :], xn.t[:, k, :], k == 0, k == 7, [slb, xn.b[k]])
                    for k in range(8):
                        mm(bb, banks[bb][:, :], w[:, 1, k, :], xn.t[:, k, :], k == 0, k == 7, [slb, xn.b[k]])
                t1 = tmp()
                S.op("act", lambda e, ba=ba, t1=t1: e.activation(out=t1.t[:], in_=banks[ba][:, :], func=AF.Silu),
                     reads=[bankb[ba]], writes=[t1.b[0]])
                pfree(ba)
                S.op("dve", lambda e, bb=bb, t1=t1, j=j: e.tensor_tensor(out=gbuf.t[:, j, :], in0=t1.t[:], in1=banks[bb][:, :], op=ALU.mult),
                     reads=[bankb[bb], t1.b[0]], writes=[gbuf.b[j]])
                pfree(bb)
            for m in range(8):
                bk = None
                for hf in range(2):
                    sl, slb = wget("ffn%d_dn%d_%d" % (f, m, hf))
                    if S.collect:
                        continue
                    if bk is None:
                        bk = palloc()
                    for j in range(hf * 11, hf * 11 + 11):
                        mm(bk, banks[bk][:, :], sl[:, (j % 11) * 128:(j % 11 + 1) * 128], gbuf.t[:, j, :], j == 0, j == NJ - 1, [slb, gbuf.b[j]])
                if S.collect:
                    continue
                S.op("dve", lambda e, bk=bk, m=m: e.scalar_tensor_tensor(out=xres.t[:, m, :], in0=banks[bk][:, :], scalar=0.5,
                                                                         in1=xres.t[:, m, :], op0=ALU.mult, op1=ALU.add),
                     reads=[bankb[bk], xres.b[m]], writes=[xres.b[m]])
                pfree(bk)

        x_issued = set()

        def issue_x(ti):
            if ti in x_issued or ti >= n_prev + n_own:
                return
            x_issued.add(ti)
            for j in range(4):
                r0 = ti * T + j * 128
                dst = gbuf.t[:, 4 * j:4 * j + 4, :].rearrange("p a b -> p (a b)").bitcast(F32)
                S.dma("sp", dst, xin_d[r0:r0 + 128, :], getsem("xg%d" % j), writes=[gbuf.b[4 * j + i] for i in range(4)])

        def load_x(ti):
            issue_x(ti)
            for j in range(4):
                xv = gbuf.t[:, 4 * j:4 * j + 4, :].rearrange("p a b -> p (a b)").bitcast(F32)
                xb = [gbuf.b[4 * j + i] for i in range(4)]
                for half in range(2):
                    bk = palloc()
                    for cc in range(4):
                        c = half * 4 + cc
                        S.op("pe", lambda e, bk=bk, cc=cc, c=c, xv=xv: e.transpose(out=banks[bk][:, cc * 128:(cc + 1) * 128],
                                                                                  in_=xv[:, c * 128:(c + 1) * 128], identity=ident),
                             reads=xb + [smallb], writes=[bankb[bk]])
                    eng = "act" if half == 0 else "dve"
                    dst = xres.t[:, half * 4:half * 4 + 4, j * 128:(j + 1) * 128]
                    src = banks[bk][:, :].rearrange("p (c t) -> p c t", c=4)
                    if eng == "act":
                        S.op("act", lambda e, dst=dst, src=src: e.copy(out=dst, in_=src),
                             reads=[bankb[bk]], writes=[xres.b[half * 4 + i] for i in range(4)])
                    else:
                        S.op("dve", lambda e, dst=dst, src=src: e.tensor_copy(out=dst, in_=src),
                             reads=[bankb[bk]], writes=[xres.b[half * 4 + i] for i in range(4)])
                    pfree(bk)

        def store_y(to, src_scaled):
            for j in range(4):
                yt = yout_t[j % 2]
                for half in range(2):
                    bk = palloc()
                    for cc in range(4):
                        c = half * 4 + cc
                        S.op("pe", lambda e, bk=bk, cc=cc, c=c, j=j: e.transpose(out=banks[bk][:, cc * 128:(cc + 1) * 128],
                                                                                in_=src_scaled.t[:, c, j * 128:(j + 1) * 128], identity=ident),
                             reads=[src_scaled.b[c], smallb], writes=[bankb[bk]])
                    dst = yt.t[:, half * 512:(half + 1) * 512]
                    if half == 0:
                        S.op("act", lambda e, dst=dst, bk=bk: e.copy(out=dst, in_=banks[bk][:, :]), reads=[bankb[bk]], writes=[yt.b[0]])
                    else:
                        S.op("dve", lambda e, dst=dst, bk=bk: e.tensor_copy(out=dst, in_=banks[bk][:, :]), reads=[bankb[bk]], writes=[yt.b[0]])
                    pfree(bk)
                r0 = to * T + j * 128
                S.dma("sp", y_d[r0:r0 + 128, :], yt.t[:], "xin%d" % (j % 2), reads=[yt.b[0]])


        BR = CFG.get("branches", ("ssm", "attn", "mem"))
        mkT = TileB(sb("mkT", [128, 4, 256], BF16)); mv = TileB(sb("mv", [128, 2, 512], BF16))
        pti = [0]
        evi = [0]

        def evac_copy(dst, src, reads, writes, scale=None):
            evi[0] += 1
            if evi[0] % 2 == 0:
                if scale is None:
                    S.op("act", lambda e: e.copy(out=dst, in_=src), reads=reads, writes=writes)
                else:
                    S.op("act", lambda e: e.mul(dst, src, scale), reads=reads, writes=writes)
            else:
                if scale is None:
                    S.op("dve", lambda e: e.tensor_copy(out=dst, in_=src), reads=reads, writes=writes)
                else:
                    S.op("dve", lambda e: e.tensor_scalar(out=dst, in0=src, scalar1=scale, scalar2=None, op0=ALU.mult), reads=reads, writes=writes)

        def proj(wname, evac):
            sl, slb = wget(wname)
            if S.collect:
                return
            bk = palloc()
            for k in range(8):
                mm(bk, banks[bk][:, :], sl[:, k * 128:(k + 1) * 128], xn.t[:, k, :], k == 0, k == 7, [slb, xn.b[k]])
            evac(bk)
            pfree(bk)

        def setup_mem(st2):
            if True:
                mt = sb("memt", [128, 2, D], F32, st2)
                mtb = [Buf(), Buf()]
                sq = sb("memsq", [128, D], F32, st2); sqb = Buf()
                ss = sb("memss", [128, 4], F32, st2); ssb = Buf()
                memnT = TileB(sb("memnT", [128, 8, 256], BF16, st2))
                if not S.collect:
                    for mb in range(2):
                        S.dma("sp", mt[:, mb, :], mem_d[mb * 128:(mb + 1) * 128, :], getsem("mem%d" % mb), writes=[mtb[mb]])
                        S.op("dve", lambda e, mb=mb: e.tensor_tensor(out=sq[:], in0=mt[:, mb, :], in1=mt[:, mb, :], op=ALU.mult),
                             reads=[mtb[mb]], writes=[sqb])
                        S.op("dve", lambda e, mb=mb: e.tensor_reduce(out=ss[:, mb:mb + 1], in_=sq[:], axis=AX.X, op=ALU.add),
                             reads=[sqb], writes=[ssb])
                    S.op("act", lambda e: e.activation(out=ss[:, 2:4], in_=ss[:, 0:2], func=AF.Ln, scale=1.0 / D, bias=epsb[:]),
                         reads=[ssb, epsbb], writes=[ssb])
                    S.op("act", lambda e: e.activation(out=ss[:, 0:2], in_=ss[:, 2:4], func=AF.Exp, scale=-0.5), reads=[ssb], writes=[ssb])
                    for mb in range(2):
                        S.op("dve", lambda e, mb=mb: e.tensor_scalar(out=mt[:, mb, :], in0=mt[:, mb, :], scalar1=ss[:, mb:mb + 1], scalar2=None, op0=ALU.mult),
                             reads=[ssb, mtb[mb]], writes=[mtb[mb]])
                    gm = gain["mem_norm"]
                    for mb in range(2):
                        for half in range(2):
                            bk = palloc()
                            for cc in range(4):
                                c = half * 4 + cc
                                S.op("pe", lambda e, bk=bk, cc=cc, c=c, mb=mb: e.transpose(out=banks[bk][:, cc * 128:(cc + 1) * 128],
                                                                                          in_=mt[:, mb, c * 128:(c + 1) * 128], identity=ident),
                                     reads=[mtb[mb], smallb], writes=[bankb[bk]])
                            for cc in range(4):
                                c = half * 4 + cc
                                S.op("dve", lambda e, bk=bk, cc=cc, c=c, mb=mb: e.tensor_scalar(out=memnT.t[:, c, mb * 128:(mb + 1) * 128], in0=banks[bk][:, cc * 128:(cc + 1) * 128],
                                                                                               scalar1=gm[:, c:c + 1], scalar2=None, op0=ALU.mult),
                                     reads=[bankb[bk], smallb], writes=[memnT.b[0]])
                            pfree(bk)
                for h in range(4):
                    sl, slb = wget("xk%d" % h)
                    if S.collect:
                        continue
                    bk = palloc()
                    for k in range(8):
                        mm(bk, banks[bk][:, 0:256], sl[:, k * 128:(k + 1) * 128], memnT.t[:, k, :], k == 0, k == 7, [slb, memnT.b[0]])
                    S.op("act", lambda e, bk=bk, h=h: e.mul(mkT.t[:, h, :], banks[bk][:, 0:256], 128.0 ** -0.5), reads=[bankb[bk]], writes=[mkT.b[0]])
                    pfree(bk)
                bks = None
                for hf in range(2):
                    ssl, sslb = wget("xv%d" % hf)
                    if S.collect:
                        continue
                    if bks is None:
                        bks = [palloc(), palloc()]
                    for k in range(hf * 4, hf * 4 + 4):
                        for mb in range(2):
                            mm(bks[mb], banks[bks[mb]][:, :], memnT.t[:, k, mb * 128:(mb + 1) * 128], ssl[:, (k % 4) * 512:(k % 4 + 1) * 512], k == 0, k == 7, [sslb, memnT.b[0]])
                if not S.collect:
                    for mb in range(2):
                        bk = bks[mb]
                        S.op("act", lambda e, bk=bk, mb=mb: e.copy(out=mv.t[:, mb, :], in_=banks[bk][:, :]), reads=[bankb[bk]], writes=[mv.b[0]])
                        pfree(bk)

        def tokgrp(ap, g, grp):
            if g == 0:
                return ap[:, grp * 128:(grp + 1) * 128]
            return ap.rearrange("p (i r) -> p r i", r=4)[:, grp, :]

        def projections(ti, own):
            slot = [ti % RING[g] for g in range(3)]
            if "ssm" in BR:
                for c6 in range(6):
                    def ev(bk, c6=c6):
                        dst = ud.t[:, c6, :].rearrange("p (s n) -> p n s", s=8)
                        src = banks[bk][:, :].rearrange("p (n s) -> p n s", s=8)
                        evac_copy(dst, src, [bankb[bk]], [ud.b[c6]])
                    proj("pu%d" % c6, ev)
            if "attn" in BR and ti >= n_prev - 4:
                for c in range(6):
                    g = c // 2; cc = c % 2
                    def ev(bk, g=g, cc=cc):
                        evac_copy(kT[g].t[:, cc, slot[g] * T:(slot[g] + 1) * T], banks[bk][:, :], [bankb[bk]], [kT[g].b[slot[g]]])
                    proj("pk%d" % c, ev)
                for g in range(3):
                    sl, slb = wget("pv%d" % g)
                    if S.collect:
                        continue
                    w = sl[:, 0:2048].rearrange("p (k c) -> p k c", k=8)
                    for gg in range(2):
                        bk = palloc()
                        for q2 in range(2):
                            grp = gg * 2 + q2
                            for k in range(8):
                                mm(bk, banks[bk][:, q2 * 256:(q2 + 1) * 256], tokgrp(xn.t[:, k, :], g, grp), w[:, k, :], k == 0, k == 7, [slb, xn.b[k]])
                        dst = vtm[g].t[:, slot[g], gg * 2:gg * 2 + 2, :]
                        src = banks[bk][:, :].rearrange("p (a c) -> p a c", a=2)
                        evac_copy(dst, src, [bankb[bk]], [vtm[g].b[slot[g]]])
                        pfree(bk)
            if not own:
                return
            if "attn" in BR:
                for c in range(6):
                    def ev(bk, c=c):
                        evac_copy(qT.t[0:64, 2 * c, :], banks[bk][0:64, :], [bankb[bk]], [qT.b[2 * c]], scale=0.125)
                        evac_copy(qT.t[64:128, 2 * c + 1, :], banks[bk][64:128, :], [bankb[bk]], [qT.b[2 * c + 1]], scale=0.125)
                    proj("pq%d" % c, ev)
            if "mem" in BR:
                for c in range(4):
                    def ev(bk, c=c):
                        evac_copy(xqT.t[:, c, :], banks[bk][:, :], [bankb[bk]], [xqT.b[c]])
                    proj("pxq%d" % c, ev)

        def attention(ti, mid_cb=None):
            if S.collect:
                return
            accs = {}
            cb = [mid_cb]

            def phase_a(h, g, grp):
                cc = h // 2
                if h == 2 and g == 0 and grp == 0 and cb[0] is not None:
                    cb[0](); cb[0] = None
                if grp == 0:
                    accs[(h, g)] = (palloc(), palloc())
                blocks = []
                for L in range(LMAX[g] + 1):
                    if g == 0:
                        if L == 0:
                            tj, kb = ti, grp
                        else:
                            tj, kb = (ti, grp - 1) if grp >= 1 else (ti - 1, 3)
                    else:
                        tj, kb = ti - L, grp
                    if tj < 0:
                        continue
                    blocks.append((L, tj, kb))
                nb = len(blocks)
                sbk = [palloc() for _ in range((nb + 3) // 4)]
                pt = PT[pti[0] % 3]; pti[0] += 1
                qi = 4 * g + h
                qap = tokgrp(qT.t[:, qi, :], g, grp)
                so = strip_off(g, h)
                for bi4 in range(len(sbk)):
                    n4 = min(4, nb - bi4 * 4)
                    bk = sbk[bi4]
                    L0 = blocks[bi4 * 4][0]
                    mm(bk, banks[bk][:, 0:n4 * 128], ident_bf[:], strips[:, so + L0 * 128: so + (L0 + n4) * 128], True, False, [identbb, stripb])
                for bi, (L, tj, kb) in enumerate(blocks):
                    ks = tj % RING[g]
                    kap = tokgrp(kT[g].t[:, cc, ks * T:(ks + 1) * T], g, kb)
                    bk = sbk[bi // 4]
                    ps = banks[bk][:, (bi % 4) * 128:(bi % 4 + 1) * 128]
                    last = (bi % 4 == 3) or (bi == nb - 1)
                    mm(bk, ps, kap, qap, False, last, [kT[g].b[ks], qT.b[qi]])
                for bi4 in range(len(sbk)):
                    n4 = min(4, nb - bi4 * 4)
                    bk = sbk[bi4]
                    S.op("act", lambda e, bk=bk, n4=n4, bi4=bi4, pt=pt: e.activation(out=pt.t[:, bi4 * 512: bi4 * 512 + n4 * 128], in_=banks[bk][:, 0:n4 * 128], func=AF.Exp),
                         reads=[bankb[bk]], writes=[pt.b[0]])
                    pfree(bk)
                return blocks, pt

            def phase_b(h, g, grp, blocks, pt):
                bN, bD = accs[(h, g)]
                nb = len(blocks)
                for bi, (L, tj, kb) in enumerate(blocks):
                    ks = tj % RING[g]
                    vap = vtm[g].t[:, ks, kb, h * 64:(h + 1) * 64]
                    p_ap = pt.t[:, bi * 128:(bi + 1) * 128]
                    mm(bN, banks[bN][0:64, grp * 128:(grp + 1) * 128], vap, p_ap, bi == 0, bi == nb - 1, [vtm[g].b[ks], pt.b[0]])
                for bi, (L, tj, kb) in enumerate(blocks):
                    if tj >= n_prev:
                        vd, vdb = ones_bf[:, 0:64], onesb
                    else:
                        vd, vdb = flag_bf[:], flagb
                    p_ap = pt.t[:, bi * 128:(bi + 1) * 128]
                    mm(bD, banks[bD][0:64, grp * 128:(grp + 1) * 128], vd, p_ap, bi == 0, bi == nb - 1, [vdb, pt.b[0]])
                if grp != 3:
                    return
                for (acc, bk) in ((accN, bN), (accD, bD)):
                    if g == 0:
                        S.op("dve", lambda e, acc=acc, bk=bk: e.tensor_copy(out=acc.t[:], in_=banks[bk][0:64, :]), reads=[bankb[bk]], writes=[acc.b[0]])
                    else:
                        dst = acc.t[:].rearrange("p (i r) -> p i r", r=4)
                        src = banks[bk][0:64, :].rearrange("p (r i) -> p i r", r=4)
                        S.op("dve", lambda e, dst=dst, src=src: e.tensor_tensor(out=dst, in0=dst, in1=src, op=ALU.add), reads=[bankb[bk], acc.b[0]], writes=[acc.b[0]])
                    pfree(bk)
                if g == 2:
                    S.op("act", lambda e: e.activation(out=accD.t[:], in_=accD.t[:], func=AF.Ln), reads=[accD.b[0]], writes=[accD.b[0]])
                    S.op("act", lambda e: e.activation(out=accD.t[:], in_=accD.t[:], func=AF.Exp, scale=-1.0), reads=[accD.b[0]], writes=[accD.b[0]])
                    S.op("dve", lambda e, h=h: e.tensor_tensor(out=gbuf.t[0:64, 18 + h, :], in0=accN.t[:], in1=accD.t[:], op=ALU.mult),
                         reads=[accN.b[0], accD.b[0]], writes=[gbuf.b[18 + h]])

            pend = []
            SKEW = 2
            for h in range(4):
                for g in range(3):
                    for grp in range(4):
                        pend.append(((h, g, grp), phase_a(h, g, grp)))
                        if len(pend) > SKEW:
                            t_, r_ = pend.pop(0)
                            phase_b(*t_, *r_)
            while pend:
                t_, r_ = pend.pop(0)
                phase_b(*t_, *r_)

        def cross_attention():
            if S.collect:
                return
            accs = {}

            def phase_a(h, mb):
                if mb == 0:
                    accs[h] = (palloc(), palloc())
                bk = palloc()
                mm(bk, banks[bk][:, :], mkT.t[:, h, mb * 128:(mb + 1) * 128], xqT.t[:, h, :], True, True, [mkT.b[0], xqT.b[h]])
                pt = PT[pti[0] % 3]; pti[0] += 1
                S.op("act", lambda e, bk=bk, pt=pt: e.activation(out=pt.t[:, 0:512], in_=banks[bk][:, :], func=AF.Exp), reads=[bankb[bk]], writes=[pt.b[0]])
                pfree(bk)
                return pt

            def phase_b(h, mb, pt):
                bN, bD = accs[h]
                mm(bN, banks[bN][:, :], mv.t[:, mb, h * 128:(h + 1) * 128], pt.t[:, 0:512], mb == 0, mb == 1, [mv.b[0], pt.b[0]])
                mm(bD, banks[bD][:, :], ones_bf[:], pt.t[:, 0:512], mb == 0, mb == 1, [onesb, pt.b[0]])
                if mb != 1:
                    return
                t1 = tmp()
                S.op("act", lambda e, bD=bD, t1=t1: e.activation(out=t1.t[:], in_=banks[bD][:, :], func=AF.Ln), reads=[bankb[bD]], writes=[t1.b[0]])
                pfree(bD)
                S.op("act", lambda e, t1=t1: e.activation(out=t1.t[:], in_=t1.t[:], func=AF.Exp, scale=-1.0), reads=[t1.b[0]], writes=[t1.b[0]])
                S.op("dve", lambda e, bN=bN, t1=t1, h=h: e.tensor_tensor(out=gbuf.t[:, 14 + h, :], in0=banks[bN][:, :], in1=t1.t[:], op=ALU.mult),
                     reads=[bankb[bN], t1.b[0]], writes=[gbuf.b[14 + h]])
                pfree(bN)

            prev = None
            for h in range(4):
                for mb in range(2):
                    cur = ((h, mb), phase_a(h, mb))
                    if prev is not None:
                        phase_b(*prev[0], prev[1])
                    prev = cur
            phase_b(*prev[0], prev[1])


        PI = math.pi
        cos2 = sb("cos2", [128, 16, 64], F32); sin2 = sb("sin2", [128, 16, 64], F32)
        Rtab = sb("Rtab", [128, 16, 64], F32); r2 = sb("r2", [128, 16], F32)
        tabb = Buf("ssmtab")
        Wd = sb("Wd", [128, 6, 8, 96], BF16); Wdb = Buf("Wd")
        XB = TileB(sb("XB", [128, 2, 16, 65], BF16))
        carry = TileB(sb("carry", [128, 2, 16], F32)); tcar = TileB(sb("tcar", [128, 2, 16], F32))
        PDOFF = 0; QOFF = 6 * 2048

        def dv(fn, reads, writes, eng="dve"):
            S.op(eng, fn, reads=reads, writes=writes)

        def setup_ssm(st2):
            if S.collect:
                return
            if True:
                N9 = 16 * 9
                dt_ = sb("dt", [128, 16], F32, st2); lr = sb("lr", [128, 16], F32, st2); li = sb("li", [128, 16], F32, st2)
                ang = sb("ang", [128, 16, 9], F32, st2); mag = sb("mag", [128, 16, 9], F32, st2)
                Are = sb("Are", [128, 16, 9], F32, st2); Aim = sb("Aim", [128, 16, 9], F32, st2)
                sA = sb("sA", [128, 16, 9], F32, st2); cA = sb("cA", [128, 16, 9], F32, st2)
                k1 = sb("k1", [128, 1024], F32, st2); k2 = sb("k2", [128, 1024], F32, st2); ki = sb("ki", [128, 1024], I32, st2)
                ang2 = sb("ang2", [128, 16, 64], F32, st2)
                sm = sb("sm", [128, 8, 16], F32, st2)
                BBr = sb("BBr", [128, 16, 32], F32, st2); BBi = sb("BBi", [128, 16, 32], F32, st2)
                BBbr = sb("BBbr", [128, 16, 32], BF16, st2); BBbi = sb("BBbi", [128, 16, 32], BF16, st2)
                CAb = sb("CAb", [128, 16, 2, 9, 32], BF16, st2)
                PdA = sb("PdA", [128, 6, 2048], BF16, st2); PdAb = Buf("PdA")
                B0 = Buf("ssmsetup")
                ssmp = sb("ssmp", [128, nssmp], F32, st2)
                S.dma("sp", ssmp[:], ssmp_d[:, :], getsem("ssmp"), writes=[B0])
                zz = sb("zz", [128, 1024], BF16, st2); zzb = Buf()
                S.op("pool", lambda e: e.memset(zz[:], 0.0), writes=[zzb])
                for i in range(SSMW // 1024):
                    S.dma("sp", sscr_d[:, i * 1024:(i + 1) * 1024], zz[:], getsem("ssmz"), reads=[zzb])
                sscrb.w = (("dma", "ssmz"), S.dma_cnt["ssmz"])

                def sp_(name):
                    o, n = pcols[name]
                    return ssmp[:, o:o + n]
                are = sp_("are"); aim = sp_("aim"); ldt = sp_("ldt")
                bre = sp_("ssm_b_re").rearrange("p (g c) -> p g c", g=16); bim = sp_("ssm_b_im").rearrange("p (g c) -> p g c", g=16)
                cre_ = sp_("ssm_c_re").rearrange("p (g c) -> p g c", g=16); cim_ = sp_("ssm_c_im").rearrange("p (g c) -> p g c", g=16)
                nidx = sp_("nidx")
                R0 = [B0, smallb]

                def D(fn, eng="dve"):
                    S.op(eng, fn, reads=R0, writes=[B0])

                def bc(ap2, n=32):
                    return ap2.unsqueeze(2).broadcast_to([128, 16, n])

                def sincos(x, n, s_out, c_out):
                    for shift, out in ((0.0, s_out), (PI / 2, c_out)):
                        D(lambda e: e.tensor_scalar(out=k1[:, 0:n], in0=x, scalar1=1.0 / (2 * PI), scalar2=shift / (2 * PI) + 0.5, op0=ALU.mult, op1=ALU.add))
                        D(lambda e: e.tensor_copy(out=ki[:, 0:n], in_=k1[:, 0:n]))
                        D(lambda e: e.tensor_copy(out=k1[:, 0:n], in_=ki[:, 0:n]))
                        D(lambda e: e.scalar_tensor_tensor(out=k2[:, 0:n], in0=k1[:, 0:n], scalar=-2 * PI, in1=x, op0=ALU.mult, op1=ALU.add))
                        if shift != 0.0:
                            D(lambda e: e.tensor_scalar(out=k2[:, 0:n], in0=k2[:, 0:n], scalar1=shift, scalar2=None, op0=ALU.add))
                        D(lambda e: e.tensor_scalar(out=k1[:, 0:n], in0=k2[:, 0:n], scalar1=PI, scalar2=-2 * PI, op0=ALU.is_gt, op1=ALU.mult))
                        D(lambda e: e.tensor_tensor(out=k2[:, 0:n], in0=k2[:, 0:n], in1=k1[:, 0:n], op=ALU.add))
                        D(lambda e: e.tensor_scalar(out=k1[:, 0:n], in0=k2[:, 0:n], scalar1=-PI, scalar2=2 * PI, op0=ALU.is_lt, op1=ALU.mult))
                        D(lambda e: e.tensor_tensor(out=k2[:, 0:n], in0=k2[:, 0:n], in1=k1[:, 0:n], op=ALU.add))
                        D(lambda e: e.tensor_scalar(out=k2[:, 0:n], in0=k2[:, 0:n], scalar1=3.1415925, scalar2=-3.1415925, op0=ALU.min, op1=ALU.max))
                        D(lambda e, out=out: e.activation(out=out, in_=k2[:, 0:n], func=AF.Sin), eng="act")

                D(lambda e: e.activation(out=dt_[:], in_=ldt, func=AF.Exp), eng="act")
                D(lambda e: e.tensor_tensor(out=lr[:], in0=are, in1=dt_[:], op=ALU.mult))
                D(lambda e: e.tensor_tensor(out=li[:], in0=aim, in1=dt_[:], op=ALU.mult))
                kidx = sp_("kidx")
                kb3 = kidx.unsqueeze(1).broadcast_to([128, 16, 9])
                D(lambda e: e.tensor_tensor(out=ang[:], in0=li[:].unsqueeze(2).broadcast_to([128, 16, 9]), in1=kb3, op=ALU.mult))
                D(lambda e: e.tensor_tensor(out=mag[:], in0=lr[:].unsqueeze(2).broadcast_to([128, 16, 9]), in1=kb3, op=ALU.mult))
                D(lambda e: e.activation(out=mag[:], in_=mag[:], func=AF.Exp), eng="act")
                sincos(ang[:].rearrange("p g k -> p (g k)"), N9, sA[:].rearrange("p g k -> p (g k)"), cA[:].rearrange("p g k -> p (g k)"))
                D(lambda e: e.tensor_tensor(out=Are[:], in0=mag[:], in1=cA[:], op=ALU.mult))
                D(lambda e: e.tensor_tensor(out=Aim[:], in0=mag[:], in1=sA[:], op=ALU.mult))
                D(lambda e: e.tensor_scalar(out=sm[:, 0, :], in0=li[:], scalar1=8.0, scalar2=None, op0=ALU.mult))
                D(lambda e: e.tensor_tensor(out=ang2[:], in0=nidx.unsqueeze(1).broadcast_to([128, 16, 64]), in1=sm[:, 0, :].unsqueeze(2).broadcast_to([128, 16, 64]), op=ALU.mult))
                sincos(ang2[:].rearrange("p g n -> p (g n)"), 1024, sin2[:].rearrange("p g n -> p (g n)"), cos2[:].rearrange("p g n -> p (g n)"))
                D(lambda e: e.activation(out=r2[:], in_=lr[:], func=AF.Exp, scale=8.0), eng="act")
                D(lambda e: e.tensor_copy(out=Rtab[:], in_=r2[:].unsqueeze(2).broadcast_to([128, 16, 64])))
                D(lambda e: e.memset(Rtab[:, :, 0:1], 0.0))
                S.op("dve", lambda e: e.memset(carry.t[:], 0.0), writes=[carry.b[0]])
                S.op("dve", lambda e: e.memset(XB.t[:], 0.0), writes=[XB.b[0]])
                nr = sm[:, 1, :]; ni = Aim[:, :, 1]; den = sm[:, 2, :]; t0 = sm[:, 3, :]; cr = sm[:, 4, :]; ci = sm[:, 5, :]; t1_ = sm[:, 6, :]
                D(lambda e: e.tensor_scalar(out=nr, in0=Are[:, :, 1], scalar1=-1.0, scalar2=None, op0=ALU.add))
                D(lambda e: e.tensor_tensor(out=den, in0=are, in1=are, op=ALU.mult))
                D(lambda e: e.tensor_tensor(out=t0, in0=aim, in1=aim, op=ALU.mult))
                D(lambda e: e.tensor_tensor(out=den, in0=den, in1=t0, op=ALU.add))
                D(lambda e: e.reciprocal(out=den, in_=den))
                D(lambda e: e.tensor_tensor(out=cr, in0=nr, in1=are, op=ALU.mult))
                D(lambda e: e.tensor_tensor(out=t0, in0=ni, in1=aim, op=ALU.mult))
                D(lambda e: e.tensor_tensor(out=cr, in0=cr, in1=t0, op=ALU.add))
                D(lambda e: e.tensor_tensor(out=cr, in0=cr, in1=den, op=ALU.mult))
                D(lambda e: e.tensor_tensor(out=ci, in0=ni, in1=are, op=ALU.mult))
                D(lambda e: e.tensor_tensor(out=t0, in0=nr, in1=aim, op=ALU.mult))
                D(lambda e: e.tensor_tensor(out=ci, in0=ci, in1=t0, op=ALU.subtract))
                D(lambda e: e.tensor_tensor(out=ci, in0=ci, in1=den, op=ALU.mult))

                u1 = sb("u1b", [128, 16, 9, 32], F32, st2); u2 = sb("u2b", [128, 16, 9, 32], F32, st2)
                PB = sb("PB", [128, 2, 16, 8, 32], BF16, st2)

                def cmul(outr, outi, ar, ai, br, bi, K, neg_i=False):
                    sh = [128, 16, K, 32]
                    A_r = ar.unsqueeze(3).broadcast_to(sh); A_i = ai.unsqueeze(3).broadcast_to(sh)
                    B_r = br.unsqueeze(2).broadcast_to(sh); B_i = bi.unsqueeze(2).broadcast_to(sh)
                    t1v = u1[:, :, 0:K, :]; t2v = u2[:, :, 0:K, :]
                    D(lambda e: e.tensor_tensor(out=t1v, in0=B_r, in1=A_r, op=ALU.mult))
                    D(lambda e: e.tensor_tensor(out=t2v, in0=B_i, in1=A_i, op=ALU.mult))
                    D(lambda e: e.tensor_tensor(out=outr, in0=t1v, in1=t2v, op=ALU.subtract))
                    D(lambda e: e.tensor_tensor(out=t1v, in0=B_i, in1=A_r, op=ALU.mult))
                    D(lambda e: e.tensor_tensor(out=t2v, in0=B_r, in1=A_i, op=ALU.mult))
                    if neg_i:
                        D(lambda e: e.scalar_tensor_tensor(out=outi, in0=t1v, scalar=-1.0, in1=t2v, op0=ALU.mult, op1=ALU.subtract))
                    else:
                        D(lambda e: e.tensor_tensor(out=outi, in0=t1v, in1=t2v, op=ALU.add))

                cmul(BBr[:].unsqueeze(2), BBi[:].unsqueeze(2), cr.unsqueeze(2), ci.unsqueeze(2), bre, bim, 1)
                D(lambda e: e.tensor_copy(out=BBbr[:], in_=BBr[:]))
                D(lambda e: e.tensor_copy(out=BBbi[:], in_=BBi[:]))
                cmul(CAb[:, :, 0, :, :], CAb[:, :, 1, :, :], Are[:], Aim[:], cre_, cim_, 9, neg_i=True)
                cmul(PB[:, 0, :, :, :], PB[:, 1, :, :, :], Are[:, :, 0:8], Aim[:, :, 0:8], BBr[:], BBi[:], 8)
                for c6 in range(6):
                    npair = 3 if c6 < 5 else 1
                    dst = sscr_d[:, QOFF + c6 * 1536: QOFF + c6 * 1536 + npair * 512].rearrange("p (j c t h) -> p j c t h", j=npair, c=2, t=8)
                    S.dma("sp", dst, CAb[:, 3 * c6:3 * c6 + npair, :, 1:9, :], getsem("ssmq"), reads=[B0, sscrb])
                S.op("dve", lambda e: e.memset(Wd[:], 0.0), writes=[Wdb])
                for c6 in range(6):
                    npair = 3 if c6 < 5 else 1
                    bk = palloc()
                    for j in range(npair):
                        G = 3 * c6 + j
                        S.op("pe", lambda e, bk=bk, j=j, G=G: e.matmul(banks[bk][32 * j:32 * j + 32, 0:256], BBbr[:, G, :], CAb[:, G, 0, 0:8, :].rearrange("p k h -> p (k h)"), start=True, stop=False),
                             reads=[B0], writes=[bankb[bk]])
                        S.op("pe", lambda e, bk=bk, j=j, G=G: e.matmul(banks[bk][32 * j:32 * j + 32, 0:256], BBbi[:, G, :], CAb[:, G, 1, 0:8, :].rearrange("p k h -> p (k h)"), start=False, stop=True),
                             reads=[B0], writes=[bankb[bk]])
                    for j in range(npair):
                        S.op("dve", lambda e, bk=bk, c6=c6, j=j: e.tensor_copy(out=Wd[32 * j:32 * j + 32, c6, :, 32 * j:32 * j + 32],
                                                                             in_=banks[bk][32 * j:32 * j + 32, 0:256].rearrange("p (d h) -> p d h", d=8)),
                             reads=[bankb[bk]], writes=[Wdb])
                    pfree(bk)
                for s_ in range(8):
                    for c in (0, 1):
                        bA = palloc(); bB = palloc()
                        for G in range(16):
                            c6 = G // 3; j = G % 3
                            bk, col = (bA, c6 * 128) if c6 < 4 else (bB, (c6 - 4) * 128)
                            S.op("pe", lambda e, bk=bk, col=col, j=j, G=G, c=c, s_=s_: e.matmul(banks[bk][32 * j:32 * j + 32, col:col + 128], PB[:, c, G, 7 - s_, :], ident_bf[:], start=True, stop=True),
                                 reads=[B0, identbb], writes=[bankb[bk]])
                        o = (c * 8 + s_) * 128
                        S.op("dve", lambda e, bA=bA, o=o: e.tensor_copy(out=PdA[0:96, 0:4, o:o + 128], in_=banks[bA][0:96, :].rearrange("p (a c) -> p a c", a=4)), reads=[bankb[bA]], writes=[PdAb])
                        S.op("act", lambda e, bB=bB, o=o: e.copy(out=PdA[0:96, 4, o:o + 128], in_=banks[bB][0:96, 0:128]), reads=[bankb[bB]], writes=[PdAb])
                        S.op("act", lambda e, bB=bB, o=o: e.copy(out=PdA[0:32, 5, o:o + 128], in_=banks[bB][0:32, 128:256]), reads=[bankb[bB]], writes=[PdAb])
                        pfree(bA); pfree(bB)
                for c6 in range(6):
                    nr_ = 96 if c6 < 5 else 32
                    S.dma("sp", sscr_d[0:nr_, PDOFF + c6 * 2048:PDOFF + (c6 + 1) * 2048], PdA[0:nr_, c6, :], getsem("ssmpa"), reads=[PdAb, sscrb])
                S.op("dve", lambda e: e.memset(tcar.t[:], 0.0), reads=[B0], writes=[tabb, tcar.b[0]])

        def ssm_tile(ti, own):
            zb = None
            for c6 in range(6):
                sl, slb = stream.get("s", PDOFF + c6 * 2048, 2048)
                if S.collect:
                    continue
                if zb is None:
                    zb = [palloc() for _ in range(4)]
                w = sl[:, 0:2048].rearrange("p (c s m) -> p c s m", c=2, s=8)
                npair = 3 if c6 < 5 else 1
                for j in range(npair):
                    G = 3 * c6 + j
                    for c in range(2):
                        bk = zb[c * 2 + G // 8]
                        ps = banks[bk][:, (G % 8) * 64:(G % 8 + 1) * 64]
                        for s_ in range(8):
                            mm(bk, ps, w[32 * j:32 * j + 32, c, s_, :], ud.t[32 * j:32 * j + 32, c6, s_ * 64:(s_ + 1) * 64], s_ == 0, s_ == 7, [slb, ud.b[c6]])
            if not S.collect:
                for c in range(2):
                    for hh in range(2):
                        bk = zb[c * 2 + hh]
                        evac_copy(zt.t[:, c, hh * 8:(hh + 1) * 8, :], banks[bk][:, :].rearrange("p (g n) -> p g n", g=8), [bankb[bk]], [zt.b[0]])
                        pfree(bk)
                zr = zt.t[:, 0, :, :]; zi = zt.t[:, 1, :, :]
                RT = [tabb]
                dv(lambda e: e.tensor_tensor(out=ta.t[:], in0=zr, in1=cos2[:], op=ALU.mult), [zt.b[0]] + RT, [ta.b[0]], "pool")
                dv(lambda e: e.tensor_tensor(out=tb_.t[:], in0=zi, in1=sin2[:], op=ALU.mult), [zt.b[0]] + RT, [tb_.b[0]], "pool")
                dv(lambda e: e.tensor_tensor(out=wt.t[:, 0, :, :], in0=ta.t[:], in1=tb_.t[:], op=ALU.add), [ta.b[0], tb_.b[0]], [wt.b[0]], "pool")
                dv(lambda e: e.tensor_tensor(out=ta.t[:], in0=zi, in1=cos2[:], op=ALU.mult), [zt.b[0]] + RT, [ta.b[0]], "pool")
                dv(lambda e: e.tensor_tensor(out=tb_.t[:], in0=zr, in1=sin2[:], op=ALU.mult), [zt.b[0]] + RT, [tb_.b[0]], "pool")
                dv(lambda e: e.tensor_tensor(out=wt.t[:, 1, :, :], in0=ta.t[:], in1=tb_.t[:], op=ALU.subtract), [ta.b[0], tb_.b[0]], [wt.b[0]], "pool")
                def part2():
                    dv(lambda e: e.tensor_tensor(out=tcar.t[:], in0=carry.t[:], in1=r2[:].unsqueeze(1).broadcast_to([128, 2, 16]), op=ALU.mult), [carry.b[0]] + RT, [tcar.b[0]])
                    dv(lambda e: e.tensor_tensor(out=wt.t[:, :, :, 0], in0=wt.t[:, :, :, 0], in1=tcar.t[:], op=ALU.add), [tcar.b[0], wt.b[0]], [wt.b[0]])
                    for c in range(2):
                        wf = wt.t[:, c, :, :].rearrange("p g n -> p (g n)")
                        dv(lambda e, wf=wf: e.tensor_tensor_scan(out=wf, data0=Rtab[:].rearrange("p g n -> p (g n)"), data1=wf, initial=0.0, op0=ALU.mult, op1=ALU.add),
                           [wt.b[0]] + RT, [wt.b[0]])
                    wr = wt.t[:, 0, :, :]; wi = wt.t[:, 1, :, :]
                    dv(lambda e: e.tensor_tensor(out=ta.t[:], in0=wr, in1=cos2[:], op=ALU.mult), [wt.b[0]] + RT, [ta.b[0]], "pool")
                    dv(lambda e: e.tensor_tensor(out=tb_.t[:], in0=wi, in1=sin2[:], op=ALU.mult), [wt.b[0]] + RT, [tb_.b[0]], "pool")
                    dv(lambda e: e.tensor_tensor(out=zt.t[:, 0, :, :], in0=ta.t[:], in1=tb_.t[:], op=ALU.subtract), [ta.b[0], tb_.b[0]], [zt.b[0]], "pool")
                    dv(lambda e: e.tensor_tensor(out=ta.t[:], in0=wi, in1=cos2[:], op=ALU.mult), [wt.b[0]] + RT, [ta.b[0]], "pool")
                    dv(lambda e: e.tensor_tensor(out=tb_.t[:], in0=wr, in1=sin2[:], op=ALU.mult), [wt.b[0]] + RT, [tb_.b[0]], "pool")
                    dv(lambda e: e.tensor_tensor(out=zt.t[:, 1, :, :], in0=ta.t[:], in1=tb_.t[:], op=ALU.add), [ta.b[0], tb_.b[0]], [zt.b[0]], "pool")
                    S.op("pool", lambda e: e.tensor_copy(out=XB.t[:, :, :, 0], in_=carry.t[:]), reads=[carry.b[0]], writes=[XB.b[0]])
                    S.op("pool", lambda e: e.tensor_copy(out=XB.t[:, :, :, 1:65], in_=zt.t[:]), reads=[zt.b[0]], writes=[XB.b[0]])
                    S.op("pool", lambda e: e.tensor_copy(out=carry.t[:], in_=zt.t[:, :, :, 63]), reads=[zt.b[0]], writes=[carry.b[0]])
                return part2
            return None

        def ssm_out():
            dvec = sc("dvec")
            for c6 in range(6):
                npair = 3 if c6 < 5 else 1
                sl, slb = stream.get("s", QOFF + c6 * 1536, npair * 512)
                if S.collect:
                    continue
                q = sl[:, 0:npair * 512].rearrange("p (j c t h) -> p j c t h", j=npair, c=2, t=8)
                bk = palloc()
                nrw = 32 * npair
                for dl in range(8):
                    mm(bk, banks[bk][0:nrw, dl * 64:512], Wd[0:nrw, c6, dl, 0:nrw], ud.t[0:nrw, c6, 0:(8 - dl) * 64], dl == 0, False, [Wdb, ud.b[c6]])
                for j in range(npair):
                    G = 3 * c6 + j
                    for t in range(8):
                        ps = banks[bk][32 * j:32 * j + 32, t * 64:(t + 1) * 64]
                        for c in range(2):
                            mm(bk, ps, q[:, j, c, t, :], XB.t[:, c, G, 0:64], False, (t == 7 and c == 1 and j == npair - 1), [slb, XB.b[0]])
                nr_ = 32 * npair
                t1 = tmp()
                S.op("dve", lambda e, bk=bk, t1=t1, c6=c6, nr_=nr_: e.scalar_tensor_tensor(out=t1.t[0:nr_, :], in0=ud.t[0:nr_, c6, :], scalar=dvec[0:nr_, c6:c6 + 1], in1=banks[bk][0:nr_, :],
                                                                                           op0=ALU.mult, op1=ALU.add),
                     reads=[bankb[bk], ud.b[c6], smallb], writes=[t1.b[0]])
                pfree(bk)
                S.op("act", lambda e, t1=t1, c6=c6, nr_=nr_: e.activation(out=gbuf.t[0:nr_, 8 + c6, :].rearrange("p (n t) -> p t n", t=8), in_=t1.t[0:nr_, :].rearrange("p (t n) -> p t n", t=8),
                                                                         func=AF.Gelu_apprx_tanh),
                     reads=[t1.b[0]], writes=[gbuf.b[8 + c6]])


        def gate(b, c):
            res = []
            def ev(bk):
                t1 = tmp()
                S.op("act", lambda e: e.activation(out=t1.t[:], in_=banks[bk][:, :], func=AF.Sigmoid), reads=[bankb[bk]], writes=[t1.b[0]])
                res.append(t1)
            proj("pg%d_%d" % (b, c), ev)
            return res[0] if res else None

        def merge_and_out():
            for c in range(8):
                first = [True]
                def accumulate(prod_fn, reads):
                    pass
                nbr = 0
                if "ssm" in BR:
                    sg = gate(0, c)
                    sl, slb = wget("glu%d" % c)
                    if not S.collect:
                        w = sl[:, 0:1536].rearrange("p (a k c) -> p a k c", a=2, k=6)
                        ba = palloc(); bb = palloc()
                        for ab, bk in ((0, ba), (1, bb)):
                            for k6 in range(6):
                                nr = 96 if k6 < 5 else 32
                                mm(bk, banks[bk][:, :], w[0:nr, ab, k6, :], gbuf.t[0:nr, 8 + k6, :], k6 == 0, k6 == 5, [slb, gbuf.b[8 + k6]])
                        t1 = tmp()
                        S.op("act", lambda e, bb=bb, t1=t1: e.activation(out=t1.t[:], in_=banks[bb][:, :], func=AF.Sigmoid), reads=[bankb[bb]], writes=[t1.b[0]])
                        pfree(bb)
                        S.op("pool", lambda e, t1=t1, sg=sg: e.tensor_tensor(out=t1.t[:], in0=t1.t[:], in1=sg.t[:], op=ALU.mult), reads=[t1.b[0], sg.b[0]], writes=[t1.b[0]])
                        S.op("dve", lambda e, ba=ba, t1=t1: e.tensor_tensor(out=macc.t[:], in0=banks[ba][:, :], in1=t1.t[:], op=ALU.mult), reads=[bankb[ba], t1.b[0]], writes=[macc.b[0]])
                        pfree(ba)
                    nbr += 1
                for (bname, bi, wn, kk, src0) in (("attn", 1, "aup%d" % c, 64, 18), ("mem", 2, "xup%d" % c, 128, 14)):
                    if bname not in BR:
                        continue
                    sg = gate(bi, c)
                    sl, slb = wget(wn)
                    if not S.collect:
                        w = sl[:, 0:512].rearrange("p (k c) -> p k c", k=4)
                        bk = palloc()
                        for k4 in range(4):
                            mm(bk, banks[bk][:, :], w[0:kk, k4, :], gbuf.t[0:kk, src0 + k4, :], k4 == 0, k4 == 3, [slb, gbuf.b[src0 + k4]])
                        if nbr == 0:
                            S.op("dve", lambda e, bk=bk, sg=sg: e.tensor_tensor(out=macc.t[:], in0=banks[bk][:, :], in1=sg.t[:], op=ALU.mult), reads=[bankb[bk], sg.b[0]], writes=[macc.b[0]])
                        else:
                            S.op("dve", lambda e, bk=bk, sg=sg: e.tensor_tensor(out=sg.t[:], in0=banks[bk][:, :], in1=sg.t[:], op=ALU.mult), reads=[bankb[bk], sg.b[0]], writes=[sg.b[0]])
                            S.op("pool", lambda e, sg=sg: e.tensor_tensor(out=macc.t[:], in0=macc.t[:], in1=sg.t[:], op=ALU.add), reads=[macc.b[0], sg.b[0]], writes=[macc.b[0]])
                        pfree(bk)
                    nbr += 1
                if not S.collect:
                    S.op("act", lambda e, c=c: e.copy(out=gbuf.t[:, c, :], in_=macc.t[:]), reads=[macc.b[0]], writes=[gbuf.b[c]])
            for c2 in range(8):
                sl, slb = wget("wo%d" % c2)
                if S.collect:
                    continue
                bk = palloc()
                for k in range(8):
                    mm(bk, banks[bk][:, :], sl[:, k * 128:(k + 1) * 128], gbuf.t[:, k, :], k == 0, k == 7, [slb, gbuf.b[k]])
                S.op("dve", lambda e, bk=bk, c2=c2: e.tensor_tensor(out=xres.t[:, c2, :], in0=banks[bk][:, :], in1=xres.t[:, c2, :], op=ALU.add),
                     reads=[bankb[bk], xres.b[c2]], writes=[xres.b[c2]])
                pfree(bk)


        def final_out(to):
            if S.collect:
                return
            g = gain["final_norm"]
            for c in range(8):
                if c % 2 == 0:
                    S.op("act", lambda e, c=c: e.activation(out=xn.t[:, c, :], in_=xres.t[:, c, :], func=AF.Square), reads=[xres.b[c]], writes=[xn.b[c]])
                else:
                    S.op("dve", lambda e, c=c: e.tensor_tensor(out=xn.t[:, c, :], in0=xres.t[:, c, :], in1=xres.t[:, c, :], op=ALU.mult), reads=[xres.b[c]], writes=[xn.b[c]])
            bk = palloc()
            for c in range(8):
                mm(bk, banks[bk][:, :], ones_bf[:], xn.t[:, c, :], c == 0, c == 7, [onesb, xn.b[c]])
            t1 = tmp(); t2 = tmp()
            S.op("act", lambda e: e.activation(out=t1.t[:], in_=banks[bk][:, :], func=AF.Ln, scale=1.0 / D, bias=epsb[:]), reads=[bankb[bk], epsbb], writes=[t1.b[0]])
            pfree(bk)
            S.op("act", lambda e: e.activation(out=t2.t[:], in_=t1.t[:], func=AF.Exp, scale=-0.5), reads=[t1.b[0]], writes=[t2.b[0]])
            for c in range(8):
                S.op("dve", lambda e, c=c: e.scalar_tensor_tensor(out=xres.t[:, c, :], in0=xres.t[:, c, :], scalar=g[:, c:c + 1], in1=t2.t[:], op0=ALU.mult, op1=ALU.mult),
                     reads=[xres.b[c], t2.b[0], smallb], writes=[xres.b[c]])
            store_y(to, xres)

        def body():
            with ExitStack() as st2:
                if S.collect:
                    if "mem" in BR and stage != "ffn1":
                        setup_mem(st2)
                else:
                    if "ssm" in BR and stage != "ffn1":
                        setup_ssm(st2)
                    if "mem" in BR and stage != "ffn1":
                        setup_mem(st2)
                if not S.collect:
                    S.barrier()
            if not S.collect:
                alloc_main()
            early = False
            for ti in range(n_prev + n_own):
                own = ti >= n_prev
                if not S.collect and not early:
                    load_x(ti)
                    rmsnorm("ffn1_norm")
                early = False
                ffn(1)
                if stage == "ffn1":
                    if own and not S.collect:
                        store_y(ti - n_prev, xres)
                    continue
                if not S.collect:
                    rmsnorm("mix_norm")
                projections(ti, own)
                if (not own) and ti + 1 < n_prev + n_own and "ssm" in BR:
                    if not S.collect:
                        load_x(ti + 1)
                        rmsnorm("ffn1_norm")
                    early = True
                ssm_p2 = None
                if "ssm" in BR:
                    ssm_p2 = ssm_tile(ti, own)
                    if ssm_p2 is not None and not (own and "attn" in BR):
                        ssm_p2(); ssm_p2 = None
                if not S.collect and ti <= 5:
                    rem = ncb - ncb_first
                    conv_issue(ncb_first + ((ti + 1) * rem + 5) // 6)
                if not own:
                    continue
                if "attn" in BR:
                    attention(ti, ssm_p2)
                if "mem" in BR:
                    cross_attention()
                if "ssm" in BR:
                    ssm_out()
                merge_and_out()
                if not S.collect:
                    rmsnorm("ffn2_norm")
                ffn(2)
                if not S.collect:
                    issue_x(ti + 1)
                final_out(ti - n_prev)

        S.collect = True
        body()
        S.collect = False
        body()
        S.barrier(final=True)
        S.emit(nc, esem, dsem)
    return nc


_CACHE = {}


def kernel(**inp):
    inp = {k: np.asarray(v) for k, v in inp.items()}
    n_prev = CFG["n_prev"]; n_own = CFG["n_own"]
    W = build_wall(inp)
    wall = W.build(8192)
    small, scols, ssmp, pcols = build_small(inp)
    key = (n_prev, n_own, CFG["stage"], tuple(CFG.get("branches", ())))
    if key not in _CACHE:
        _CACHE[key] = build_program(W.off, W.total, scols, small.shape[1], pcols, ssmp.shape[1], W.first_end)
    nc = _CACHE[key]
    x = inp["x"]; mem = inp["mem"]
    in_maps = []
    for core in range(8):
        b = core // 2; half = core % 2
        own = x[b, half * HALF: half * HALF + n_own * T]
        if half == 1:
            prev = x[b, HALF - n_prev * T: HALF]
            flag = np.ones((128, 64), np.float32)
        else:
            prev = np.zeros((n_prev * T, D), np.float32)
            flag = np.zeros((128, 64), np.float32)
        xin = np.ascontiguousarray(np.concatenate([prev, own], axis=0))
        in_maps.append({"xin": xin, "wall": wall, "small": small, "ssmp": ssmp, "flag": flag,
                        "mem": np.ascontiguousarray(mem[b])})
    res = run_bass_kernel_spmd(nc, in_maps, core_ids=list(range(8)))
    out = np.zeros((4, SEQ, D), np.float32)
    for core in range(8):
        b = core // 2; half = core % 2
        out[b, half * HALF: half * HALF + n_own * T] = res.results[core]["y"]
    return out
```

```python
import numpy as np
import math
from contextlib import ExitStack
import concourse.bass as bass
import concourse.mybir as mybir
from concourse.bass_utils import run_bass_kernel_spmd

F32 = mybir.dt.float32
BF16 = mybir.dt.bfloat16
I32 = mybir.dt.int32
AF = mybir.ActivationFunctionType
ALU = mybir.AluOpType
AX = mybir.AxisListType

D = 1024; DFF = 2816; NJ = 22; T = 512; NCH = 8
SEQ = 8192; HALF = 4096
EPS = 1e-6
NEG = -30000.0
RING = (2, 2, 5)
LMAX = (1, 1, 4)
STRIPW = (256, 256, 640)
STRIP_OFF = (0, 1024, 2048)
STRIP_TOT = 4 * (256 + 256 + 640)

CFG = dict(n_prev=8, n_own=8, stage="full")


class Buf:
    __slots__ = ("name", "w", "r")

    def __init__(self, name=""):
        self.name = name
        self.w = None
        self.r = {}


ENGS = ("pe", "act", "dve", "pool", "sp")


class Sched:
    def __init__(self):
        self.ops = {e: [] for e in ENGS}
        self.known = {e: {} for e in ENGS}
        self.dma_cnt = {}
        self.emitted = {e: 0 for e in ENGS}
        self.sigbase = {e: 0 for e in ENGS}
        self.snap = {e: {} for e in ENGS}
        self.collect = False

    def _waits(self, eng, idx, reads, writes):
        deps = {}
        for b in reads:
            if b.w is not None:
                k, v = b.w
                if deps.get(k, -1) < v:
                    deps[k] = v
        for b in writes:
            if b.w is not None:
                k, v = b.w
                if deps.get(k, -1) < v:
                    deps[k] = v
            for k, v in b.r.items():
                if deps.get(k, -1) < v:
                    deps[k] = v
        waits = []
        kn = self.known[eng]
        for k, v in deps.items():
            if k == eng and eng == "pe":
                continue
            if kn.get(k, -1) >= v:
                continue
            kn[k] = v
            waits.append((k, v))
            if not isinstance(k, tuple):
                self.ops[k][v][2] = True
                for k3, v3 in self.ops[k][v][3].items():
                    if kn.get(k3, -1) < v3:
                        kn[k3] = v3
        if waits:
            self.snap[eng] = dict(kn)
        return waits

    def op(self, eng, fn, reads=(), writes=()):
        if self.collect:
            return
        idx = len(self.ops[eng])
        waits = self._waits(eng, idx, reads, writes)
        self.ops[eng].append([fn, waits, False, self.snap[eng]])
        for b in reads:
            if b.r.get(eng, -1) < idx:
                b.r[eng] = idx
        for b in writes:
            b.w = (eng, idx)
            b.r = {}

    def dma(self, q, out_ap, in_ap, sem, reads=(), writes=(), **kw):
        if self.collect:
            return
        val = self.dma_cnt.get(sem, 0) + 16
        self.dma_cnt[sem] = val
        key = ("dma", sem)
        idx = len(self.ops[q])
        waits = self._waits(q, idx, reads, writes)
        self.ops[q].append([("dma", out_ap, in_ap, sem, kw), waits, False, self.snap[q]])
        for b in reads:
            b.r[key] = val
        for b in writes:
            b.w = (key, val)
            b.r = {}

    def barrier(self, final=False):
        if self.collect:
            return
        last = {e: len(self.ops[e]) - 1 for e in ENGS if len(self.ops[e]) > 0}
        for e in ENGS:
            waits = []
            for e2, i2 in last.items():
                if self.known[e].get(e2, -1) >= i2:
                    continue
                self.known[e][e2] = i2
                self.ops[e2][i2][2] = True
                waits.append((e2, i2))
            for sem, val in self.dma_cnt.items():
                k = ("dma", sem)
                if sem.startswith("cv") and not final:
                    continue
                if self.known[e].get(k, -1) >= val:
                    continue
                self.known[e][k] = val
                waits.append((k, val))
            if waits:
                self.snap[e] = dict(self.known[e])
            self.ops[e].append([None, waits, False, self.snap[e]])

    def emit(self, nc, esem, dsem):
        sigcount = {}
        for e in ENGS:
            c = 0
            lst = []
            for o in self.ops[e]:
                if o[2]:
                    c += 1
                lst.append(c)
            sigcount[e] = lst
        ops = self.ops
        emitted = self.emitted

        def run(e_name, eng):
            lst = ops[e_name]
            for i in range(emitted[e_name], len(lst)):
                fn, waits, sig = lst[i][0], lst[i][1], lst[i][2]
                for k, v in waits:
                    if isinstance(k, tuple):
                        eng.wait_ge(dsem[k[1]], v)
                    else:
                        eng.wait_ge(esem[k], sigcount[k][v])
                if fn is None:
                    if sig:
                        eng.nop().then_inc(esem[e_name], 1)
                    continue
                if isinstance(fn, tuple):
                    _, o_ap, i_ap, sem, kw = fn
                    ins = eng.dma_start(out=o_ap, in_=i_ap, **kw).then_inc(dsem[sem], 16)
                    if sig:
                        eng.nop().then_inc(esem[e_name], 1)
                else:
                    ins = fn(eng)
                    if sig:
                        ins.then_inc(esem[e_name], 1)
            emitted[e_name] = len(lst)

        with nc.Block() as block:
            @block.tensor
            def _(e):
                run("pe", e)

            @block.scalar
            def _(e):
                run("act", e)

            @block.vector
            def _(e):
                run("dve", e)

            @block.gpsimd
            def _(e):
                run("pool", e)

            @block.sync
            def _(e):
                run("sp", e)


def t5_causal_buckets(dist):
    dist = np.asarray(dist, np.int32)
    max_exact = 16
    safe = np.maximum(dist, 1).astype(np.float32)
    large = max_exact + (np.log(safe / max_exact) / np.log(2048 / max_exact) * (32 - max_exact)).astype(np.int32)
    large = np.minimum(large, 31)
    return np.where(dist < max_exact, dist, large).astype(np.int32)


class Wall:
    def __init__(self):
        self.parts = []
        self.off = {}
        self.total = 0

    def add(self, name, arr):
        arr = np.ascontiguousarray(arr, dtype=np.float32).reshape(128, -1)
        self.off[name] = (self.total, arr.shape[1])
        self.parts.append(arr)
        self.total += arr.shape[1]

    def build(self, pad_to):
        tot = ((self.total + pad_to - 1) // pad_to) * pad_to
        if tot > self.total:
            self.parts.append(np.zeros((128, tot - self.total), np.float32))
        self.total = tot
        return np.concatenate(self.parts, axis=1)


def kchunks(w):
    k = w.shape[0] // 128
    return w.reshape(k, 128, w.shape[1]).transpose(1, 0, 2)


def ssm_feat_index():
    idx = -np.ones((6, 128), np.int64)
    for c6 in range(6):
        for p in range(96):
            G = 3 * c6 + p // 32
            if G < 16:
                idx[c6, p] = G * 32 + p % 32
    return idx


def build_wall(inp, names_only=False):
    W = Wall()
    uidx = ssm_feat_index()
    z = lambda *s: np.zeros(s, np.float32)
    rel = inp["rel_table"]
    strips = np.full((128, STRIP_TOT), NEG, np.float32)
    kk = np.arange(128)[:, None]
    for g, (dil, cs) in enumerate(((1, 1), (4, 4), (16, 4))):
        Wd = STRIPW[g]
        X = np.arange(Wd)[None, :]
        dl = X - kk
        dist = dl * cs
        valid = (dl >= 0) & (dist % dil == 0) & (dist // dil <= 128)
        bk = t5_causal_buckets(np.maximum(dist, 0))
        for h in range(4):
            vals = rel[bk, 4 * g + h]
            o = 4 * STRIP_OFF[g] // 4 * 0 + sum(4 * STRIPW[i] for i in range(g)) + h * Wd
            strips[:, o:o + Wd] = np.where(valid, vals, NEG)
    W.add("strips", strips)
    kv = kchunks(inp["xattn_w_kv"][0])
    for h in range(4):
        W.add("xk%d" % h, kv[:, :, h * 128:(h + 1) * 128])
    W.add("xv0", kv[:, 0:4, 512:1024])
    W.add("xv1", kv[:, 4:8, 512:1024])
    FF = (1,)
    ffn_w = {1: (inp["ffn1_w_in"], inp["ffn1_w_down"]), 2: (inp["ffn2_w_in"], inp["ffn2_w_down"])}
    for f in FF:
        w_in = ffn_w[f][0][0]
        w_dn = ffn_w[f][1][0]
        kin = kchunks(w_in)
        for j in range(NJ):
            a = kin[:, :, j * 128:(j + 1) * 128]
            b = kin[:, :, DFF + j * 128:DFF + (j + 1) * 128]
            W.add("ffn%d_in%d" % (f, j), np.stack([a, b], axis=1))
        kdn = kchunks(w_dn)
        for m in range(8):
            W.add("ffn%d_dn%d_0" % (f, m), kdn[:, 0:11, m * 128:(m + 1) * 128])
            W.add("ffn%d_dn%d_1" % (f, m), kdn[:, 11:22, m * 128:(m + 1) * 128])
    w_in = inp["w_in"][0]
    kw = kchunks(w_in)
    for c6 in range(6):
        a = z(128, 8, 128)
        for col in range(128):
            fi = uidx[c6, col]
            if fi >= 0:
                a[:, :, col] = kw[:, :, fi]
        W.add("pu%d" % c6, a)
    for c in range(6):
        W.add("pk%d" % c, kw[:, :, 1280 + c * 128:1280 + (c + 1) * 128])
    for g in range(3):
        W.add("pv%d" % g, kw[:, :, 2048 + g * 256:2048 + (g + 1) * 256])
    W.first_end = W.total
    for c in range(6):
        W.add("pq%d" % c, kw[:, :, 512 + c * 128:512 + (c + 1) * 128])
    for c in range(4):
        W.add("pxq%d" % c, kw[:, :, 2816 + c * 128:2816 + (c + 1) * 128])
    for b in range(3):
        for c in range(8):
            o = 3328 + b * 1024 + c * 128
            W.add("pg%d_%d" % (b, c), kw[:, :, o:o + 128])
    glu = inp["ssm_w_glu"][0]
    gk = z(128, 6, 2048)
    for c6 in range(6):
        for p in range(128):
            fi = uidx[c6, p]
            if fi >= 0:
                gk[p, c6, :] = glu[fi, :]
    for c in range(8):
        a = gk[:, :, c * 128:(c + 1) * 128]
        b = gk[:, :, 1024 + c * 128:1024 + (c + 1) * 128]
        W.add("glu%d" % c, np.stack([a, b], axis=1))
    aup = inp["attn_w_up"][0]
    ak = z(128, 4, 1024)
    ak[:64] = aup.reshape(4, 64, 1024).transpose(1, 0, 2)
    for c in range(8):
        W.add("aup%d" % c, ak[:, :, c * 128:(c + 1) * 128])
    xk = kchunks(inp["xattn_w_up"][0])
    for c in range(8):
        W.add("xup%d" % c, xk[:, :, c * 128:(c + 1) * 128])
    wo = kchunks(inp["w_out"][0])
    for c in range(8):
        W.add("wo%d" % c, wo[:, :, c * 128:(c + 1) * 128])
    FF = (2,)
    for f in FF:
        w_in = ffn_w[f][0][0]
        w_dn = ffn_w[f][1][0]
        kin = kchunks(w_in)
        for j in range(NJ):
            a = kin[:, :, j * 128:(j + 1) * 128]
            b = kin[:, :, DFF + j * 128:DFF + (j + 1) * 128]
            W.add("ffn%d_in%d" % (f, j), np.stack([a, b], axis=1))
        kdn = kchunks(w_dn)
        for m in range(8):
            W.add("ffn%d_dn%d_0" % (f, m), kdn[:, 0:11, m * 128:(m + 1) * 128])
            W.add("ffn%d_dn%d_1" % (f, m), kdn[:, 11:22, m * 128:(m + 1) * 128])
    return W


def strip_off(g, h):
    return sum(4 * STRIPW[i] for i in range(g)) + h * STRIPW[g]


def pairs_layout(a):
    s = a.shape
    a = a.reshape((16, 2, 64) + s[2:])
    perm = (1, 2, 0) + tuple(range(3, a.ndim))
    return np.ascontiguousarray(a.transpose(perm)).reshape((128, 16) + s[2:])


def build_small(inp):
    cols = {}
    parts = []
    tot = [0]

    def add(name, arr):
        arr = np.ascontiguousarray(arr, np.float32).reshape(128, -1)
        cols[name] = (tot[0], arr.shape[1])
        parts.append(arr)
        tot[0] += arr.shape[1]

    for nm in ("ffn1_norm", "mix_norm", "ffn2_norm", "mem_norm"):
        add(nm, inp[nm][0].reshape(8, 128).T)
    add("final_norm", inp["final_norm"].reshape(8, 128).T)
    add("ident", np.eye(128, dtype=np.float32))
    add("nidx", np.tile(np.arange(1, 65, dtype=np.float32)[None, :], (128, 1)))
    add("kidx", np.tile(np.arange(0, 9, dtype=np.float32)[None, :], (128, 1)))
    are = inp["ssm_a_re"][0]; aim = inp["ssm_a_im"][0]
    add("are", pairs_layout(are))
    add("aim", pairs_layout(aim))
    ldt = np.repeat(inp["ssm_log_dt"][0][:, None], 64, axis=1)
    add("ldt", pairs_layout(ldt))
    for nm in ("ssm_b_re", "ssm_b_im"):
        b = pairs_layout(inp[nm][0])
        bp = np.zeros((128, 16, 32), np.float32)
        bp[:64, :, 0:16] = b[:64]
        bp[64:, :, 16:32] = b[64:]
        add(nm, bp)
    for nm in ("ssm_c_re", "ssm_c_im"):
        c = inp[nm][0].transpose(0, 2, 1)
        c = pairs_layout(c)
        cp = np.zeros((128, 16, 32), np.float32)
        cp[:64, :, 0:16] = c[:64]
        cp[64:, :, 16:32] = c[64:]
        add(nm, cp)
    dsk = inp["ssm_d"][0].reshape(512)
    uidx = ssm_feat_index()
    dv = np.zeros((128, 6), np.float32)
    for c6 in range(6):
        for p in range(128):
            if uidx[c6, p] >= 0:
                dv[p, c6] = dsk[uidx[c6, p]]
    add("dvec", dv)
    P_NAMES = ("ffn1_norm", "mix_norm", "ffn2_norm", "mem_norm", "final_norm", "ident", "dvec")
    pa, pc, sa, sc_ = [], {}, [], {}
    po = so = 0
    for (name, (o, n)), arr in zip(cols.items(), parts):
        if name in P_NAMES:
            pa.append(arr); pc[name] = (po, n); po += n
        else:
            sa.append(arr); sc_[name] = (so, n); so += n
    return np.concatenate(pa, axis=1), pc, np.concatenate(sa, axis=1), sc_


class TileB:
    def __init__(self, t, n=1):
        self.t = t
        self.b = [Buf() for _ in range(n)]


def build_program(woff, wtotal, scols, nsmall, pcols, nssmp, first_end):
    n_prev = CFG["n_prev"]; n_own = CFG["n_own"]; stage = CFG["stage"]
    NTOK = (n_prev + n_own) * T
    nc = bass.Bass("TRN2", target_bir_lowering=False)
    xin_d = nc.dram_tensor("xin", [NTOK, D], F32, kind="ExternalInput").ap()
    wall_d = nc.dram_tensor("wall", [128, wtotal], F32, kind="ExternalInput").ap()
    small_d = nc.dram_tensor("small", [128, nsmall], F32, kind="ExternalInput").ap()
    ssmp_d = nc.dram_tensor("ssmp", [128, nssmp], F32, kind="ExternalInput").ap()
    flag_d = nc.dram_tensor("flag", [128, 64], F32, kind="ExternalInput").ap()
    mem_d = nc.dram_tensor("mem", [256, D], F32, kind="ExternalInput").ap()
    y_d = nc.dram_tensor("y", [n_own * T, D], F32, kind="ExternalOutput").ap()
    scr_d = nc.dram_tensor("scr", [128, wtotal], BF16).ap()
    SSMW = 6 * 2048 + 4 * 2048
    sscr_d = nc.dram_tensor("sscr", [128, SSMW], BF16).ap()

    S = Sched()
    CB = 8192
    ncb = wtotal // CB
    with ExitStack() as st:
        esem = {e: st.enter_context(nc.semaphore("e_" + e)) for e in ENGS}
        dsem = {}

        def getsem(name):
            if name not in dsem:
                dsem[name] = st.enter_context(nc.semaphore("d_" + name))
            return name

        uniq = [0]

        def sb(name, shape, dt, stack=st):
            uniq[0] += 1
            return stack.enter_context(nc.sbuf_tensor("s%d_%s" % (uniq[0], name), shape, dt))

        cvb = [Buf("cv%d" % i) for i in range(ncb)]
        cv_next = [0]

        def conv_issue(upto):
            upto = min(upto, ncb)
            while cv_next[0] < upto:
                i = cv_next[0]
                S.dma("pool", scr_d[:, i * CB:(i + 1) * CB], wall_d[:, i * CB:(i + 1) * CB],
                      getsem("cv%d" % i), writes=[cvb[i]], max_dma_last_dim=8192)
                cv_next[0] += 1
        ncb_first = (first_end + CB - 1) // CB
        conv_issue(ncb_first)

        small = sb("small", [128, nsmall], F32)
        smallb = Buf("small")
        S.dma("sp", small[:], small_d[:, :], getsem("small"), writes=[smallb])

        def sc(name, a=None, b=None):
            o, n = scols[name]
            if a is None:
                return small[:, o:o + n]
            return small[:, o + a:o + b]

        ident = sc("ident")
        ones_bf = sb("ones_bf", [128, 128], BF16)
        onesb = Buf("ones")
        S.op("dve", lambda e: e.memset(ones_bf[:], 1.0), writes=[onesb])
        ident_bf = sb("ident_bf", [128, 128], BF16)
        identbb = Buf("identbf")
        S.op("dve", lambda e: e.tensor_copy(out=ident_bf[:], in_=ident), reads=[smallb], writes=[identbb])
        flag_f = sb("flag_f", [128, 64], F32)
        flag_bf = sb("flag_bf", [128, 64], BF16)
        flagb = Buf("flag")
        S.dma("sp", flag_f[:], flag_d[:, :], getsem("flag"), writes=[flagb])
        S.op("dve", lambda e: e.tensor_copy(out=flag_bf[:], in_=flag_f[:]), reads=[flagb], writes=[flagb])
        epsb = sb("epsb", [128, 1], F32)
        epsbb = Buf("eps")
        S.op("dve", lambda e: e.memset(epsb[:], EPS), writes=[epsbb])

        strips = sb("strips", [128, STRIP_TOT], BF16)
        stripb = Buf("strips")
        so_, sn_ = woff["strips"]
        S.dma("sp", strips[:], scr_d[:, so_:so_ + sn_], getsem("strips"),
              reads=[cvb[i] for i in range(so_ // CB, (so_ + sn_ - 1) // CB + 1)], writes=[stripb])

        NBANK = 8
        banks = [st.enter_context(nc.psum_tensor("bank%d" % i, [128, 512], F32)) for i in range(NBANK)]
        bankb = [Buf("bank%d" % i) for i in range(NBANK)]
        bank_free = list(range(NBANK))

        def palloc():
            assert bank_free, "psum pool exhausted"
            return bank_free.pop(0)

        def pfree(i):
            bank_free.append(i)

        NSLOT = 4
        SLOTF = 2048
        slots = [sb("slot%d" % i, [128, SLOTF], BF16) for i in range(NSLOT)]
        slotb = [Buf("slot%d" % i) for i in range(NSLOT)]
        for i in range(NSLOT):
            getsem("slot%d" % i)

        class Stream:
            def __init__(self):
                self.plan = []
                self.pos = 0
                self.issued = 0

            def get(self, src, off, n):
                if S.collect:
                    self.plan.append((src, off, n))
                    return None, None
                r = self.pos
                assert self.plan[r] == (src, off, n), (r, self.plan[r], (src, off, n))
                self.pos += 1
                while self.issued < len(self.plan) and self.issued <= r + (NSLOT - 1):
                    q = self.issued
                    s2, o2, n2 = self.plan[q]
                    sl = q % NSLOT
                    if s2 == "w":
                        rd = [cvb[i] for i in range(o2 // CB, (o2 + n2 - 1) // CB + 1)]
                        S.dma("sp", slots[sl][:, 0:n2], scr_d[:, o2:o2 + n2], "slot%d" % sl, reads=rd, writes=[slotb[sl]])
                    else:
                        S.dma("sp", slots[sl][:, 0:n2], sscr_d[:, o2:o2 + n2], "slot%d" % sl, reads=[sscrb], writes=[slotb[sl]])
                    self.issued += 1
                sl = r % NSLOT
                return slots[sl], slotb[sl]

        stream = Stream()
        sscrb = Buf("sscr")

        def wget(name):
            o, n = woff[name]
            return stream.get("w", o, n)

        xres = xn = gbuf = xin_t = yout_t = tmpf = ud = qT = xqT = kT = vtm = accN = accD = PT = zt = wt = ta = tb_ = macc = None
        tmpi = [0]

        def tmp():
            tmpi[0] = (tmpi[0] + 1) % 3
            return tmpf[tmpi[0]]

        def alloc_main():
            nonlocal xres, xn, gbuf, xin_t, yout_t, tmpf, ud, qT, xqT, kT, vtm, accN, accD, PT, zt, wt, ta, tb_, macc
            xres = TileB(sb("xres", [128, 8, T], F32), 8)
            xn = TileB(sb("xn", [128, 8, T], BF16), 8)
            gbuf = TileB(sb("gbuf", [128, NJ, T], BF16), NJ)
            xin_t = [TileB(sb("xin%d" % i, [128, D], F32)) for i in range(2)]
            yout_t = xin_t
            tmpf = [TileB(sb("tmpf%d" % i, [128, T], F32)) for i in range(3)]
            ud = TileB(sb("ud", [128, 6, T], BF16), 6)
            qT = TileB(sb("qT", [128, 12, T], BF16), 12)
            S.op("pool", lambda e: e.memset(qT.t[:], 0.0), writes=qT.b)
            xqT = TileB(sb("xqT", [128, 4, T], BF16), 4)
            kT = [TileB(sb("kT%d" % g, [128, 2, RING[g] * T], BF16), RING[g]) for g in range(3)]
            vtm = [TileB(sb("vtm%d" % g, [128, RING[g], 4, 256], BF16), RING[g]) for g in range(3)]
            accN = TileB(sb("accN", [64, T], F32)); accD = TileB(sb("accD", [64, T], F32))
            PT = [TileB(sb("PT%d" % i, [128, 640], BF16)) for i in range(3)]
            zt = TileB(sb("zt", [128, 2, 16, 64], F32)); wt = TileB(sb("wt", [128, 2, 16, 64], F32))
            ta = TileB(sb("ta", [128, 16, 64], F32)); tb_ = TileB(sb("tb", [128, 16, 64], F32))
            macc = TileB(sb("macc", [128, T], F32))
            for i in range(2):
                getsem("xin%d" % i)

        gain = {nm: sc(nm) for nm in ("ffn1_norm", "mix_norm", "ffn2_norm", "mem_norm", "final_norm")}

        def mm(bank, ps_ap, lhsT, rhs, start, stop, reads):
            S.op("pe", lambda e: e.matmul(ps_ap, lhsT, rhs, start=start, stop=stop), reads=reads, writes=[bankb[bank]])

        def rmsnorm(gname):
            g = gain[gname]
            for c in range(8):
                if c % 2 == 0:
                    S.op("act", lambda e, c=c: e.activation(out=xn.t[:, c, :], in_=xres.t[:, c, :], func=AF.Square),
                         reads=[xres.b[c]], writes=[xn.b[c]])
                else:
                    S.op("dve", lambda e, c=c: e.tensor_tensor(out=xn.t[:, c, :], in0=xres.t[:, c, :], in1=xres.t[:, c, :], op=ALU.mult),
                         reads=[xres.b[c]], writes=[xn.b[c]])
            bk = palloc()
            for c in range(8):
                mm(bk, banks[bk][:, :], ones_bf[:], xn.t[:, c, :], c == 0, c == 7, [onesb, xn.b[c]])
            t1 = tmp(); t2 = tmp()
            S.op("act", lambda e: e.activation(out=t1.t[:], in_=banks[bk][:, :], func=AF.Ln, scale=1.0 / D, bias=epsb[:]),
                 reads=[bankb[bk], epsbb], writes=[t1.b[0]])
            pfree(bk)
            S.op("act", lambda e: e.activation(out=t2.t[:], in_=t1.t[:], func=AF.Exp, scale=-0.5),
                 reads=[t1.b[0]], writes=[t2.b[0]])
            for c in range(8):
                S.op("dve", lambda e, c=c: e.scalar_tensor_tensor(out=xn.t[:, c, :], in0=xres.t[:, c, :], scalar=g[:, c:c + 1],
                                                                  in1=t2.t[:], op0=ALU.mult, op1=ALU.mult),
                     reads=[xres.b[c], t2.b[0], smallb], writes=[xn.b[c]])
            return t2

        def ffn(f):
            for j in range(NJ):
                sl, slb = wget("ffn%d_in%d" % (f, j))
                if S.collect:
                    continue
                w = sl[:, 0:2048].rearrange("p (a k c) -> p a k c", a=2, k=8)
                ba = palloc(); bb = palloc()
                if j == 0:
                    for k in range(8):
                        mm(ba, banks[ba][:, :], w[:, 0, k, :], xn.t[:, k, :], k == 0, k == 7, [slb, xn.b[k]])
                        mm(bb, banks[bb][:, :], w[:, 1, k, :], xn.t[:, k, :], k == 0, k == 7, [slb, xn.b[k]])
                else:
                    for k in range(8):
                        mm(ba, banks[ba][:, :], w[:, 0, k, :], xn.t[:, k, :], k == 0, k == 7, [slb, xn.b[k]])
                    for k in range(8):
                        mm(bb, banks[bb][:, :], w[:, 1, k, :], xn.t[:, k, :], k == 0, k == 7, [slb, xn.b[k]])
                t1 = tmp()
                S.op("act", lambda e, ba=ba, t1=t1: e.activation(out=t1.t[:], in_=banks[ba][:, :], func=AF.Silu),
                     reads=[bankb[ba]], writes=[t1.b[0]])
                pfree(ba)
                S.op("dve", lambda e, bb=bb, t1=t1, j=j: e.tensor_tensor(out=gbuf.t[:, j, :], in0=t1.t[:], in1=banks[bb][:, :], op=ALU.mult),
                     reads=[bankb[bb], t1.b[0]], writes=[gbuf.b[j]])
                pfree(bb)
            for m in range(8):
                bk = None
                for hf in range(2):
                    sl, slb = wget("ffn%d_dn%d_%d" % (f, m, hf))
                    if S.collect:
                        continue
                    if bk is None:
                        bk = palloc()
                    for j in range(hf * 11, hf * 11 + 11):
                        mm(bk, banks[bk][:, :], sl[:, (j % 11) * 128:(j % 11 + 1) * 128], gbuf.t[:, j, :], j == 0, j == NJ - 1, [slb, gbuf.b[j]])
                if S.collect:
                    continue
                S.op("dve", lambda e, bk=bk, m=m: e.scalar_tensor_tensor(out=xres.t[:, m, :], in0=banks[bk][:, :], scalar=0.5,
                                                                         in1=xres.t[:, m, :], op0=ALU.mult, op1=ALU.add),
                     reads=[bankb[bk], xres.b[m]], writes=[xres.b[m]])
                pfree(bk)

        x_issued = set()

        def issue_x(ti):
            if ti in x_issued or ti >= n_prev + n_own:
                return
            x_issued.add(ti)
            for j in range(4):
                r0 = ti * T + j * 128
                dst = gbuf.t[:, 4 * j:4 * j + 4, :].rearrange("p a b -> p (a b)").bitcast(F32)
                S.dma("sp", dst, xin_d[r0:r0 + 128, :], getsem("xg%d" % j), writes=[gbuf.b[4 * j + i] for i in range(4)])

        def load_x(ti):
            issue_x(ti)
            for j in range(4):
                xv = gbuf.t[:, 4 * j:4 * j + 4, :].rearrange("p a b -> p (a b)").bitcast(F32)
                xb = [gbuf.b[4 * j + i] for i in range(4)]
                for half in range(2):
                    bk = palloc()
                    for cc in range(4):
                        c = half * 4 + cc
                        S.op("pe", lambda e, bk=bk, cc=cc, c=c, xv=xv: e.transpose(out=banks[bk][:, cc * 128:(cc + 1) * 128],
                                                                                  in_=xv[:, c * 128:(c + 1) * 128], identity=ident),
                             reads=xb + [smallb], writes=[bankb[bk]])
                    eng = "act" if half == 0 else "dve"
                    dst = xres.t[:, half * 4:half * 4 + 4, j * 128:(j + 1) * 128]
                    src = banks[bk][:, :].rearrange("p (c t) -> p c t", c=4)
                    if eng == "act":
                        S.op("act", lambda e, dst=dst, src=src: e.copy(out=dst, in_=src),
                             reads=[bankb[bk]], writes=[xres.b[half * 4 + i] for i in range(4)])
                    else:
                        S.op("dve", lambda e, dst=dst, src=src: e.tensor_copy(out=dst, in_=src),
                             reads=[bankb[bk]], writes=[xres.b[half * 4 + i] for i in range(4)])
                    pfree(bk)

        def store_y(to, src_scaled):
            for j in range(4):
                yt = yout_t[j % 2]
                for half in range(2):
                    bk = palloc()
                    for cc in range(4):
                        c = half * 4 + cc
                        S.op("pe", lambda e, bk=bk, cc=cc, c=c, j=j: e.transpose(out=banks[bk][:, cc * 128:(cc + 1) * 128],
                                                                                in_=src_scaled.t[:, c, j * 128:(j + 1) * 128], identity=ident),
                             reads=[src_scaled.b[c], smallb], writes=[bankb[bk]])
                    dst = yt.t[:, half * 512:(half + 1) * 512]
                    if half == 0:
                        S.op("act", lambda e, dst=dst, bk=bk: e.copy(out=dst, in_=banks[bk][:, :]), reads=[bankb[bk]], writes=[yt.b[0]])
                    else:
                        S.op("dve", lambda e, dst=dst, bk=bk: e.tensor_copy(out=dst, in_=banks[bk][:, :]), reads=[bankb[bk]], writes=[yt.b[0]])
                    pfree(bk)
                r0 = to * T + j * 128
                S.dma("sp", y_d[r0:r0 + 128, :], yt.t[:], "xin%d" % (j % 2), reads=[yt.b[0]])


        BR = CFG.get("branches", ("ssm", "attn", "mem"))
        mkT = TileB(sb("mkT", [128, 4, 256], BF16)); mv = TileB(sb("mv", [128, 2, 512], BF16))
        pti = [0]
        evi = [0]

        def evac_copy(dst, src, reads, writes, scale=None):
            evi[0] += 1
            if evi[0] % 2 == 0:
                if scale is None:
                    S.op("act", lambda e: e.copy(out=dst, in_=src), reads=reads, writes=writes)
                else:
                    S.op("act", lambda e: e.mul(dst, src, scale), reads=reads, writes=writes)
            else:
                if scale is None:
                    S.op("dve", lambda e: e.tensor_copy(out=dst, in_=src), reads=reads, writes=writes)
                else:
                    S.op("dve", lambda e: e.tensor_scalar(out=dst, in0=src, scalar1=scale, scalar2=None, op0=ALU.mult), reads=reads, writes=writes)

        def proj(wname, evac):
            sl, slb = wget(wname)
            if S.collect:
                return
            bk = palloc()
            for k in range(8):
                mm(bk, banks[bk][:, :], sl[:, k * 128:(k + 1) * 128], xn.t[:, k, :], k == 0, k == 7, [slb, xn.b[k]])
            evac(bk)
            pfree(bk)

        def setup_mem(st2):
            if True:
                mt = sb("memt", [128, 2, D], F32, st2)
                mtb = [Buf(), Buf()]
                sq = sb("memsq", [128, D], F32, st2); sqb = Buf()
                ss = sb("memss", [128, 4], F32, st2); ssb = Buf()
                memnT = TileB(sb("memnT", [128, 8, 256], BF16, st2))
                if not S.collect:
                    for mb in range(2):
                        S.dma("sp", mt[:, mb, :], mem_d[mb * 128:(mb + 1) * 128, :], getsem("mem%d" % mb), writes=[mtb[mb]])
                        S.op("dve", lambda e, mb=mb: e.tensor_tensor(out=sq[:], in0=mt[:, mb, :], in1=mt[:, mb, :], op=ALU.mult),
                             reads=[mtb[mb]], writes=[sqb])
                        S.op("dve", lambda e, mb=mb: e.tensor_reduce(out=ss[:, mb:mb + 1], in_=sq[:], axis=AX.X, op=ALU.add),
                             reads=[sqb], writes=[ssb])
                    S.op("act", lambda e: e.activation(out=ss[:, 2:4], in_=ss[:, 0:2], func=AF.Ln, scale=1.0 / D, bias=epsb[:]),
                         reads=[ssb, epsbb], writes=[ssb])
                    S.op("act", lambda e: e.activation(out=ss[:, 0:2], in_=ss[:, 2:4], func=AF.Exp, scale=-0.5), reads=[ssb], writes=[ssb])
                    for mb in range(2):
                        S.op("dve", lambda e, mb=mb: e.tensor_scalar(out=mt[:, mb, :], in0=mt[:, mb, :], scalar1=ss[:, mb:mb + 1], scalar2=None, op0=ALU.mult),
                             reads=[ssb, mtb[mb]], writes=[mtb[mb]])
                    gm = gain["mem_norm"]
                    for mb in range(2):
                        for half in range(2):
                            bk = palloc()
                            for cc in range(4):
                                c = half * 4 + cc
                                S.op("pe", lambda e, bk=bk, cc=cc, c=c, mb=mb: e.transpose(out=banks[bk][:, cc * 128:(cc + 1) * 128],
                                                                                          in_=mt[:, mb, c * 128:(c + 1) * 128], identity=ident),
                                     reads=[mtb[mb], smallb], writes=[bankb[bk]])
                            for cc in range(4):
                                c = half * 4 + cc
                                S.op("dve", lambda e, bk=bk, cc=cc, c=c, mb=mb: e.tensor_scalar(out=memnT.t[:, c, mb * 128:(mb + 1) * 128], in0=banks[bk][:, cc * 128:(cc + 1) * 128],
                                                                                               scalar1=gm[:, c:c + 1], scalar2=None, op0=ALU.mult),
                                     reads=[bankb[bk], smallb], writes=[memnT.b[0]])
                            pfree(bk)
                for h in range(4):
                    sl, slb = wget("xk%d" % h)
                    if S.collect:
                        continue
                    bk = palloc()
                    for k in range(8):
                        mm(bk, banks[bk][:, 0:256], sl[:, k * 128:(k + 1) * 128], memnT.t[:, k, :], k == 0, k == 7, [slb, memnT.b[0]])
                    S.op("act", lambda e, bk=bk, h=h: e.mul(mkT.t[:, h, :], banks[bk][:, 0:256], 128.0 ** -0.5), reads=[bankb[bk]], writes=[mkT.b[0]])
                    pfree(bk)
                bks = None
                for hf in range(2):
                    ssl, sslb = wget("xv%d" % hf)
                    if S.collect:
                        continue
                    if bks is None:
                        bks = [palloc(), palloc()]
                    for k in range(hf * 4, hf * 4 + 4):
                        for mb in range(2):
                            mm(bks[mb], banks[bks[mb]][:, :], memnT.t[:, k, mb * 128:(mb + 1) * 128], ssl[:, (k % 4) * 512:(k % 4 + 1) * 512], k == 0, k == 7, [sslb, memnT.b[0]])
                if not S.collect:
                    for mb in range(2):
                        bk = bks[mb]
                        S.op("act", lambda e, bk=bk, mb=mb: e.copy(out=mv.t[:, mb, :], in_=banks[bk][:, :]), reads=[bankb[bk]], writes=[mv.b[0]])
                        pfree(bk)

        def tokgrp(ap, g, grp):
            if g == 0:
                return ap[:, grp * 128:(grp + 1) * 128]
            return ap.rearrange("p (i r) -> p r i", r=4)[:, grp, :]

        def projections(ti, own):
            slot = [ti % RING[g] for g in range(3)]
            if "ssm" in BR:
                for c6 in range(6):
                    def ev(bk, c6=c6):
                        dst = ud.t[:, c6, :].rearrange("p (s n) -> p n s", s=8)
                        src = banks[bk][:, :].rearrange("p (n s) -> p n s", s=8)
                        evac_copy(dst, src, [bankb[bk]], [ud.b[c6]])
                    proj("pu%d" % c6, ev)
            if "attn" in BR and ti >= n_prev - 4:
                for c in range(6):
                    g = c // 2; cc = c % 2
                    def ev(bk, g=g, cc=cc):
                        evac_copy(kT[g].t[:, cc, slot[g] * T:(slot[g] + 1) * T], banks[bk][:, :], [bankb[bk]], [kT[g].b[slot[g]]])
                    proj("pk%d" % c, ev)
                for g in range(3):
                    sl, slb = wget("pv%d" % g)
                    if S.collect:
                        continue
                    w = sl[:, 0:2048].rearrange("p (k c) -> p k c", k=8)
                    for gg in range(2):
                        bk = palloc()
                        for q2 in range(2):
                            grp = gg * 2 + q2
                            for k in range(8):
                                mm(bk, banks[bk][:, q2 * 256:(q2 + 1) * 256], tokgrp(xn.t[:, k, :], g, grp), w[:, k, :], k == 0, k == 7, [slb, xn.b[k]])
                        dst = vtm[g].t[:, slot[g], gg * 2:gg * 2 + 2, :]
                        src = banks[bk][:, :].rearrange("p (a c) -> p a c", a=2)
                        evac_copy(dst, src, [bankb[bk]], [vtm[g].b[slot[g]]])
                        pfree(bk)
            if not own:
                return
            if "attn" in BR:
                for c in range(6):
                    def ev(bk, c=c):
                        evac_copy(qT.t[0:64, 2 * c, :], banks[bk][0:64, :], [bankb[bk]], [qT.b[2 * c]], scale=0.125)
                        evac_copy(qT.t[64:128, 2 * c + 1, :], banks[bk][64:128, :], [bankb[bk]], [qT.b[2 * c + 1]], scale=0.125)
                    proj("pq%d" % c, ev)
            if "mem" in BR:
                for c in range(4):
                    def ev(bk, c=c):
                        evac_copy(xqT.t[:, c, :], banks[bk][:, :], [bankb[bk]], [xqT.b[c]])
                    proj("pxq%d" % c, ev)

        def attention(ti, mid_cb=None):
            if S.collect:
                return
            accs = {}
            cb = [mid_cb]

            def phase_a(h, g, grp):
                cc = h // 2
                if h == 2 and g == 0 and grp == 0 and cb[0] is not None:
                    cb[0](); cb[0] = None
                if grp == 0:
                    accs[(h, g)] = (palloc(), palloc())
                blocks = []
                for L in range(LMAX[g] + 1):
                    if g == 0:
                        if L == 0:
                            tj, kb = ti, grp
                        else:
                            tj, kb = (ti, grp - 1) if grp >= 1 else (ti - 1, 3)
                    else:
                        tj, kb = ti - L, grp
                    if tj < 0:
                        continue
                    blocks.append((L, tj, kb))
                nb = len(blocks)
                sbk = [palloc() for _ in range((nb + 3) // 4)]
                pt = PT[pti[0] % 3]; pti[0] += 1
                qi = 4 * g + h
                qap = tokgrp(qT.t[:, qi, :], g, grp)
                so = strip_off(g, h)
                for bi4 in range(len(sbk)):
                    n4 = min(4, nb - bi4 * 4)
                    bk = sbk[bi4]
                    L0 = blocks[bi4 * 4][0]
                    mm(bk, banks[bk][:, 0:n4 * 128], ident_bf[:], strips[:, so + L0 * 128: so + (L0 + n4) * 128], True, False, [identbb, stripb])
                for bi, (L, tj, kb) in enumerate(blocks):
                    ks = tj % RING[g]
                    kap = tokgrp(kT[g].t[:, cc, ks * T:(ks + 1) * T], g, kb)
                    bk = sbk[bi // 4]
                    ps = banks[bk][:, (bi % 4) * 128:(bi % 4 + 1) * 128]
                    last = (bi % 4 == 3) or (bi == nb - 1)
                    mm(bk, ps, kap, qap, False, last, [kT[g].b[ks], qT.b[qi]])
                for bi4 in range(len(sbk)):
                    n4 = min(4, nb - bi4 * 4)
                    bk = sbk[bi4]
                    S.op("act", lambda e, bk=bk, n4=n4, bi4=bi4, pt=pt: e.activation(out=pt.t[:, bi4 * 512: bi4 * 512 + n4 * 128], in_=banks[bk][:, 0:n4 * 128], func=AF.Exp),
                         reads=[bankb[bk]], writes=[pt.b[0]])
                    pfree(bk)
                return blocks, pt

            def phase_b(h, g, grp, blocks, pt):
                bN, bD = accs[(h, g)]
                nb = len(blocks)
                for bi, (L, tj, kb) in enumerate(blocks):
                    ks = tj % RING[g]
                    vap = vtm[g].t[:, ks, kb, h * 64:(h + 1) * 64]
                    p_ap = pt.t[:, bi * 128:(bi + 1) * 128]
                    mm(bN, banks[bN][0:64, grp * 128:(grp + 1) * 128], vap, p_ap, bi == 0, bi == nb - 1, [vtm[g].b[ks], pt.b[0]])
                for bi, (L, tj, kb) in enumerate(blocks):
                    if tj >= n_prev:
                        vd, vdb = ones_bf[:, 0:64], onesb
                    else:
                        vd, vdb = flag_bf[:], flagb
                    p_ap = pt.t[:, bi * 128:(bi + 1) * 128]
                    mm(bD, banks[bD][0:64, grp * 128:(grp + 1) * 128], vd, p_ap, bi == 0, bi == nb - 1, [vdb, pt.b[0]])
                if grp != 3:
                    return
                for (acc, bk) in ((accN, bN), (accD, bD)):
                    if g == 0:
                        S.op("dve", lambda e, acc=acc, bk=bk: e.tensor_copy(out=acc.t[:], in_=banks[bk][0:64, :]), reads=[bankb[bk]], writes=[acc.b[0]])
                    else:
                        dst = acc.t[:].rearrange("p (i r) -> p i r", r=4)
                        src = banks[bk][0:64, :].rearrange("p (r i) -> p i r", r=4)
                        S.op("dve", lambda e, dst=dst, src=src: e.tensor_tensor(out=dst, in0=dst, in1=src, op=ALU.add), reads=[bankb[bk], acc.b[0]], writes=[acc.b[0]])
                    pfree(bk)
                if g == 2:
                    S.op("act", lambda e: e.activation(out=accD.t[:], in_=accD.t[:], func=AF.Ln), reads=[accD.b[0]], writes=[accD.b[0]])
                    S.op("act", lambda e: e.activation(out=accD.t[:], in_=accD.t[:], func=AF.Exp, scale=-1.0), reads=[accD.b[0]], writes=[accD.b[0]])
                    S.op("dve", lambda e, h=h: e.tensor_tensor(out=gbuf.t[0:64, 18 + h, :], in0=accN.t[:], in1=accD.t[:], op=ALU.mult),
                         reads=[accN.b[0], accD.b[0]], writes=[gbuf.b[18 + h]])

            pend = []
            SKEW = 2
            for h in range(4):
                for g in range(3):
                    for grp in range(4):
                        pend.append(((h, g, grp), phase_a(h, g, grp)))
                        if len(pend) > SKEW:
                            t_, r_ = pend.pop(0)
                            phase_b(*t_, *r_)
            while pend:
                t_, r_ = pend.pop(0)
                phase_b(*t_, *r_)

        def cross_attention():
            if S.collect:
                return
            accs = {}

            def phase_a(h, mb):
                if mb == 0:
                    accs[h] = (palloc(), palloc())
                bk = palloc()
                mm(bk, banks[bk][:, :], mkT.t[:, h, mb * 128:(mb + 1) * 128], xqT.t[:, h, :], True, True, [mkT.b[0], xqT.b[h]])
                pt = PT[pti[0] % 3]; pti[0] += 1
                S.op("act", lambda e, bk=bk, pt=pt: e.activation(out=pt.t[:, 0:512], in_=banks[bk][:, :], func=AF.Exp), reads=[bankb[bk]], writes=[pt.b[0]])
                pfree(bk)
                return pt

            def phase_b(h, mb, pt):
                bN, bD = accs[h]
                mm(bN, banks[bN][:, :], mv.t[:, mb, h * 128:(h + 1) * 128], pt.t[:, 0:512], mb == 0, mb == 1, [mv.b[0], pt.b[0]])
                mm(bD, banks[bD][:, :], ones_bf[:], pt.t[:, 0:512], mb == 0, mb == 1, [onesb, pt.b[0]])
                if mb != 1:
                    return
                t1 = tmp()
                S.op("act", lambda e, bD=bD, t1=t1: e.activation(out=t1.t[:], in_=banks[bD][:, :], func=AF.Ln), reads=[bankb[bD]], writes=[t1.b[0]])
                pfree(bD)
                S.op("act", lambda e, t1=t1: e.activation(out=t1.t[:], in_=t1.t[:], func=AF.Exp, scale=-1.0), reads=[t1.b[0]], writes=[t1.b[0]])
                S.op("dve", lambda e, bN=bN, t1=t1, h=h: e.tensor_tensor(out=gbuf.t[:, 14 + h, :], in0=banks[bN][:, :], in1=t1.t[:], op=ALU.mult),
                     reads=[bankb[bN], t1.b[0]], writes=[gbuf.b[14 + h]])
                pfree(bN)

            prev = None
            for h in range(4):
                for mb in range(2):
                    cur = ((h, mb), phase_a(h, mb))
                    if prev is not None:
                        phase_b(*prev[0], prev[1])
                    prev = cur
            phase_b(*prev[0], prev[1])


        PI = math.pi
        cos2 = sb("cos2", [128, 16, 64], F32); sin2 = sb("sin2", [128, 16, 64], F32)
        Rtab = sb("Rtab", [128, 16, 64], F32); r2 = sb("r2", [128, 16], F32)
        tabb = Buf("ssmtab")
        Wd = sb("Wd", [128, 6, 8, 96], BF16); Wdb = Buf("Wd")
        XB = TileB(sb("XB", [128, 2, 16, 65], BF16))
        carry = TileB(sb("carry", [128, 2, 16], F32)); tcar = TileB(sb("tcar", [128, 2, 16], F32))
        PDOFF = 0; QOFF = 6 * 2048

        def dv(fn, reads, writes, eng="dve"):
            S.op(eng, fn, reads=reads, writes=writes)

        def setup_ssm(st2):
            if S.collect:
                return
            if True:
                N9 = 16 * 9
                dt_ = sb("dt", [128, 16], F32, st2); lr = sb("lr", [128, 16], F32, st2); li = sb("li", [128, 16], F32, st2)
                ang = sb("ang", [128, 16, 9], F32, st2); mag = sb("mag", [128, 16, 9], F32, st2)
                Are = sb("Are", [128, 16, 9], F32, st2); Aim = sb("Aim", [128, 16, 9], F32, st2)
                sA = sb("sA", [128, 16, 9], F32, st2); cA = sb("cA", [128, 16, 9], F32, st2)
                k1 = sb("k1", [128, 1024], F32, st2); k2 = sb("k2", [128, 1024], F32, st2); ki = sb("ki", [128, 1024], I32, st2)
                ang2 = sb("ang2", [128, 16, 64], F32, st2)
                sm = sb("sm", [128, 8, 16], F32, st2)
                BBr = sb("BBr", [128, 16, 32], F32, st2); BBi = sb("BBi", [128, 16, 32], F32, st2)
                BBbr = sb("BBbr", [128, 16, 32], BF16, st2); BBbi = sb("BBbi", [128, 16, 32], BF16, st2)
                CAb = sb("CAb", [128, 16, 2, 9, 32], BF16, st2)
                PdA = sb("PdA", [128, 6, 2048], BF16, st2); PdAb = Buf("PdA")
                B0 = Buf("ssmsetup")
                ssmp = sb("ssmp", [128, nssmp], F32, st2)
                S.dma("sp", ssmp[:], ssmp_d[:, :], getsem("ssmp"), writes=[B0])
                zz = sb("zz", [128, 1024], BF16, st2); zzb = Buf()
                S.op("pool", lambda e: e.memset(zz[:], 0.0), writes=[zzb])
                for i in range(SSMW // 1024):
                    S.dma("sp", sscr_d[:, i * 1024:(i + 1) * 1024], zz[:], getsem("ssmz"), reads=[zzb])
                sscrb.w = (("dma", "ssmz"), S.dma_cnt["ssmz"])

                def sp_(name):
                    o, n = pcols[name]
                    return ssmp[:, o:o + n]
                are = sp_("are"); aim = sp_("aim"); ldt = sp_("ldt")
                bre = sp_("ssm_b_re").rearrange("p (g c) -> p g c", g=16); bim = sp_("ssm_b_im").rearrange("p (g c) -> p g c", g=16)
                cre_ = sp_("ssm_c_re").rearrange("p (g c) -> p g c", g=16); cim_ = sp_("ssm_c_im").rearrange("p (g c) -> p g c", g=16)
                nidx = sp_("nidx")
                R0 = [B0, smallb]

                def D(fn, eng="dve"):
                    S.op(eng, fn, reads=R0, writes=[B0])

                def bc(ap2, n=32):
                    return ap2.unsqueeze(2).broadcast_to([128, 16, n])

                def sincos(x, n, s_out, c_out):
                    for shift, out in ((0.0, s_out), (PI / 2, c_out)):
                        D(lambda e: e.tensor_scalar(out=k1[:, 0:n], in0=x, scalar1=1.0 / (2 * PI), scalar2=shift / (2 * PI) + 0.5, op0=ALU.mult, op1=ALU.add))
                        D(lambda e: e.tensor_copy(out=ki[:, 0:n], in_=k1[:, 0:n]))
                        D(lambda e: e.tensor_copy(out=k1[:, 0:n], in_=ki[:, 0:n]))
                        D(lambda e: e.scalar_tensor_tensor(out=k2[:, 0:n], in0=k1[:, 0:n], scalar=-2 * PI, in1=x, op0=ALU.mult, op1=ALU.add))
                        if shift != 0.0:
                            D(lambda e: e.tensor_scalar(out=k2[:, 0:n], in0=k2[:, 0:n], scalar1=shift, scalar2=None, op0=ALU.add))
                        D(lambda e: e.tensor_scalar(out=k1[:, 0:n], in0=k2[:, 0:n], scalar1=PI, scalar2=-2 * PI, op0=ALU.is_gt, op1=ALU.mult))
                        D(lambda e: e.tensor_tensor(out=k2[:, 0:n], in0=k2[:, 0:n], in1=k1[:, 0:n], op=ALU.add))
                        D(lambda e: e.tensor_scalar(out=k1[:, 0:n], in0=k2[:, 0:n], scalar1=-PI, scalar2=2 * PI, op0=ALU.is_lt, op1=ALU.mult))
                        D(lambda e: e.tensor_tensor(out=k2[:, 0:n], in0=k2[:, 0:n], in1=k1[:, 0:n], op=ALU.add))
                        D(lambda e: e.tensor_scalar(out=k2[:, 0:n], in0=k2[:, 0:n], scalar1=3.1415925, scalar2=-3.1415925, op0=ALU.min, op1=ALU.max))
                        D(lambda e, out=out: e.activation(out=out, in_=k2[:, 0:n], func=AF.Sin), eng="act")

                D(lambda e: e.activation(out=dt_[:], in_=ldt, func=AF.Exp), eng="act")
                D(lambda e: e.tensor_tensor(out=lr[:], in0=are, in1=dt_[:], op=ALU.mult))
                D(lambda e: e.tensor_tensor(out=li[:], in0=aim, in1=dt_[:], op=ALU.mult))
                kidx = sp_("kidx")
                kb3 = kidx.unsqueeze(1).broadcast_to([128, 16, 9])
                D(lambda e: e.tensor_tensor(out=ang[:], in0=li[:].unsqueeze(2).broadcast_to([128, 16, 9]), in1=kb3, op=ALU.mult))
                D(lambda e: e.tensor_tensor(out=mag[:], in0=lr[:].unsqueeze(2).broadcast_to([128, 16, 9]), in1=kb3, op=ALU.mult))
                D(lambda e: e.activation(out=mag[:], in_=mag[:], func=AF.Exp), eng="act")
                sincos(ang[:].rearrange("p g k -> p (g k)"), N9, sA[:].rearrange("p g k -> p (g k)"), cA[:].rearrange("p g k -> p (g k)"))
                D(lambda e: e.tensor_tensor(out=Are[:], in0=mag[:], in1=cA[:], op=ALU.mult))
                D(lambda e: e.tensor_tensor(out=Aim[:], in0=mag[:], in1=sA[:], op=ALU.mult))
                D(lambda e: e.tensor_scalar(out=sm[:, 0, :], in0=li[:], scalar1=8.0, scalar2=None, op0=ALU.mult))
                D(lambda e: e.tensor_tensor(out=ang2[:], in0=nidx.unsqueeze(1).broadcast_to([128, 16, 64]), in1=sm[:, 0, :].unsqueeze(2).broadcast_to([128, 16, 64]), op=ALU.mult))
                sincos(ang2[:].rearrange("p g n -> p (g n)"), 1024, sin2[:].rearrange("p g n -> p (g n)"), cos2[:].rearrange("p g n -> p (g n)"))
                D(lambda e: e.activation(out=r2[:], in_=lr[:], func=AF.Exp, scale=8.0), eng="act")
                D(lambda e: e.tensor_copy(out=Rtab[:], in_=r2[:].unsqueeze(2).broadcast_to([128, 16, 64])))
                D(lambda e: e.memset(Rtab[:, :, 0:1], 0.0))
                S.op("dve", lambda e: e.memset(carry.t[:], 0.0), writes=[carry.b[0]])
                S.op("dve", lambda e: e.memset(XB.t[:], 0.0), writes=[XB.b[0]])
                nr = sm[:, 1, :]; ni = Aim[:, :, 1]; den = sm[:, 2, :]; t0 = sm[:, 3, :]; cr = sm[:, 4, :]; ci = sm[:, 5, :]; t1_ = sm[:, 6, :]
                D(lambda e: e.tensor_scalar(out=nr, in0=Are[:, :, 1], scalar1=-1.0, scalar2=None, op0=ALU.add))
                D(lambda e: e.tensor_tensor(out=den, in0=are, in1=are, op=ALU.mult))
                D(lambda e: e.tensor_tensor(out=t0, in0=aim, in1=aim, op=ALU.mult))
                D(lambda e: e.tensor_tensor(out=den, in0=den, in1=t0, op=ALU.add))
                D(lambda e: e.reciprocal(out=den, in_=den))
                D(lambda e: e.tensor_tensor(out=cr, in0=nr, in1=are, op=ALU.mult))
                D(lambda e: e.tensor_tensor(out=t0, in0=ni, in1=aim, op=ALU.mult))
                D(lambda e: e.tensor_tensor(out=cr, in0=cr, in1=t0, op=ALU.add))
                D(lambda e: e.tensor_tensor(out=cr, in0=cr, in1=den, op=ALU.mult))
                D(lambda e: e.tensor_tensor(out=ci, in0=ni, in1=are, op=ALU.mult))
                D(lambda e: e.tensor_tensor(out=t0, in0=nr, in1=aim, op=ALU.mult))
                D(lambda e: e.tensor_tensor(out=ci, in0=ci, in1=t0, op=ALU.subtract))
                D(lambda e: e.tensor_tensor(out=ci, in0=ci, in1=den, op=ALU.mult))

                u1 = sb("u1b", [128, 16, 9, 32], F32, st2); u2 = sb("u2b", [128, 16, 9, 32], F32, st2)
                PB = sb("PB", [128, 2, 16, 8, 32], BF16, st2)

                def cmul(outr, outi, ar, ai, br, bi, K, neg_i=False):
                    sh = [128, 16, K, 32]
                    A_r = ar.unsqueeze(3).broadcast_to(sh); A_i = ai.unsqueeze(3).broadcast_to(sh)
                    B_r = br.unsqueeze(2).broadcast_to(sh); B_i = bi.unsqueeze(2).broadcast_to(sh)
                    t1v = u1[:, :, 0:K, :]; t2v = u2[:, :, 0:K, :]
                    D(lambda e: e.tensor_tensor(out=t1v, in0=B_r, in1=A_r, op=ALU.mult))
                    D(lambda e: e.tensor_tensor(out=t2v, in0=B_i, in1=A_i, op=ALU.mult))
                    D(lambda e: e.tensor_tensor(out=outr, in0=t1v, in1=t2v, op=ALU.subtract))
                    D(lambda e: e.tensor_tensor(out=t1v, in0=B_i, in1=A_r, op=ALU.mult))
                    D(lambda e: e.tensor_tensor(out=t2v, in0=B_r, in1=A_i, op=ALU.mult))
                    if neg_i:
                        D(lambda e: e.scalar_tensor_tensor(out=outi, in0=t1v, scalar=-1.0, in1=t2v, op0=ALU.mult, op1=ALU.subtract))
                    else:
                        D(lambda e: e.tensor_tensor(out=outi, in0=t1v, in1=t2v, op=ALU.add))

                cmul(BBr[:].unsqueeze(2), BBi[:].unsqueeze(2), cr.unsqueeze(2), ci.unsqueeze(2), bre, bim, 1)
                D(lambda e: e.tensor_copy(out=BBbr[:], in_=BBr[:]))
                D(lambda e: e.tensor_copy(out=BBbi[:], in_=BBi[:]))
                cmul(CAb[:, :, 0, :, :], CAb[:, :, 1, :, :], Are[:], Aim[:], cre_, cim_, 9, neg_i=True)
                cmul(PB[:, 0, :, :, :], PB[:, 1, :, :, :], Are[:, :, 0:8], Aim[:, :, 0:8], BBr[:], BBi[:], 8)
                for c6 in range(6):
                    npair = 3 if c6 < 5 else 1
                    dst = sscr_d[:, QOFF + c6 * 1536: QOFF + c6 * 1536 + npair * 512].rearrange("p (j c t h) -> p j c t h", j=npair, c=2, t=8)
                    S.dma("act", dst, CAb[:, 3 * c6:3 * c6 + npair, :, 1:9, :], getsem("ssmq"), reads=[B0, sscrb])
                S.op("dve", lambda e: e.memset(Wd[:], 0.0), writes=[Wdb])
                for c6 in range(6):
                    npair = 3 if c6 < 5 else 1
                    bk = palloc()
                    for j in range(npair):
                        G = 3 * c6 + j
                        S.op("pe", lambda e, bk=bk, j=j, G=G: e.matmul(banks[bk][32 * j:32 * j + 32, 0:256], BBbr[:, G, :], CAb[:, G, 0, 0:8, :].rearrange("p k h -> p (k h)"), start=True, stop=False),
                             reads=[B0], writes=[bankb[bk]])
                        S.op("pe", lambda e, bk=bk, j=j, G=G: e.matmul(banks[bk][32 * j:32 * j + 32, 0:256], BBbi[:, G, :], CAb[:, G, 1, 0:8, :].rearrange("p k h -> p (k h)"), start=False, stop=True),
                             reads=[B0], writes=[bankb[bk]])
                    for j in range(npair):
                        S.op("dve", lambda e, bk=bk, c6=c6, j=j: e.tensor_copy(out=Wd[32 * j:32 * j + 32, c6, :, 32 * j:32 * j + 32],
                                                                             in_=banks[bk][32 * j:32 * j + 32, 0:256].rearrange("p (d h) -> p d h", d=8)),
                             reads=[bankb[bk]], writes=[Wdb])
                    pfree(bk)
                for s_ in range(8):
                    for c in (0, 1):
                        bA = palloc(); bB = palloc()
                        for G in range(16):
                            c6 = G // 3; j = G % 3
                            bk, col = (bA, c6 * 128) if c6 < 4 else (bB, (c6 - 4) * 128)
                            S.op("pe", lambda e, bk=bk, col=col, j=j, G=G, c=c, s_=s_: e.matmul(banks[bk][32 * j:32 * j + 32, col:col + 128], PB[:, c, G, 7 - s_, :], ident_bf[:], start=True, stop=True),
                                 reads=[B0, identbb], writes=[bankb[bk]])
                        o = (c * 8 + s_) * 128
                        S.op("dve", lambda e, bA=bA, o=o: e.tensor_copy(out=PdA[0:96, 0:4, o:o + 128], in_=banks[bA][0:96, :].rearrange("p (a c) -> p a c", a=4)), reads=[bankb[bA]], writes=[PdAb])
                        S.op("act", lambda e, bB=bB, o=o: e.copy(out=PdA[0:96, 4, o:o + 128], in_=banks[bB][0:96, 0:128]), reads=[bankb[bB]], writes=[PdAb])
                        S.op("act", lambda e, bB=bB, o=o: e.copy(out=PdA[0:32, 5, o:o + 128], in_=banks[bB][0:32, 128:256]), reads=[bankb[bB]], writes=[PdAb])
                        pfree(bA); pfree(bB)
                for c6 in range(6):
                    nr_ = 96 if c6 < 5 else 32
                    S.dma("act", sscr_d[0:nr_, PDOFF + c6 * 2048:PDOFF + (c6 + 1) * 2048], PdA[0:nr_, c6, :], getsem("ssmpa"), reads=[PdAb, sscrb])
                S.op("dve", lambda e: e.memset(tcar.t[:], 0.0), reads=[B0], writes=[tabb, tcar.b[0]])

        def ssm_tile(ti, own):
            zb = None
            for c6 in range(6):
                sl, slb = stream.get("s", PDOFF + c6 * 2048, 2048)
                if S.collect:
                    continue
                if zb is None:
                    zb = [palloc() for _ in range(4)]
                w = sl[:, 0:2048].rearrange("p (c s m) -> p c s m", c=2, s=8)
                npair = 3 if c6 < 5 else 1
                for j in range(npair):
                    G = 3 * c6 + j
                    for c in range(2):
                        bk = zb[c * 2 + G // 8]
                        ps = banks[bk][:, (G % 8) * 64:(G % 8 + 1) * 64]
                        for s_ in range(8):
                            mm(bk, ps, w[32 * j:32 * j + 32, c, s_, :], ud.t[32 * j:32 * j + 32, c6, s_ * 64:(s_ + 1) * 64], s_ == 0, s_ == 7, [slb, ud.b[c6]])
            if not S.collect:
                for c in range(2):
                    for hh in range(2):
                        bk = zb[c * 2 + hh]
                        evac_copy(zt.t[:, c, hh * 8:(hh + 1) * 8, :], banks[bk][:, :].rearrange("p (g n) -> p g n", g=8), [bankb[bk]], [zt.b[0]])
                        pfree(bk)
                zr = zt.t[:, 0, :, :]; zi = zt.t[:, 1, :, :]
                RT = [tabb]
                dv(lambda e: e.tensor_tensor(out=ta.t[:], in0=zr, in1=cos2[:], op=ALU.mult), [zt.b[0]] + RT, [ta.b[0]], "pool")
                dv(lambda e: e.tensor_tensor(out=tb_.t[:], in0=zi, in1=sin2[:], op=ALU.mult), [zt.b[0]] + RT, [tb_.b[0]], "pool")
                dv(lambda e: e.tensor_tensor(out=wt.t[:, 0, :, :], in0=ta.t[:], in1=tb_.t[:], op=ALU.add), [ta.b[0], tb_.b[0]], [wt.b[0]], "pool")
                dv(lambda e: e.tensor_tensor(out=ta.t[:], in0=zi, in1=cos2[:], op=ALU.mult), [zt.b[0]] + RT, [ta.b[0]], "pool")
                dv(lambda e: e.tensor_tensor(out=tb_.t[:], in0=zr, in1=sin2[:], op=ALU.mult), [zt.b[0]] + RT, [tb_.b[0]], "pool")
                dv(lambda e: e.tensor_tensor(out=wt.t[:, 1, :, :], in0=ta.t[:], in1=tb_.t[:], op=ALU.subtract), [ta.b[0], tb_.b[0]], [wt.b[0]], "pool")
                def part2():
                    dv(lambda e: e.tensor_tensor(out=tcar.t[:], in0=carry.t[:], in1=r2[:].unsqueeze(1).broadcast_to([128, 2, 16]), op=ALU.mult), [carry.b[0]] + RT, [tcar.b[0]])
                    dv(lambda e: e.tensor_tensor(out=wt.t[:, :, :, 0], in0=wt.t[:, :, :, 0], in1=tcar.t[:], op=ALU.add), [tcar.b[0], wt.b[0]], [wt.b[0]])
                    for c in range(2):
                        wf = wt.t[:, c, :, :].rearrange("p g n -> p (g n)")
                        dv(lambda e, wf=wf: e.tensor_tensor_scan(out=wf, data0=Rtab[:].rearrange("p g n -> p (g n)"), data1=wf, initial=0.0, op0=ALU.mult, op1=ALU.add),
                           [wt.b[0]] + RT, [wt.b[0]])
                    wr = wt.t[:, 0, :, :]; wi = wt.t[:, 1, :, :]
                    dv(lambda e: e.tensor_tensor(out=ta.t[:], in0=wr, in1=cos2[:], op=ALU.mult), [wt.b[0]] + RT, [ta.b[0]], "pool")
                    dv(lambda e: e.tensor_tensor(out=tb_.t[:], in0=wi, in1=sin2[:], op=ALU.mult), [wt.b[0]] + RT, [tb_.b[0]], "pool")
                    dv(lambda e: e.tensor_tensor(out=zt.t[:, 0, :, :], in0=ta.t[:], in1=tb_.t[:], op=ALU.subtract), [ta.b[0], tb_.b[0]], [zt.b[0]], "pool")
                    dv(lambda e: e.tensor_tensor(out=ta.t[:], in0=wi, in1=cos2[:], op=ALU.mult), [wt.b[0]] + RT, [ta.b[0]], "pool")
                    dv(lambda e: e.tensor_tensor(out=tb_.t[:], in0=wr, in1=sin2[:], op=ALU.mult), [wt.b[0]] + RT, [tb_.b[0]], "pool")
                    dv(lambda e: e.tensor_tensor(out=zt.t[:, 1, :, :], in0=ta.t[:], in1=tb_.t[:], op=ALU.add), [ta.b[0], tb_.b[0]], [zt.b[0]], "pool")
                    S.op("pool", lambda e: e.tensor_copy(out=XB.t[:, :, :, 0], in_=carry.t[:]), reads=[carry.b[0]], writes=[XB.b[0]])
                    S.op("pool", lambda e: e.tensor_copy(out=XB.t[:, :, :, 1:65], in_=zt.t[:]), reads=[zt.b[0]], writes=[XB.b[0]])
                    S.op("pool", lambda e: e.tensor_copy(out=carry.t[:], in_=zt.t[:, :, :, 63]), reads=[zt.b[0]], writes=[carry.b[0]])
                return part2
            return None

        def ssm_out():
            dvec = sc("dvec")
            for c6 in range(6):
                npair = 3 if c6 < 5 else 1
                sl, slb = stream.get("s", QOFF + c6 * 1536, npair * 512)
                if S.collect:
                    continue
                q = sl[:, 0:npair * 512].rearrange("p (j c t h) -> p j c t h", j=npair, c=2, t=8)
                bk = palloc()
                nrw = 32 * npair
                for dl in range(8):
                    mm(bk, banks[bk][0:nrw, dl * 64:512], Wd[0:nrw, c6, dl, 0:nrw], ud.t[0:nrw, c6, 0:(8 - dl) * 64], dl == 0, False, [Wdb, ud.b[c6]])
                for j in range(npair):
                    G = 3 * c6 + j
                    for t in range(8):
                        ps = banks[bk][32 * j:32 * j + 32, t * 64:(t + 1) * 64]
                        for c in range(2):
                            mm(bk, ps, q[:, j, c, t, :], XB.t[:, c, G, 0:64], False, (t == 7 and c == 1 and j == npair - 1), [slb, XB.b[0]])
                nr_ = 32 * npair
                t1 = tmp()
                S.op("dve", lambda e, bk=bk, t1=t1, c6=c6, nr_=nr_: e.scalar_tensor_tensor(out=t1.t[0:nr_, :], in0=ud.t[0:nr_, c6, :], scalar=dvec[0:nr_, c6:c6 + 1], in1=banks[bk][0:nr_, :],
                                                                                           op0=ALU.mult, op1=ALU.add),
                     reads=[bankb[bk], ud.b[c6], smallb], writes=[t1.b[0]])
                pfree(bk)
                S.op("act", lambda e, t1=t1, c6=c6, nr_=nr_: e.activation(out=gbuf.t[0:nr_, 8 + c6, :].rearrange("p (n t) -> p t n", t=8), in_=t1.t[0:nr_, :].rearrange("p (t n) -> p t n", t=8),
                                                                         func=AF.Gelu_apprx_tanh),
                     reads=[t1.b[0]], writes=[gbuf.b[8 + c6]])


        def gate(b, c):
            res = []
            def ev(bk):
                t1 = tmp()
                S.op("act", lambda e: e.activation(out=t1.t[:], in_=banks[bk][:, :], func=AF.Sigmoid), reads=[bankb[bk]], writes=[t1.b[0]])
                res.append(t1)
            proj("pg%d_%d" % (b, c), ev)
            return res[0] if res else None

        def merge_and_out():
            for c in range(8):
                first = [True]
                def accumulate(prod_fn, reads):
                    pass
                nbr = 0
                if "ssm" in BR:
                    sg = gate(0, c)
                    sl, slb = wget("glu%d" % c)
                    if not S.collect:
                        w = sl[:, 0:1536].rearrange("p (a k c) -> p a k c", a=2, k=6)
                        ba = palloc(); bb = palloc()
                        for ab, bk in ((0, ba), (1, bb)):
                            for k6 in range(6):
                                nr = 96 if k6 < 5 else 32
                                mm(bk, banks[bk][:, :], w[0:nr, ab, k6, :], gbuf.t[0:nr, 8 + k6, :], k6 == 0, k6 == 5, [slb, gbuf.b[8 + k6]])
                        t1 = tmp()
                        S.op("act", lambda e, bb=bb, t1=t1: e.activation(out=t1.t[:], in_=banks[bb][:, :], func=AF.Sigmoid), reads=[bankb[bb]], writes=[t1.b[0]])
                        pfree(bb)
                        S.op("pool", lambda e, t1=t1, sg=sg: e.tensor_tensor(out=t1.t[:], in0=t1.t[:], in1=sg.t[:], op=ALU.mult), reads=[t1.b[0], sg.b[0]], writes=[t1.b[0]])
                        S.op("dve", lambda e, ba=ba, t1=t1: e.tensor_tensor(out=macc.t[:], in0=banks[ba][:, :], in1=t1.t[:], op=ALU.mult), reads=[bankb[ba], t1.b[0]], writes=[macc.b[0]])
                        pfree(ba)
                    nbr += 1
                for (bname, bi, wn, kk, src0) in (("attn", 1, "aup%d" % c, 64, 18), ("mem", 2, "xup%d" % c, 128, 14)):
                    if bname not in BR:
                        continue
                    sg = gate(bi, c)
                    sl, slb = wget(wn)
                    if not S.collect:
                        w = sl[:, 0:512].rearrange("p (k c) -> p k c", k=4)
                        bk = palloc()
                        for k4 in range(4):
                            mm(bk, banks[bk][:, :], w[0:kk, k4, :], gbuf.t[0:kk, src0 + k4, :], k4 == 0, k4 == 3, [slb, gbuf.b[src0 + k4]])
                        if nbr == 0:
                            S.op("dve", lambda e, bk=bk, sg=sg: e.tensor_tensor(out=macc.t[:], in0=banks[bk][:, :], in1=sg.t[:], op=ALU.mult), reads=[bankb[bk], sg.b[0]], writes=[macc.b[0]])
                        else:
                            S.op("dve", lambda e, bk=bk, sg=sg: e.tensor_tensor(out=sg.t[:], in0=banks[bk][:, :], in1=sg.t[:], op=ALU.mult), reads=[bankb[bk], sg.b[0]], writes=[sg.b[0]])
                            S.op("pool", lambda e, sg=sg: e.tensor_tensor(out=macc.t[:], in0=macc.t[:], in1=sg.t[:], op=ALU.add), reads=[macc.b[0], sg.b[0]], writes=[macc.b[0]])
                        pfree(bk)
                    nbr += 1
                if not S.collect:
                    S.op("act", lambda e, c=c: e.copy(out=gbuf.t[:, c, :], in_=macc.t[:]), reads=[macc.b[0]], writes=[gbuf.b[c]])
            for c2 in range(8):
                sl, slb = wget("wo%d" % c2)
                if S.collect:
                    continue
                bk = palloc()
                for k in range(8):
                    mm(bk, banks[bk][:, :], sl[:, k * 128:(k + 1) * 128], gbuf.t[:, k, :], k == 0, k == 7, [slb, gbuf.b[k]])
                S.op("dve", lambda e, bk=bk, c2=c2: e.tensor_tensor(out=xres.t[:, c2, :], in0=banks[bk][:, :], in1=xres.t[:, c2, :], op=ALU.add),
                     reads=[bankb[bk], xres.b[c2]], writes=[xres.b[c2]])
                pfree(bk)


        def final_out(to):
            if S.collect:
                return
            g = gain["final_norm"]
            for c in range(8):
                if c % 2 == 0:
                    S.op("act", lambda e, c=c: e.activation(out=xn.t[:, c, :], in_=xres.t[:, c, :], func=AF.Square), reads=[xres.b[c]], writes=[xn.b[c]])
                else:
                    S.op("dve", lambda e, c=c: e.tensor_tensor(out=xn.t[:, c, :], in0=xres.t[:, c, :], in1=xres.t[:, c, :], op=ALU.mult), reads=[xres.b[c]], writes=[xn.b[c]])
            bk = palloc()
            for c in range(8):
                mm(bk, banks[bk][:, :], ones_bf[:], xn.t[:, c, :], c == 0, c == 7, [onesb, xn.b[c]])
            t1 = tmp(); t2 = tmp()
            S.op("act", lambda e: e.activation(out=t1.t[:], in_=banks[bk][:, :], func=AF.Ln, scale=1.0 / D, bias=epsb[:]), reads=[bankb[bk], epsbb], writes=[t1.b[0]])
            pfree(bk)
            S.op("act", lambda e: e.activation(out=t2.t[:], in_=t1.t[:], func=AF.Exp, scale=-0.5), reads=[t1.b[0]], writes=[t2.b[0]])
            for c in range(8):
                S.op("dve", lambda e, c=c: e.scalar_tensor_tensor(out=xres.t[:, c, :], in0=xres.t[:, c, :], scalar=g[:, c:c + 1], in1=t2.t[:], op0=ALU.mult, op1=ALU.mult),
                     reads=[xres.b[c], t2.b[0], smallb], writes=[xres.b[c]])
            store_y(to, xres)

        def body():
            with ExitStack() as st2:
                if S.collect:
                    if "mem" in BR and stage != "ffn1":
                        setup_mem(st2)
                else:
                    if "ssm" in BR and stage != "ffn1":
                        setup_ssm(st2)
                    if "mem" in BR and stage != "ffn1":
                        setup_mem(st2)
                if not S.collect:
                    S.barrier()
            if not S.collect:
                alloc_main()
            early = False
            for ti in range(n_prev + n_own):
                own = ti >= n_prev
                if not S.collect and not early:
                    load_x(ti)
                    rmsnorm("ffn1_norm")
                early = False
                ffn(1)
                if stage == "ffn1":
                    if own and not S.collect:
                        store_y(ti - n_prev, xres)
                    continue
                if not S.collect:
                    rmsnorm("mix_norm")
                projections(ti, own)
                if (not own) and ti + 1 < n_prev + n_own and "ssm" in BR:
                    if not S.collect:
                        load_x(ti + 1)
                        rmsnorm("ffn1_norm")
                    early = True
                ssm_p2 = None
                if "ssm" in BR:
                    ssm_p2 = ssm_tile(ti, own)
                    if ssm_p2 is not None and not (own and "attn" in BR):
                        ssm_p2(); ssm_p2 = None
                if not S.collect and ti <= 5:
                    rem = ncb - ncb_first
                    conv_issue(ncb_first + ((ti + 1) * rem + 5) // 6)
                if not own:
                    continue
                if "attn" in BR:
                    attention(ti, ssm_p2)
                if "mem" in BR:
                    cross_attention()
                if "ssm" in BR:
                    ssm_out()
                merge_and_out()
                if not S.collect:
                    rmsnorm("ffn2_norm")
                ffn(2)
                if not S.collect:
                    issue_x(ti + 1)
                final_out(ti - n_prev)

        S.collect = True
        body()
        S.collect = False
        body()
        S.barrier(final=True)
        S.emit(nc, esem, dsem)
    return nc


_CACHE = {}


def kernel(**inp):
    inp = {k: np.asarray(v) for k, v in inp.items()}
    n_prev = CFG["n_prev"]; n_own = CFG["n_own"]
    W = build_wall(inp)
    wall = W.build(8192)
    small, scols, ssmp, pcols = build_small(inp)
    key = (n_prev, n_own, CFG["stage"], tuple(CFG.get("branches", ())))
    if key not in _CACHE:
        _CACHE[key] = build_program(W.off, W.total, scols, small.shape[1], pcols, ssmp.shape[1], W.first_end)
    nc = _CACHE[key]
    x = inp["x"]; mem = inp["mem"]
    in_maps = []
    for core in range(8):
        b = core // 2; half = core % 2
        own = x[b, half * HALF: half * HALF + n_own * T]
        if half == 1:
            prev = x[b, HALF - n_prev * T: HALF]
            flag = np.ones((128, 64), np.float32)
        else:
            prev = np.zeros((n_prev * T, D), np.float32)
            flag = np.zeros((128, 64), np.float32)
        xin = np.ascontiguousarray(np.concatenate([prev, own], axis=0))
        in_maps.append({"xin": xin, "wall": wall, "small": small, "ssmp": ssmp, "flag": flag,
                        "mem": np.ascontiguousarray(mem[b])})
    res = run_bass_kernel_spmd(nc, in_maps, core_ids=list(range(8)))
    out = np.zeros((4, SEQ, D), np.float32)
    for core in range(8):
        b = core // 2; half = core % 2
        out[b, half * HALF: half * HALF + n_own * T] = res.results[core]["y"]
    return out
```

```python
import numpy as np
import math
from contextlib import ExitStack
import concourse.bass as bass
import concourse.mybir as mybir
from concourse.bass_utils import run_bass_kernel_spmd

F32 = mybir.dt.float32
BF16 = mybir.dt.bfloat16
I32 = mybir.dt.int32
AF = mybir.ActivationFunctionType
ALU = mybir.AluOpType
AX = mybir.AxisListType

D = 1024; DFF = 2816; NJ = 22; T = 512; NCH = 8
SEQ = 8192; HALF = 4096
EPS = 1e-6
NEG = -30000.0
RING = (2, 2, 5)
LMAX = (1, 1, 4)
STRIPW = (256, 256, 640)
STRIP_OFF = (0, 1024, 2048)
STRIP_TOT = 4 * (256 + 256 + 640)

CFG = dict(n_prev=8, n_own=8, stage="full")


class Buf:
    __slots__ = ("name", "w", "r")

    def __init__(self, name=""):
        self.name = name
        self.w = None
        self.r = {}


ENGS = ("pe", "act", "dve", "pool", "sp")


class Sched:
    def __init__(self):
        self.ops = {e: [] for e in ENGS}
        self.known = {e: {} for e in ENGS}
        self.dma_cnt = {}
        self.emitted = {e: 0 for e in ENGS}
        self.sigbase = {e: 0 for e in ENGS}
        self.snap = {e: {} for e in ENGS}
        self.collect = False

    def _waits(self, eng, idx, reads, writes):
        deps = {}
        for b in reads:
            if b.w is not None:
                k, v = b.w
                if deps.get(k, -1) < v:
                    deps[k] = v
        for b in writes:
            if b.w is not None:
                k, v = b.w
                if deps.get(k, -1) < v:
                    deps[k] = v
            for k, v in b.r.items():
                if deps.get(k, -1) < v:
                    deps[k] = v
        waits = []
        kn = self.known[eng]
        for k, v in deps.items():
            if k == eng and eng == "pe":
                continue
            if kn.get(k, -1) >= v:
                continue
            kn[k] = v
            waits.append((k, v))
            if not isinstance(k, tuple):
                self.ops[k][v][2] = True
                for k3, v3 in self.ops[k][v][3].items():
                    if kn.get(k3, -1) < v3:
                        kn[k3] = v3
        if waits:
            self.snap[eng] = dict(kn)
        return waits

    def op(self, eng, fn, reads=(), writes=()):
        if self.collect:
            return
        idx = len(self.ops[eng])
        waits = self._waits(eng, idx, reads, writes)
        self.ops[eng].append([fn, waits, False, self.snap[eng]])
        for b in reads:
            if b.r.get(eng, -1) < idx:
                b.r[eng] = idx
        for b in writes:
            b.w = (eng, idx)
            b.r = {}

    def dma(self, q, out_ap, in_ap, sem, reads=(), writes=(), **kw):
        if self.collect:
            return
        val = self.dma_cnt.get(sem, 0) + 16
        self.dma_cnt[sem] = val
        key = ("dma", sem)
        idx = len(self.ops[q])
        waits = self._waits(q, idx, reads, writes)
        self.ops[q].append([("dma", out_ap, in_ap, sem, kw), waits, False, self.snap[q]])
        for b in reads:
            b.r[key] = val
        for b in writes:
            b.w = (key, val)
            b.r = {}

    def barrier(self, final=False):
        if self.collect:
            return
        last = {e: len(self.ops[e]) - 1 for e in ENGS if len(self.ops[e]) > 0}
        for e in ENGS:
            waits = []
            for e2, i2 in last.items():
                if self.known[e].get(e2, -1) >= i2:
                    continue
                self.known[e][e2] = i2
                self.ops[e2][i2][2] = True
                waits.append((e2, i2))
            for sem, val in self.dma_cnt.items():
                k = ("dma", sem)
                if sem.startswith("cv") and not final:
                    continue
                if self.known[e].get(k, -1) >= val:
                    continue
                self.known[e][k] = val
                waits.append((k, val))
            if waits:
                self.snap[e] = dict(self.known[e])
            self.ops[e].append([None, waits, False, self.snap[e]])

    def emit(self, nc, esem, dsem):
        sigcount = {}
        for e in ENGS:
            c = 0
            lst = []
            for o in self.ops[e]:
                if o[2]:
                    c += 1
                lst.append(c)
            sigcount[e] = lst
        ops = self.ops
        emitted = self.emitted

        def run(e_name, eng):
            lst = ops[e_name]
            for i in range(emitted[e_name], len(lst)):
                fn, waits, sig = lst[i][0], lst[i][1], lst[i][2]
                for k, v in waits:
                    if isinstance(k, tuple):
                        eng.wait_ge(dsem[k[1]], v)
                    else:
                        eng.wait_ge(esem[k], sigcount[k][v])
                if fn is None:
                    if sig:
                        eng.nop().then_inc(esem[e_name], 1)
                    continue
                if isinstance(fn, tuple):
                    _, o_ap, i_ap, sem, kw = fn
                    ins = eng.dma_start(out=o_ap, in_=i_ap, **kw).then_inc(dsem[sem], 16)
                    if sig:
                        eng.nop().then_inc(esem[e_name], 1)
                else:
                    ins = fn(eng)
                    if sig:
                        ins.then_inc(esem[e_name], 1)
            emitted[e_name] = len(lst)

        with nc.Block() as block:
            @block.tensor
            def _(e):
                run("pe", e)

            @block.scalar
            def _(e):
                run("act", e)

            @block.vector
            def _(e):
                run("dve", e)

            @block.gpsimd
            def _(e):
                run("pool", e)

            @block.sync
            def _(e):
                run("sp", e)


def t5_causal_buckets(dist):
    dist = np.asarray(dist, np.int32)
    max_exact = 16
    safe = np.maximum(dist, 1).astype(np.float32)
    large = max_exact + (np.log(safe / max_exact) / np.log(2048 / max_exact) * (32 - max_exact)).astype(np.int32)
    large = np.minimum(large, 31)
    return np.where(dist < max_exact, dist, large).astype(np.int32)


class Wall:
    def __init__(self):
        self.parts = []
        self.off = {}
        self.total = 0

    def add(self, name, arr):
        arr = np.ascontiguousarray(arr, dtype=np.float32).reshape(128, -1)
        self.off[name] = (self.total, arr.shape[1])
        self.parts.append(arr)
        self.total += arr.shape[1]

    def build(self, pad_to):
        tot = ((self.total + pad_to - 1) // pad_to) * pad_to
        if tot > self.total:
            self.parts.append(np.zeros((128, tot - self.total), np.float32))
        self.total = tot
        return np.concatenate(self.parts, axis=1)


def kchunks(w):
    k = w.shape[0] // 128
    return w.reshape(k, 128, w.shape[1]).transpose(1, 0, 2)


def ssm_feat_index():
    idx = -np.ones((6, 128), np.int64)
    for c6 in range(6):
        for p in range(96):
            G = 3 * c6 + p // 32
            if G < 16:
                idx[c6, p] = G * 32 + p % 32
    return idx


def build_wall(inp, names_only=False):
    W = Wall()
    uidx = ssm_feat_index()
    z = lambda *s: np.zeros(s, np.float32)
    rel = inp["rel_table"]
    strips = np.full((128, STRIP_TOT), NEG, np.float32)
    kk = np.arange(128)[:, None]
    for g, (dil, cs) in enumerate(((1, 1), (4, 4), (16, 4))):
        Wd = STRIPW[g]
        X = np.arange(Wd)[None, :]
        dl = X - kk
        dist = dl * cs
        valid = (dl >= 0) & (dist % dil == 0) & (dist // dil <= 128)
        bk = t5_causal_buckets(np.maximum(dist, 0))
        for h in range(4):
            vals = rel[bk, 4 * g + h]
            o = 4 * STRIP_OFF[g] // 4 * 0 + sum(4 * STRIPW[i] for i in range(g)) + h * Wd
            strips[:, o:o + Wd] = np.where(valid, vals, NEG)
    W.add("strips", strips)
    kv = kchunks(inp["xattn_w_kv"][0])
    for h in range(4):
        W.add("xk%d" % h, kv[:, :, h * 128:(h + 1) * 128])
    W.add("xv0", kv[:, 0:4, 512:1024])
    W.add("xv1", kv[:, 4:8, 512:1024])
    FF = (1,)
    ffn_w = {1: (inp["ffn1_w_in"], inp["ffn1_w_down"]), 2: (inp["ffn2_w_in"], inp["ffn2_w_down"])}
    for f in FF:
        w_in = ffn_w[f][0][0]
        w_dn = ffn_w[f][1][0]
        kin = kchunks(w_in)
        for j in range(NJ):
            a = kin[:, :, j * 128:(j + 1) * 128]
            b = kin[:, :, DFF + j * 128:DFF + (j + 1) * 128]
            W.add("ffn%d_in%d" % (f, j), np.stack([a, b], axis=1))
        kdn = kchunks(w_dn)
        for m in range(8):
            W.add("ffn%d_dn%d_0" % (f, m), kdn[:, 0:11, m * 128:(m + 1) * 128])
            W.add("ffn%d_dn%d_1" % (f, m), kdn[:, 11:22, m * 128:(m + 1) * 128])
    w_in = inp["w_in"][0]
    kw = kchunks(w_in)
    for c6 in range(6):
        a = z(128, 8, 128)
        for col in range(128):
            fi = uidx[c6, col]
            if fi >= 0:
                a[:, :, col] = kw[:, :, fi]
        W.add("pu%d" % c6, a)
    for c in range(6):
        W.add("pk%d" % c, kw[:, :, 1280 + c * 128:1280 + (c + 1) * 128])
    for g in range(3):
        W.add("pv%d" % g, kw[:, :, 2048 + g * 256:2048 + (g + 1) * 256])
    W.first_end = W.total
    for c in range(6):
        W.add("pq%d" % c, kw[:, :, 512 + c * 128:512 + (c + 1) * 128])
    for c in range(4):
        W.add("pxq%d" % c, kw[:, :, 2816 + c * 128:2816 + (c + 1) * 128])
    for b in range(3):
        for c in range(8):
            o = 3328 + b * 1024 + c * 128
            W.add("pg%d_%d" % (b, c), kw[:, :, o:o + 128])
    glu = inp["ssm_w_glu"][0]
    gk = z(128, 6, 2048)
    for c6 in range(6):
        for p in range(128):
            fi = uidx[c6, p]
            if fi >= 0:
                gk[p, c6, :] = glu[fi, :]
    for c in range(8):
        a = gk[:, :, c * 128:(c + 1) * 128]
        b = gk[:, :, 1024 + c * 128:1024 + (c + 1) * 128]
        W.add("glu%d" % c, np.stack([a, b], axis=1))
    aup = inp["attn_w_up"][0]
    ak = z(128, 4, 1024)
    ak[:64] = aup.reshape(4, 64, 1024).transpose(1, 0, 2)
    for c in range(8):
        W.add("aup%d" % c, ak[:, :, c * 128:(c + 1) * 128])
    xk = kchunks(inp["xattn_w_up"][0])
    for c in range(8):
        W.add("xup%d" % c, xk[:, :, c * 128:(c + 1) * 128])
    wo = kchunks(inp["w_out"][0])
    for c in range(8):
        W.add("wo%d" % c, wo[:, :, c * 128:(c + 1) * 128])
    FF = (2,)
    for f in FF:
        w_in = ffn_w[f][0][0]
        w_dn = ffn_w[f][1][0]
        kin = kchunks(w_in)
        for j in range(NJ):
            a = kin[:, :, j * 128:(j + 1) * 128]
            b = kin[:, :, DFF + j * 128:DFF + (j + 1) * 128]
            W.add("ffn%d_in%d" % (f, j), np.stack([a, b], axis=1))
        kdn = kchunks(w_dn)
        for m in range(8):
            W.add("ffn%d_dn%d_0" % (f, m), kdn[:, 0:11, m * 128:(m + 1) * 128])
            W.add("ffn%d_dn%d_1" % (f, m), kdn[:, 11:22, m * 128:(m + 1) * 128])
    return W


def strip_off(g, h):
    return sum(4 * STRIPW[i] for i in range(g)) + h * STRIPW[g]


def pairs_layout(a):
    s = a.shape
    a = a.reshape((16, 2, 64) + s[2:])
    perm = (1, 2, 0) + tuple(range(3, a.ndim))
    return np.ascontiguousarray(a.transpose(perm)).reshape((128, 16) + s[2:])


def build_small(inp):
    cols = {}
    parts = []
    tot = [0]

    def add(name, arr):
        arr = np.ascontiguousarray(arr, np.float32).reshape(128, -1)
        cols[name] = (tot[0], arr.shape[1])
        parts.append(arr)
        tot[0] += arr.shape[1]

    for nm in ("ffn1_norm", "mix_norm", "ffn2_norm", "mem_norm"):
        add(nm, inp[nm][0].reshape(8, 128).T)
    add("final_norm", inp["final_norm"].reshape(8, 128).T)
    add("ident", np.eye(128, dtype=np.float32))
    add("nidx", np.tile(np.arange(1, 65, dtype=np.float32)[None, :], (128, 1)))
    add("kidx", np.tile(np.arange(0, 9, dtype=np.float32)[None, :], (128, 1)))
    are = inp["ssm_a_re"][0]; aim = inp["ssm_a_im"][0]
    add("are", pairs_layout(are))
    add("aim", pairs_layout(aim))
    ldt = np.repeat(inp["ssm_log_dt"][0][:, None], 64, axis=1)
    add("ldt", pairs_layout(ldt))
    for nm in ("ssm_b_re", "ssm_b_im"):
        b = pairs_layout(inp[nm][0])
        bp = np.zeros((128, 16, 32), np.float32)
        bp[:64, :, 0:16] = b[:64]
        bp[64:, :, 16:32] = b[64:]
        add(nm, bp)
    for nm in ("ssm_c_re", "ssm_c_im"):
        c = inp[nm][0].transpose(0, 2, 1)
        c = pairs_layout(c)
        cp = np.zeros((128, 16, 32), np.float32)
        cp[:64, :, 0:16] = c[:64]
        cp[64:, :, 16:32] = c[64:]
        add(nm, cp)
    dsk = inp["ssm_d"][0].reshape(512)
    uidx = ssm_feat_index()
    dv = np.zeros((128, 6), np.float32)
    for c6 in range(6):
        for p in range(128):
            if uidx[c6, p] >= 0:
                dv[p, c6] = dsk[uidx[c6, p]]
    add("dvec", dv)
    P_NAMES = ("ffn1_norm", "mix_norm", "ffn2_norm", "mem_norm", "final_norm", "ident", "dvec")
    pa, pc, sa, sc_ = [], {}, [], {}
    po = so = 0
    for (name, (o, n)), arr in zip(cols.items(), parts):
        if name in P_NAMES:
            pa.append(arr); pc[name] = (po, n); po += n
        else:
            sa.append(arr); sc_[name] = (so, n); so += n
    return np.concatenate(pa, axis=1), pc, np.concatenate(sa, axis=1), sc_


class TileB:
    def __init__(self, t, n=1):
        self.t = t
        self.b = [Buf() for _ in range(n)]


def build_program(woff, wtotal, scols, nsmall, pcols, nssmp, first_end):
    n_prev = CFG["n_prev"]; n_own = CFG["n_own"]; stage = CFG["stage"]
    NTOK = (n_prev + n_own) * T
    nc = bass.Bass("TRN2", target_bir_lowering=False)
    xin_d = nc.dram_tensor("xin", [NTOK, D], F32, kind="ExternalInput").ap()
    wall_d = nc.dram_tensor("wall", [128, wtotal], F32, kind="ExternalInput").ap()
    small_d = nc.dram_tensor("small", [128, nsmall], F32, kind="ExternalInput").ap()
    ssmp_d = nc.dram_tensor("ssmp", [128, nssmp], F32, kind="ExternalInput").ap()
    flag_d = nc.dram_tensor("flag", [128, 64], F32, kind="ExternalInput").ap()
    mem_d = nc.dram_tensor("mem", [256, D], F32, kind="ExternalInput").ap()
    y_d = nc.dram_tensor("y", [n_own * T, D], F32, kind="ExternalOutput").ap()
    scr_d = nc.dram_tensor("scr", [128, wtotal], BF16).ap()
    SSMW = 6 * 2048 + 4 * 2048
    sscr_d = nc.dram_tensor("sscr", [128, SSMW], BF16).ap()

    S = Sched()
    CB = 8192
    ncb = wtotal // CB
    with ExitStack() as st:
        esem = {e: st.enter_context(nc.semaphore("e_" + e)) for e in ENGS}
        dsem = {}

        def getsem(name):
            if name not in dsem:
                dsem[name] = st.enter_context(nc.semaphore("d_" + name))
            return name

        uniq = [0]

        def sb(name, shape, dt, stack=st):
            uniq[0] += 1
            return stack.enter_context(nc.sbuf_tensor("s%d_%s" % (uniq[0], name), shape, dt))

        cvb = [Buf("cv%d" % i) for i in range(ncb)]
        cv_next = [0]

        def conv_issue(upto):
            upto = min(upto, ncb)
            while cv_next[0] < upto:
                i = cv_next[0]
                S.dma("pool", scr_d[:, i * CB:(i + 1) * CB], wall_d[:, i * CB:(i + 1) * CB],
                      getsem("cv%d" % i), writes=[cvb[i]], max_dma_last_dim=8192)
                cv_next[0] += 1
        ncb_first = (first_end + CB - 1) // CB
        conv_issue(ncb_first)

        small = sb("small", [128, nsmall], F32)
        smallb = Buf("small")
        S.dma("sp", small[:], small_d[:, :], getsem("small"), writes=[smallb])

        def sc(name, a=None, b=None):
            o, n = scols[name]
            if a is None:
                return small[:, o:o + n]
            return small[:, o + a:o + b]

        ident = sc("ident")
        ones_bf = sb("ones_bf", [128, 128], BF16)
        onesb = Buf("ones")
        S.op("dve", lambda e: e.memset(ones_bf[:], 1.0), writes=[onesb])
        ident_bf = sb("ident_bf", [128, 128], BF16)
        identbb = Buf("identbf")
        S.op("dve", lambda e: e.tensor_copy(out=ident_bf[:], in_=ident), reads=[smallb], writes=[identbb])
        flag_f = sb("flag_f", [128, 64], F32)
        flag_bf = sb("flag_bf", [128, 64], BF16)
        flagb = Buf("flag")
        S.dma("sp", flag_f[:], flag_d[:, :], getsem("flag"), writes=[flagb])
        S.op("dve", lambda e: e.tensor_copy(out=flag_bf[:], in_=flag_f[:]), reads=[flagb], writes=[flagb])
        epsb = sb("epsb", [128, 1], F32)
        epsbb = Buf("eps")
        S.op("dve", lambda e: e.memset(epsb[:], EPS), writes=[epsbb])

        strips = sb("strips", [128, STRIP_TOT], BF16)
        stripb = Buf("strips")
        so_, sn_ = woff["strips"]
        S.dma("sp", strips[:], scr_d[:, so_:so_ + sn_], getsem("strips"),
              reads=[cvb[i] for i in range(so_ // CB, (so_ + sn_ - 1) // CB + 1)], writes=[stripb])

        NBANK = 8
        banks = [st.enter_context(nc.psum_tensor("bank%d" % i, [128, 512], F32)) for i in range(NBANK)]
        bankb = [Buf("bank%d" % i) for i in range(NBANK)]
        bank_free = list(range(NBANK))

        def palloc():
            assert bank_free, "psum pool exhausted"
            return bank_free.pop(0)

        def pfree(i):
            bank_free.append(i)

        NSLOT = 4
        SLOTF = 2048
        slots = [sb("slot%d" % i, [128, SLOTF], BF16) for i in range(NSLOT)]
        slotb = [Buf("slot%d" % i) for i in range(NSLOT)]
        for i in range(NSLOT):
            getsem("slot%d" % i)

        class Stream:
            def __init__(self):
                self.plan = []
                self.pos = 0
                self.issued = 0

            def get(self, src, off, n):
                if S.collect:
                    self.plan.append((src, off, n))
                    return None, None
                r = self.pos
                assert self.plan[r] == (src, off, n), (r, self.plan[r], (src, off, n))
                self.pos += 1
                while self.issued < len(self.plan) and self.issued <= r + (NSLOT - 1):
                    q = self.issued
                    s2, o2, n2 = self.plan[q]
                    sl = q % NSLOT
                    if s2 == "w":
                        rd = [cvb[i] for i in range(o2 // CB, (o2 + n2 - 1) // CB + 1)]
                        S.dma("sp", slots[sl][:, 0:n2], scr_d[:, o2:o2 + n2], "slot%d" % sl, reads=rd, writes=[slotb[sl]])
                    else:
                        S.dma("sp", slots[sl][:, 0:n2], sscr_d[:, o2:o2 + n2], "slot%d" % sl, reads=[sscrb], writes=[slotb[sl]])
                    self.issued += 1
                sl = r % NSLOT
                return slots[sl], slotb[sl]

        stream = Stream()
        sscrb = Buf("sscr")

        def wget(name):
            o, n = woff[name]
            return stream.get("w", o, n)

        xres = xn = gbuf = xin_t = yout_t = tmpf = ud = qT = xqT = kT = vtm = accN = accD = PT = zt = wt = ta = tb_ = macc = None
        tmpi = [0]

        def tmp():
            tmpi[0] = (tmpi[0] + 1) % 3
            return tmpf[tmpi[0]]

        def alloc_main():
            nonlocal xres, xn, gbuf, xin_t, yout_t, tmpf, ud, qT, xqT, kT, vtm, accN, accD, PT, zt, wt, ta, tb_, macc
            xres = TileB(sb("xres", [128, 8, T], F32), 8)
            xn = TileB(sb("xn", [128, 8, T], BF16), 8)
            gbuf = TileB(sb("gbuf", [128, NJ, T], BF16), NJ)
            xin_t = [TileB(sb("xin%d" % i, [128, D], F32)) for i in range(2)]
            yout_t = xin_t
            tmpf = [TileB(sb("tmpf%d" % i, [128, T], F32)) for i in range(3)]
            ud = TileB(sb("ud", [128, 6, T], BF16), 6)
            qT = TileB(sb("qT", [128, 12, T], BF16), 12)
            S.op("pool", lambda e: e.memset(qT.t[:], 0.0), writes=qT.b)
            xqT = TileB(sb("xqT", [128, 4, T], BF16), 4)
            kT = [TileB(sb("kT%d" % g, [128, 2, RING[g] * T], BF16), RING[g]) for g in range(3)]
            vtm = [TileB(sb("vtm%d" % g, [128, RING[g], 4, 256], BF16), RING[g]) for g in range(3)]
            accN = TileB(sb("accN", [64, T], F32)); accD = TileB(sb("accD", [64, T], F32))
            PT = [TileB(sb("PT%d" % i, [128, 640], BF16)) for i in range(3)]
            zt = TileB(sb("zt", [128, 2, 16, 64], F32)); wt = TileB(sb("wt", [128, 2, 16, 64], F32))
            ta = TileB(sb("ta", [128, 16, 64], F32)); tb_ = TileB(sb("tb", [128, 16, 64], F32))
            macc = TileB(sb("macc", [128, T], F32))
            for i in range(2):
                getsem("xin%d" % i)

        gain = {nm: sc(nm) for nm in ("ffn1_norm", "mix_norm", "ffn2_norm", "mem_norm", "final_norm")}

        def mm(bank, ps_ap, lhsT, rhs, start, stop, reads):
            S.op("pe", lambda e: e.matmul(ps_ap, lhsT, rhs, start=start, stop=stop), reads=reads, writes=[bankb[bank]])

        def rmsnorm(gname):
            g = gain[gname]
            for c in range(8):
                if c % 2 == 0:
                    S.op("act", lambda e, c=c: e.activation(out=xn.t[:, c, :], in_=xres.t[:, c, :], func=AF.Square),
                         reads=[xres.b[c]], writes=[xn.b[c]])
                else:
                    S.op("dve", lambda e, c=c: e.tensor_tensor(out=xn.t[:, c, :], in0=xres.t[:, c, :], in1=xres.t[:, c, :], op=ALU.mult),
                         reads=[xres.b[c]], writes=[xn.b[c]])
            bk = palloc()
            for c in range(8):
                mm(bk, banks[bk][:, :], ones_bf[:], xn.t[:, c, :], c == 0, c == 7, [onesb, xn.b[c]])
            t1 = tmp(); t2 = tmp()
            S.op("act", lambda e: e.activation(out=t1.t[:], in_=banks[bk][:, :], func=AF.Ln, scale=1.0 / D, bias=epsb[:]),
                 reads=[bankb[bk], epsbb], writes=[t1.b[0]])
            pfree(bk)
            S.op("act", lambda e: e.activation(out=t2.t[:], in_=t1.t[:], func=AF.Exp, scale=-0.5),
                 reads=[t1.b[0]], writes=[t2.b[0]])
            for c in range(8):
                S.op("dve", lambda e, c=c: e.scalar_tensor_tensor(out=xn.t[:, c, :], in0=xres.t[:, c, :], scalar=g[:, c:c + 1],
                                                                  in1=t2.t[:], op0=ALU.mult, op1=ALU.mult),
                     reads=[xres.b[c], t2.b[0], smallb], writes=[xn.b[c]])
            return t2

        def ffn(f):
            for j in range(NJ):
                sl, slb = wget("ffn%d_in%d" % (f, j))
                if S.collect:
                    continue
                w = sl[:, 0:2048].rearrange("p (a k c) -> p a k c", a=2, k=8)
                ba = palloc(); bb = palloc()
                if j == 0:
                    for k in range(8):
                        mm(ba, banks[ba][:, :], w[:, 0, k, :], xn.t[:, k, :], k == 0, k == 7, [slb, xn.b[k]])
                        mm(bb, banks[bb][:, :], w[:, 1, k, :], xn.t[:, k, :], k == 0, k == 7, [slb, xn.b[k]])
                else:
                    for k in range(8):
                        mm(ba, banks[ba][:, :], w[:, 0, k, :], xn.t[:, k, :], k == 0, k == 7, [slb, xn.b[k]])
                    for k in range(8):
                        mm(bb, banks[bb][:, :], w[:, 1, k, :], xn.t[:, k, :], k == 0, k == 7, [slb, xn.b[k]])
                t1 = tmp()
                S.op("act", lambda e, ba=ba, t1=t1: e.activation(out=t1.t[:], in_=banks[ba][:, :], func=AF.Silu),
                     reads=[bankb[ba]], writes=[t1.b[0]])
                pfree(ba)
                S.op("dve", lambda e, bb=bb, t1=t1, j=j: e.tensor_tensor(out=gbuf.t[:, j, :], in0=t1.t[:], in1=banks[bb][:, :], op=ALU.mult),
                     reads=[bankb[bb], t1.b[0]], writes=[gbuf.b[j]])
                pfree(bb)
            for m in range(8):
                bk = None
                for hf in range(2):
                    sl, slb = wget("ffn%d_dn%d_%d" % (f, m, hf))
                    if S.collect:
                        continue
                    if bk is None:
                        bk = palloc()
                    for j in range(hf * 11, hf * 11 + 11):
                        mm(bk, banks[bk][:, :], sl[:, (j % 11) * 128:(j % 11 + 1) * 128], gbuf.t[:, j, :], j == 0, j == NJ - 1, [slb, gbuf.b[j]])
                if S.collect:
                    continue
                S.op("dve", lambda e, bk=bk, m=m: e.scalar_tensor_tensor(out=xres.t[:, m, :], in0=banks[bk][:, :], scalar=0.5,
                                                                         in1=xres.t[:, m, :], op0=ALU.mult, op1=ALU.add),
                     reads=[bankb[bk], xres.b[m]], writes=[xres.b[m]])
                pfree(bk)

        x_issued = set()

        def issue_x(ti):
            if ti in x_issued or ti >= n_prev + n_own:
                return
            x_issued.add(ti)
            for j in range(4):
                r0 = ti * T + j * 128
                dst = gbuf.t[:, 4 * j:4 * j + 4, :].rearrange("p a b -> p (a b)").bitcast(F32)
                S.dma("sp", dst, xin_d[r0:r0 + 128, :], getsem("xg%d" % j), writes=[gbuf.b[4 * j + i] for i in range(4)])

        def load_x(ti):
            issue_x(ti)
            for j in range(4):
                xv = gbuf.t[:, 4 * j:4 * j + 4, :].rearrange("p a b -> p (a b)").bitcast(F32)
                xb = [gbuf.b[4 * j + i] for i in range(4)]
                for half in range(2):
                    bk = palloc()
                    for cc in range(4):
                        c = half * 4 + cc
                        S.op("pe", lambda e, bk=bk, cc=cc, c=c, xv=xv: e.transpose(out=banks[bk][:, cc * 128:(cc + 1) * 128],
                                                                                  in_=xv[:, c * 128:(c + 1) * 128], identity=ident),
                             reads=xb + [smallb], writes=[bankb[bk]])
                    eng = "act" if half == 0 else "dve"
                    dst = xres.t[:, half * 4:half * 4 + 4, j * 128:(j + 1) * 128]
                    src = banks[bk][:, :].rearrange("p (c t) -> p c t", c=4)
                    if eng == "act":
                        S.op("act", lambda e, dst=dst, src=src: e.copy(out=dst, in_=src),
                             reads=[bankb[bk]], writes=[xres.b[half * 4 + i] for i in range(4)])
                    else:
                        S.op("dve", lambda e, dst=dst, src=src: e.tensor_copy(out=dst, in_=src),
                             reads=[bankb[bk]], writes=[xres.b[half * 4 + i] for i in range(4)])
                    pfree(bk)

        def store_y(to, src_scaled):
            for j in range(4):
                yt = yout_t[j % 2]
                for half in range(2):
                    bk = palloc()
                    for cc in range(4):
                        c = half * 4 + cc
                        S.op("pe", lambda e, bk=bk, cc=cc, c=c, j=j: e.transpose(out=banks[bk][:, cc * 128:(cc + 1) * 128],
                                                                                in_=src_scaled.t[:, c, j * 128:(j + 1) * 128], identity=ident),
                             reads=[src_scaled.b[c], smallb], writes=[bankb[bk]])
                    dst = yt.t[:, half * 512:(half + 1) * 512]
                    if half == 0:
                        S.op("act", lambda e, dst=dst, bk=bk: e.copy(out=dst, in_=banks[bk][:, :]), reads=[bankb[bk]], writes=[yt.b[0]])
                    else:
                        S.op("dve", lambda e, dst=dst, bk=bk: e.tensor_copy(out=dst, in_=banks[bk][:, :]), reads=[bankb[bk]], writes=[yt.b[0]])
                    pfree(bk)
                r0 = to * T + j * 128
                S.dma("sp", y_d[r0:r0 + 128, :], yt.t[:], "xin%d" % (j % 2), reads=[yt.b[0]])


        BR = CFG.get("branches", ("ssm", "attn", "mem"))
        mkT = TileB(sb("mkT", [128, 4, 256], BF16)); mv = TileB(sb("mv", [128, 2, 512], BF16))
        pti = [0]
        evi = [0]

        def evac_copy(dst, src, reads, writes, scale=None):
            evi[0] += 1
            if evi[0] % 2 == 0:
                if scale is None:
                    S.op("act", lambda e: e.copy(out=dst, in_=src), reads=reads, writes=writes)
                else:
                    S.op("act", lambda e: e.mul(dst, src, scale), reads=reads, writes=writes)
            else:
                if scale is None:
                    S.op("dve", lambda e: e.tensor_copy(out=dst, in_=src), reads=reads, writes=writes)
                else:
                    S.op("dve", lambda e: e.tensor_scalar(out=dst, in0=src, scalar1=scale, scalar2=None, op0=ALU.mult), reads=reads, writes=writes)

        def proj(wname, evac):
            sl, slb = wget(wname)
            if S.collect:
                return
            bk = palloc()
            for k in range(8):
                mm(bk, banks[bk][:, :], sl[:, k * 128:(k + 1) * 128], xn.t[:, k, :], k == 0, k == 7, [slb, xn.b[k]])
            evac(bk)
            pfree(bk)

        def setup_mem(st2):
            if True:
                mt = sb("memt", [128, 2, D], F32, st2)
                mtb = [Buf(), Buf()]
                sq = sb("memsq", [128, D], F32, st2); sqb = Buf()
                ss = sb("memss", [128, 4], F32, st2); ssb = Buf()
                memnT = TileB(sb("memnT", [128, 8, 256], BF16, st2))
                if not S.collect:
                    for mb in range(2):
                        S.dma("sp", mt[:, mb, :], mem_d[mb * 128:(mb + 1) * 128, :], getsem("mem%d" % mb), writes=[mtb[mb]])
                        S.op("dve", lambda e, mb=mb: e.tensor_tensor(out=sq[:], in0=mt[:, mb, :], in1=mt[:, mb, :], op=ALU.mult),
                             reads=[mtb[mb]], writes=[sqb])
                        S.op("dve", lambda e, mb=mb: e.tensor_reduce(out=ss[:, mb:mb + 1], in_=sq[:], axis=AX.X, op=ALU.add),
                             reads=[sqb], writes=[ssb])
                    S.op("act", lambda e: e.activation(out=ss[:, 2:4], in_=ss[:, 0:2], func=AF.Ln, scale=1.0 / D, bias=epsb[:]),
                         reads=[ssb, epsbb], writes=[ssb])
                    S.op("act", lambda e: e.activation(out=ss[:, 0:2], in_=ss[:, 2:4], func=AF.Exp, scale=-0.5), reads=[ssb], writes=[ssb])
                    for mb in range(2):
                        S.op("dve", lambda e, mb=mb: e.tensor_scalar(out=mt[:, mb, :], in0=mt[:, mb, :], scalar1=ss[:, mb:mb + 1], scalar2=None, op0=ALU.mult),
                             reads=[ssb, mtb[mb]], writes=[mtb[mb]])
                    gm = gain["mem_norm"]
                    for mb in range(2):
                        for half in range(2):
                            bk = palloc()
                            for cc in range(4):
                                c = half * 4 + cc
                                S.op("pe", lambda e, bk=bk, cc=cc, c=c, mb=mb: e.transpose(out=banks[bk][:, cc * 128:(cc + 1) * 128],
                                                                                          in_=mt[:, mb, c * 128:(c + 1) * 128], identity=ident),
                                     reads=[mtb[mb], smallb], writes=[bankb[bk]])
                            for cc in range(4):
                                c = half * 4 + cc
                                S.op("dve", lambda e, bk=bk, cc=cc, c=c, mb=mb: e.tensor_scalar(out=memnT.t[:, c, mb * 128:(mb + 1) * 128], in0=banks[bk][:, cc * 128:(cc + 1) * 128],
                                                                                               scalar1=gm[:, c:c + 1], scalar2=None, op0=ALU.mult),
                                     reads=[bankb[bk], smallb], writes=[memnT.b[0]])
                            pfree(bk)
                for h in range(4):
                    sl, slb = wget("xk%d" % h)
                    if S.collect:
                        continue
                    bk = palloc()
                    for k in range(8):
                        mm(bk, banks[bk][:, 0:256], sl[:, k * 128:(k + 1) * 128], memnT.t[:, k, :], k == 0, k == 7, [slb, memnT.b[0]])
                    S.op("act", lambda e, bk=bk, h=h: e.mul(mkT.t[:, h, :], banks[bk][:, 0:256], 128.0 ** -0.5), reads=[bankb[bk]], writes=[mkT.b[0]])
                    pfree(bk)
                bks = None
                for hf in range(2):
                    ssl, sslb = wget("xv%d" % hf)
                    if S.collect:
                        continue
                    if bks is None:
                        bks = [palloc(), palloc()]
                    for k in range(hf * 4, hf * 4 + 4):
                        for mb in range(2):
                            mm(bks[mb], banks[bks[mb]][:, :], memnT.t[:, k, mb * 128:(mb + 1) * 128], ssl[:, (k % 4) * 512:(k % 4 + 1) * 512], k == 0, k == 7, [sslb, memnT.b[0]])
                if not S.collect:
                    for mb in range(2):
                        bk = bks[mb]
                        S.op("act", lambda e, bk=bk, mb=mb: e.copy(out=mv.t[:, mb, :], in_=banks[bk][:, :]), reads=[bankb[bk]], writes=[mv.b[0]])
                        pfree(bk)

        def tokgrp(ap, g, grp):
            if g == 0:
                return ap[:, grp * 128:(grp + 1) * 128]
            return ap.rearrange("p (i r) -> p r i", r=4)[:, grp, :]

        def projections(ti, own):
            slot = [ti % RING[g] for g in range(3)]
            if "ssm" in BR:
                for c6 in range(6):
                    def ev(bk, c6=c6):
                        dst = ud.t[:, c6, :].rearrange("p (s n) -> p n s", s=8)
                        src = banks[bk][:, :].rearrange("p (n s) -> p n s", s=8)
                        evac_copy(dst, src, [bankb[bk]], [ud.b[c6]])
                    proj("pu%d" % c6, ev)
            if "attn" in BR and ti >= n_prev - 4:
                for c in range(6):
                    g = c // 2; cc = c % 2
                    def ev(bk, g=g, cc=cc):
                        evac_copy(kT[g].t[:, cc, slot[g] * T:(slot[g] + 1) * T], banks[bk][:, :], [bankb[bk]], [kT[g].b[slot[g]]])
                    proj("pk%d" % c, ev)
                for g in range(3):
                    sl, slb = wget("pv%d" % g)
                    if S.collect:
                        continue
                    w = sl[:, 0:2048].rearrange("p (k c) -> p k c", k=8)
                    for gg in range(2):
                        bk = palloc()
                        for q2 in range(2):
                            grp = gg * 2 + q2
                            for k in range(8):
                                mm(bk, banks[bk][:, q2 * 256:(q2 + 1) * 256], tokgrp(xn.t[:, k, :], g, grp), w[:, k, :], k == 0, k == 7, [slb, xn.b[k]])
                        dst = vtm[g].t[:, slot[g], gg * 2:gg * 2 + 2, :]
                        src = banks[bk][:, :].rearrange("p (a c) -> p a c", a=2)
                        evac_copy(dst, src, [bankb[bk]], [vtm[g].b[slot[g]]])
                        pfree(bk)
            if not own:
                return
            if "attn" in BR:
                for c in range(6):
                    def ev(bk, c=c):
                        evac_copy(qT.t[0:64, 2 * c, :], banks[bk][0:64, :], [bankb[bk]], [qT.b[2 * c]], scale=0.125)
                        evac_copy(qT.t[64:128, 2 * c + 1, :], banks[bk][64:128, :], [bankb[bk]], [qT.b[2 * c + 1]], scale=0.125)
                    proj("pq%d" % c, ev)
            if "mem" in BR:
                for c in range(4):
                    def ev(bk, c=c):
                        evac_copy(xqT.t[:, c, :], banks[bk][:, :], [bankb[bk]], [xqT.b[c]])
                    proj("pxq%d" % c, ev)

        def attention(ti, mid_cb=None):
            if S.collect:
                return
            accs = {}
            cb = [mid_cb]

            def phase_a(h, g, grp):
                cc = h // 2
                if h == 2 and g == 0 and grp == 0 and cb[0] is not None:
                    cb[0](); cb[0] = None
                if grp == 0:
                    accs[(h, g)] = (palloc(), palloc())
                blocks = []
                for L in range(LMAX[g] + 1):
                    if g == 0:
                        if L == 0:
                            tj, kb = ti, grp
                        else:
                            tj, kb = (ti, grp - 1) if grp >= 1 else (ti - 1, 3)
                    else:
                        tj, kb = ti - L, grp
                    if tj < 0:
                        continue
                    blocks.append((L, tj, kb))
                nb = len(blocks)
                sbk = [palloc() for _ in range((nb + 3) // 4)]
                pt = PT[pti[0] % 3]; pti[0] += 1
                qi = 4 * g + h
                qap = tokgrp(qT.t[:, qi, :], g, grp)
                so = strip_off(g, h)
                for bi4 in range(len(sbk)):
                    n4 = min(4, nb - bi4 * 4)
                    bk = sbk[bi4]
                    L0 = blocks[bi4 * 4][0]
                    mm(bk, banks[bk][:, 0:n4 * 128], ident_bf[:], strips[:, so + L0 * 128: so + (L0 + n4) * 128], True, False, [identbb, stripb])
                for bi, (L, tj, kb) in enumerate(blocks):
                    ks = tj % RING[g]
                    kap = tokgrp(kT[g].t[:, cc, ks * T:(ks + 1) * T], g, kb)
                    bk = sbk[bi // 4]
                    ps = banks[bk][:, (bi % 4) * 128:(bi % 4 + 1) * 128]
                    last = (bi % 4 == 3) or (bi == nb - 1)
                    mm(bk, ps, kap, qap, False, last, [kT[g].b[ks], qT.b[qi]])
                for bi4 in range(len(sbk)):
                    n4 = min(4, nb - bi4 * 4)
                    bk = sbk[bi4]
                    S.op("act", lambda e, bk=bk, n4=n4, bi4=bi4, pt=pt: e.activation(out=pt.t[:, bi4 * 512: bi4 * 512 + n4 * 128], in_=banks[bk][:, 0:n4 * 128], func=AF.Exp),
                         reads=[bankb[bk]], writes=[pt.b[0]])
                    pfree(bk)
                return blocks, pt

            def phase_b(h, g, grp, blocks, pt):
                bN, bD = accs[(h, g)]
                nb = len(blocks)
                for bi, (L, tj, kb) in enumerate(blocks):
                    ks = tj % RING[g]
                    vap = vtm[g].t[:, ks, kb, h * 64:(h + 1) * 64]
                    p_ap = pt.t[:, bi * 128:(bi + 1) * 128]
                    mm(bN, banks[bN][0:64, grp * 128:(grp + 1) * 128], vap, p_ap, bi == 0, bi == nb - 1, [vtm[g].b[ks], pt.b[0]])
                for bi, (L, tj, kb) in enumerate(blocks):
                    if tj >= n_prev:
                        vd, vdb = ones_bf[:, 0:64], onesb
                    else:
                        vd, vdb = flag_bf[:], flagb
                    p_ap = pt.t[:, bi * 128:(bi + 1) * 128]
                    mm(bD, banks[bD][0:64, grp * 128:(grp + 1) * 128], vd, p_ap, bi == 0, bi == nb - 1, [vdb, pt.b[0]])
                if grp != 3:
                    return
                for (acc, bk) in ((accN, bN), (accD, bD)):
                    if g == 0:
                        S.op("dve", lambda e, acc=acc, bk=bk: e.tensor_copy(out=acc.t[:], in_=banks[bk][0:64, :]), reads=[bankb[bk]], writes=[acc.b[0]])
                    else:
                        dst = acc.t[:].rearrange("p (i r) -> p i r", r=4)
                        src = banks[bk][0:64, :].rearrange("p (r i) -> p i r", r=4)
                        S.op("dve", lambda e, dst=dst, src=src: e.tensor_tensor(out=dst, in0=dst, in1=src, op=ALU.add), reads=[bankb[bk], acc.b[0]], writes=[acc.b[0]])
                    pfree(bk)
                if g == 2:
                    S.op("act", lambda e: e.activation(out=accD.t[:], in_=accD.t[:], func=AF.Ln), reads=[accD.b[0]], writes=[accD.b[0]])
                    S.op("act", lambda e: e.activation(out=accD.t[:], in_=accD.t[:], func=AF.Exp, scale=-1.0), reads=[accD.b[0]], writes=[accD.b[0]])
                    S.op("dve", lambda e, h=h: e.tensor_tensor(out=gbuf.t[0:64, 18 + h, :], in0=accN.t[:], in1=accD.t[:], op=ALU.mult),
                         reads=[accN.b[0], accD.b[0]], writes=[gbuf.b[18 + h]])

            pend = []
            SKEW = 2
            for h in range(4):
                for g in range(3):
                    for grp in range(4):
                        pend.append(((h, g, grp), phase_a(h, g, grp)))
                        if len(pend) > SKEW:
                            t_, r_ = pend.pop(0)
                            phase_b(*t_, *r_)
            while pend:
                t_, r_ = pend.pop(0)
                phase_b(*t_, *r_)

        def cross_attention():
            if S.collect:
                return
            accs = {}

            def phase_a(h, mb):
                if mb == 0:
                    accs[h] = (palloc(), palloc())
                bk = palloc()
                mm(bk, banks[bk][:, :], mkT.t[:, h, mb * 128:(mb + 1) * 128], xqT.t[:, h, :], True, True, [mkT.b[0], xqT.b[h]])
                pt = PT[pti[0] % 3]; pti[0] += 1
                S.op("act", lambda e, bk=bk, pt=pt: e.activation(out=pt.t[:, 0:512], in_=banks[bk][:, :], func=AF.Exp), reads=[bankb[bk]], writes=[pt.b[0]])
                pfree(bk)
                return pt

            def phase_b(h, mb, pt):
                bN, bD = accs[h]
                mm(bN, banks[bN][:, :], mv.t[:, mb, h * 128:(h + 1) * 128], pt.t[:, 0:512], mb == 0, mb == 1, [mv.b[0], pt.b[0]])
                mm(bD, banks[bD][:, :], ones_bf[:], pt.t[:, 0:512], mb == 0, mb == 1, [onesb, pt.b[0]])
                if mb != 1:
                    return
                t1 = tmp()
                S.op("act", lambda e, bD=bD, t1=t1: e.activation(out=t1.t[:], in_=banks[bD][:, :], func=AF.Ln), reads=[bankb[bD]], writes=[t1.b[0]])
                pfree(bD)
                S.op("act", lambda e, t1=t1: e.activation(out=t1.t[:], in_=t1.t[:], func=AF.Exp, scale=-1.0), reads=[t1.b[0]], writes=[t1.b[0]])
                S.op("dve", lambda e, bN=bN, t1=t1, h=h: e.tensor_tensor(out=gbuf.t[:, 14 + h, :], in0=banks[bN][:, :], in1=t1.t[:], op=ALU.mult),
                     reads=[bankb[bN], t1.b[0]], writes=[gbuf.b[14 + h]])
                pfree(bN)

            prev = None
            for h in range(4):
                for mb in range(2):
                    cur = ((h, mb), phase_a(h, mb))
                    if prev is not None:
                        phase_b(*prev[0], prev[1])
                    prev = cur
            phase_b(*prev[0], prev[1])


        PI = math.pi
        cos2 = sb("cos2", [128, 16, 64], F32); sin2 = sb("sin2", [128, 16, 64], F32)
        Rtab = sb("Rtab", [128, 16, 64], F32); r2 = sb("r2", [128, 16], F32)
        tabb = Buf("ssmtab")
        Wd = sb("Wd", [128, 6, 8, 96], BF16); Wdb = Buf("Wd")
        XB = TileB(sb("XB", [128, 2, 16, 65], BF16))
        carry = TileB(sb("carry", [128, 2, 16], F32)); tcar = TileB(sb("tcar", [128, 2, 16], F32))
        PDOFF = 0; QOFF = 6 * 2048

        def dv(fn, reads, writes, eng="dve"):
            S.op(eng, fn, reads=reads, writes=writes)

        def setup_ssm(st2):
            if S.collect:
                return
            if True:
                N9 = 16 * 9
                dt_ = sb("dt", [128, 16], F32, st2); lr = sb("lr", [128, 16], F32, st2); li = sb("li", [128, 16], F32, st2)
                ang = sb("ang", [128, 16, 9], F32, st2); mag = sb("mag", [128, 16, 9], F32, st2)
                Are = sb("Are", [128, 16, 9], F32, st2); Aim = sb("Aim", [128, 16, 9], F32, st2)
                sA = sb("sA", [128, 16, 9], F32, st2); cA = sb("cA", [128, 16, 9], F32, st2)
                k1 = sb("k1", [128, 1024], F32, st2); k2 = sb("k2", [128, 1024], F32, st2); ki = sb("ki", [128, 1024], I32, st2)
                ang2 = sb("ang2", [128, 16, 64], F32, st2)
                sm = sb("sm", [128, 8, 16], F32, st2)
                BBr = sb("BBr", [128, 16, 32], F32, st2); BBi = sb("BBi", [128, 16, 32], F32, st2)
                BBbr = sb("BBbr", [128, 16, 32], BF16, st2); BBbi = sb("BBbi", [128, 16, 32], BF16, st2)
                CAb = sb("CAb", [128, 16, 2, 9, 32], BF16, st2)
                PdA = sb("PdA", [128, 6, 2048], BF16, st2); PdAb = Buf("PdA")
                B0 = Buf("ssmsetup")
                ssmp = sb("ssmp", [128, nssmp], F32, st2)
                S.dma("sp", ssmp[:], ssmp_d[:, :], getsem("ssmp"), writes=[B0])
                zz = sb("zz", [128, 1024], BF16, st2); zzb = Buf()
                S.op("pool", lambda e: e.memset(zz[:], 0.0), writes=[zzb])
                for i in range(SSMW // 1024):
                    S.dma("sp", sscr_d[:, i * 1024:(i + 1) * 1024], zz[:], getsem("ssmz"), reads=[zzb])
                sscrb.w = (("dma", "ssmz"), S.dma_cnt["ssmz"])

                def sp_(name):
                    o, n = pcols[name]
                    return ssmp[:, o:o + n]
                are = sp_("are"); aim = sp_("aim"); ldt = sp_("ldt")
                bre = sp_("ssm_b_re").rearrange("p (g c) -> p g c", g=16); bim = sp_("ssm_b_im").rearrange("p (g c) -> p g c", g=16)
                cre_ = sp_("ssm_c_re").rearrange("p (g c) -> p g c", g=16); cim_ = sp_("ssm_c_im").rearrange("p (g c) -> p g c", g=16)
                nidx = sp_("nidx")
                R0 = [B0, smallb]

                def D(fn, eng="dve"):
                    S.op(eng, fn, reads=R0, writes=[B0])

                def bc(ap2, n=32):
                    return ap2.unsqueeze(2).broadcast_to([128, 16, n])

                def sincos(x, n, s_out, c_out):
                    for shift, out in ((0.0, s_out), (PI / 2, c_out)):
                        D(lambda e: e.tensor_scalar(out=k1[:, 0:n], in0=x, scalar1=1.0 / (2 * PI), scalar2=shift / (2 * PI) + 0.5, op0=ALU.mult, op1=ALU.add))
                        D(lambda e: e.tensor_copy(out=ki[:, 0:n], in_=k1[:, 0:n]))
                        D(lambda e: e.tensor_copy(out=k1[:, 0:n], in_=ki[:, 0:n]))
                        D(lambda e: e.scalar_tensor_tensor(out=k2[:, 0:n], in0=k1[:, 0:n], scalar=-2 * PI, in1=x, op0=ALU.mult, op1=ALU.add))
                        if shift != 0.0:
                            D(lambda e: e.tensor_scalar(out=k2[:, 0:n], in0=k2[:, 0:n], scalar1=shift, scalar2=None, op0=ALU.add))
                        D(lambda e: e.tensor_scalar(out=k1[:, 0:n], in0=k2[:, 0:n], scalar1=PI, scalar2=-2 * PI, op0=ALU.is_gt, op1=ALU.mult))
                        D(lambda e: e.tensor_tensor(out=k2[:, 0:n], in0=k2[:, 0:n], in1=k1[:, 0:n], op=ALU.add))
                        D(lambda e: e.tensor_scalar(out=k1[:, 0:n], in0=k2[:, 0:n], scalar1=-PI, scalar2=2 * PI, op0=ALU.is_lt, op1=ALU.mult))
                        D(lambda e: e.tensor_tensor(out=k2[:, 0:n], in0=k2[:, 0:n], in1=k1[:, 0:n], op=ALU.add))
                        D(lambda e: e.tensor_scalar(out=k2[:, 0:n], in0=k2[:, 0:n], scalar1=3.1415925, scalar2=-3.1415925, op0=ALU.min, op1=ALU.max))
                        D(lambda e, out=out: e.activation(out=out, in_=k2[:, 0:n], func=AF.Sin), eng="act")

                D(lambda e: e.activation(out=dt_[:], in_=ldt, func=AF.Exp), eng="act")
                D(lambda e: e.tensor_tensor(out=lr[:], in0=are, in1=dt_[:], op=ALU.mult))
                D(lambda e: e.tensor_tensor(out=li[:], in0=aim, in1=dt_[:], op=ALU.mult))
                kidx = sp_("kidx")
                kb3 = kidx.unsqueeze(1).broadcast_to([128, 16, 9])
                D(lambda e: e.tensor_tensor(out=ang[:], in0=li[:].unsqueeze(2).broadcast_to([128, 16, 9]), in1=kb3, op=ALU.mult))
                D(lambda e: e.tensor_tensor(out=mag[:], in0=lr[:].unsqueeze(2).broadcast_to([128, 16, 9]), in1=kb3, op=ALU.mult))
                D(lambda e: e.activation(out=mag[:], in_=mag[:], func=AF.Exp), eng="act")
                sincos(ang[:].rearrange("p g k -> p (g k)"), N9, sA[:].rearrange("p g k -> p (g k)"), cA[:].rearrange("p g k -> p (g k)"))
                D(lambda e: e.tensor_tensor(out=Are[:], in0=mag[:], in1=cA[:], op=ALU.mult))
                D(lambda e: e.tensor_tensor(out=Aim[:], in0=mag[:], in1=sA[:], op=ALU.mult))
                D(lambda e: e.tensor_scalar(out=sm[:, 0, :], in0=li[:], scalar1=8.0, scalar2=None, op0=ALU.mult))
                D(lambda e: e.tensor_tensor(out=ang2[:], in0=nidx.unsqueeze(1).broadcast_to([128, 16, 64]), in1=sm[:, 0, :].unsqueeze(2).broadcast_to([128, 16, 64]), op=ALU.mult))
                sincos(ang2[:].rearrange("p g n -> p (g n)"), 1024, sin2[:].rearrange("p g n -> p (g n)"), cos2[:].rearrange("p g n -> p (g n)"))
                D(lambda e: e.activation(out=r2[:], in_=lr[:], func=AF.Exp, scale=8.0), eng="act")
                D(lambda e: e.tensor_copy(out=Rtab[:], in_=r2[:].unsqueeze(2).broadcast_to([128, 16, 64])))
                D(lambda e: e.memset(Rtab[:, :, 0:1], 0.0))
                S.op("dve", lambda e: e.memset(carry.t[:], 0.0), writes=[carry.b[0]])
                S.op("dve", lambda e: e.memset(XB.t[:], 0.0), writes=[XB.b[0]])
                nr = sm[:, 1, :]; ni = Aim[:, :, 1]; den = sm[:, 2, :]; t0 = sm[:, 3, :]; cr = sm[:, 4, :]; ci = sm[:, 5, :]; t1_ = sm[:, 6, :]
                D(lambda e: e.tensor_scalar(out=nr, in0=Are[:, :, 1], scalar1=-1.0, scalar2=None, op0=ALU.add))
                D(lambda e: e.tensor_tensor(out=den, in0=are, in1=are, op=ALU.mult))
                D(lambda e: e.tensor_tensor(out=t0, in0=aim, in1=aim, op=ALU.mult))
                D(lambda e: e.tensor_tensor(out=den, in0=den, in1=t0, op=ALU.add))
                D(lambda e: e.reciprocal(out=den, in_=den))
                D(lambda e: e.tensor_tensor(out=cr, in0=nr, in1=are, op=ALU.mult))
                D(lambda e: e.tensor_tensor(out=t0, in0=ni, in1=aim, op=ALU.mult))
                D(lambda e: e.tensor_tensor(out=cr, in0=cr, in1=t0, op=ALU.add))
                D(lambda e: e.tensor_tensor(out=cr, in0=cr, in1=den, op=ALU.mult))
                D(lambda e: e.tensor_tensor(out=ci, in0=ni, in1=are, op=ALU.mult))
                D(lambda e: e.tensor_tensor(out=t0, in0=nr, in1=aim, op=ALU.mult))
                D(lambda e: e.tensor_tensor(out=ci, in0=ci, in1=t0, op=ALU.subtract))
                D(lambda e: e.tensor_tensor(out=ci, in0=ci, in1=den, op=ALU.mult))

                u1 = sb("u1b", [128, 16, 9, 32], F32, st2); u2 = sb("u2b", [128, 16, 9, 32], F32, st2)
                PB = sb("PB", [128, 2, 16, 8, 32], BF16, st2)

                def cmul(outr, outi, ar, ai, br, bi, K, neg_i=False):
                    sh = [128, 16, K, 32]
                    A_r = ar.unsqueeze(3).broadcast_to(sh); A_i = ai.unsqueeze(3).broadcast_to(sh)
                    B_r = br.unsqueeze(2).broadcast_to(sh); B_i = bi.unsqueeze(2).broadcast_to(sh)
                    t1v = u1[:, :, 0:K, :]; t2v = u2[:, :, 0:K, :]
                    D(lambda e: e.tensor_tensor(out=t1v, in0=B_r, in1=A_r, op=ALU.mult))
                    D(lambda e: e.tensor_tensor(out=t2v, in0=B_i, in1=A_i, op=ALU.mult))
                    D(lambda e: e.tensor_tensor(out=outr, in0=t1v, in1=t2v, op=ALU.subtract))
                    D(lambda e: e.tensor_tensor(out=t1v, in0=B_i, in1=A_r, op=ALU.mult))
                    D(lambda e: e.tensor_tensor(out=t2v, in0=B_r, in1=A_i, op=ALU.mult))
                    if neg_i:
                        D(lambda e: e.scalar_tensor_tensor(out=outi, in0=t1v, scalar=-1.0, in1=t2v, op0=ALU.mult, op1=ALU.subtract))
                    else:
                        D(lambda e: e.tensor_tensor(out=outi, in0=t1v, in1=t2v, op=ALU.add))

                cmul(BBr[:].unsqueeze(2), BBi[:].unsqueeze(2), cr.unsqueeze(2), ci.unsqueeze(2), bre, bim, 1)
                D(lambda e: e.tensor_copy(out=BBbr[:], in_=BBr[:]))
                D(lambda e: e.tensor_copy(out=BBbi[:], in_=BBi[:]))
                cmul(CAb[:, :, 0, :, :], CAb[:, :, 1, :, :], Are[:], Aim[:], cre_, cim_, 9, neg_i=True)
                cmul(PB[:, 0, :, :, :], PB[:, 1, :, :, :], Are[:, :, 0:8], Aim[:, :, 0:8], BBr[:], BBi[:], 8)
                for c6 in range(6):
                    npair = 3 if c6 < 5 else 1
                    dst = sscr_d[:, QOFF + c6 * 1536: QOFF + c6 * 1536 + npair * 512].rearrange("p (j c t h) -> p j c t h", j=npair, c=2, t=8)
                    S.dma("act", dst, CAb[:, 3 * c6:3 * c6 + npair, :, 1:9, :], getsem("ssmq"), reads=[B0, sscrb])
                S.op("dve", lambda e: e.memset(Wd[:], 0.0), writes=[Wdb])
                for c6 in range(6):
                    npair = 3 if c6 < 5 else 1
                    bk = palloc()
                    for j in range(npair):
                        G = 3 * c6 + j
                        S.op("pe", lambda e, bk=bk, j=j, G=G: e.matmul(banks[bk][32 * j:32 * j + 32, 0:256], BBbr[:, G, :], CAb[:, G, 0, 0:8, :].rearrange("p k h -> p (k h)"), start=True, stop=False),
                             reads=[B0], writes=[bankb[bk]])
                        S.op("pe", lambda e, bk=bk, j=j, G=G: e.matmul(banks[bk][32 * j:32 * j + 32, 0:256], BBbi[:, G, :], CAb[:, G, 1, 0:8, :].rearrange("p k h -> p (k h)"), start=False, stop=True),
                             reads=[B0], writes=[bankb[bk]])
                    for j in range(npair):
                        S.op("dve", lambda e, bk=bk, c6=c6, j=j: e.tensor_copy(out=Wd[32 * j:32 * j + 32, c6, :, 32 * j:32 * j + 32],
                                                                             in_=banks[bk][32 * j:32 * j + 32, 0:256].rearrange("p (d h) -> p d h", d=8)),
                             reads=[bankb[bk]], writes=[Wdb])
                    pfree(bk)
                for s_ in range(8):
                    for c in (0, 1):
                        bA = palloc(); bB = palloc()
                        for G in range(16):
                            c6 = G // 3; j = G % 3
                            bk, col = (bA, c6 * 128) if c6 < 4 else (bB, (c6 - 4) * 128)
                            S.op("pe", lambda e, bk=bk, col=col, j=j, G=G, c=c, s_=s_: e.matmul(banks[bk][32 * j:32 * j + 32, col:col + 128], PB[:, c, G, 7 - s_, :], ident_bf[:], start=True, stop=True),
                                 reads=[B0, identbb], writes=[bankb[bk]])
                        o = (c * 8 + s_) * 128
                        S.op("dve", lambda e, bA=bA, o=o: e.tensor_copy(out=PdA[0:96, 0:4, o:o + 128], in_=banks[bA][0:96, :].rearrange("p (a c) -> p a c", a=4)), reads=[bankb[bA]], writes=[PdAb])
                        S.op("act", lambda e, bB=bB, o=o: e.copy(out=PdA[0:96, 4, o:o + 128], in_=banks[bB][0:96, 0:128]), reads=[bankb[bB]], writes=[PdAb])
                        S.op("act", lambda e, bB=bB, o=o: e.copy(out=PdA[0:32, 5, o:o + 128], in_=banks[bB][0:32, 128:256]), reads=[bankb[bB]], writes=[PdAb])
                        pfree(bA); pfree(bB)
                for c6 in range(6):
                    nr_ = 96 if c6 < 5 else 32
                    S.dma("act", sscr_d[0:nr_, PDOFF + c6 * 2048:PDOFF + (c6 + 1) * 2048], PdA[0:nr_, c6, :], getsem("ssmpa"), reads=[PdAb, sscrb])
                S.op("dve", lambda e: e.memset(tcar.t[:], 0.0), reads=[B0], writes=[tabb, tcar.b[0]])

        def ssm_tile(ti, own):
            zb = None
            for c6 in range(6):
                sl, slb = stream.get("s", PDOFF + c6 * 2048, 2048)
                if S.collect:
                    continue
                if zb is None:
                    zb = [palloc() for _ in range(4)]
                w = sl[:, 0:2048].rearrange("p (c s m) -> p c s m", c=2, s=8)
                npair = 3 if c6 < 5 else 1
                for j in range(npair):
                    G = 3 * c6 + j
                    for c in range(2):
                        bk = zb[c * 2 + G // 8]
                        ps = banks[bk][:, (G % 8) * 64:(G % 8 + 1) * 64]
                        for s_ in range(8):
                            mm(bk, ps, w[32 * j:32 * j + 32, c, s_, :], ud.t[32 * j:32 * j + 32, c6, s_ * 64:(s_ + 1) * 64], s_ == 0, s_ == 7, [slb, ud.b[c6]])
            if not S.collect:
                for c in range(2):
                    for hh in range(2):
                        bk = zb[c * 2 + hh]
                        evac_copy(zt.t[:, c, hh * 8:(hh + 1) * 8, :], banks[bk][:, :].rearrange("p (g n) -> p g n", g=8), [bankb[bk]], [zt.b[0]])
                        pfree(bk)
                zr = zt.t[:, 0, :, :]; zi = zt.t[:, 1, :, :]
                RT = [tabb]
                dv(lambda e: e.tensor_tensor(out=ta.t[:], in0=zr, in1=cos2[:], op=ALU.mult), [zt.b[0]] + RT, [ta.b[0]], "pool")
                dv(lambda e: e.tensor_tensor(out=tb_.t[:], in0=zi, in1=sin2[:], op=ALU.mult), [zt.b[0]] + RT, [tb_.b[0]], "pool")
                dv(lambda e: e.tensor_tensor(out=wt.t[:, 0, :, :], in0=ta.t[:], in1=tb_.t[:], op=ALU.add), [ta.b[0], tb_.b[0]], [wt.b[0]], "pool")
                dv(lambda e: e.tensor_tensor(out=ta.t[:], in0=zi, in1=cos2[:], op=ALU.mult), [zt.b[0]] + RT, [ta.b[0]], "pool")
                dv(lambda e: e.tensor_tensor(out=tb_.t[:], in0=zr, in1=sin2[:], op=ALU.mult), [zt.b[0]] + RT, [tb_.b[0]], "pool")
                dv(lambda e: e.tensor_tensor(out=wt.t[:, 1, :, :], in0=ta.t[:], in1=tb_.t[:], op=ALU.subtract), [ta.b[0], tb_.b[0]], [wt.b[0]], "pool")
                def part2():
                    dv(lambda e: e.tensor_tensor(out=tcar.t[:], in0=carry.t[:], in1=r2[:].unsqueeze(1).broadcast_to([128, 2, 16]), op=ALU.mult), [carry.b[0]] + RT, [tcar.b[0]])
                    dv(lambda e: e.tensor_tensor(out=wt.t[:, :, :, 0], in0=wt.t[:, :, :, 0], in1=tcar.t[:], op=ALU.add), [tcar.b[0], wt.b[0]], [wt.b[0]])
                    for c in range(2):
                        wf = wt.t[:, c, :, :].rearrange("p g n -> p (g n)")
                        dv(lambda e, wf=wf: e.tensor_tensor_scan(out=wf, data0=Rtab[:].rearrange("p g n -> p (g n)"), data1=wf, initial=0.0, op0=ALU.mult, op1=ALU.add),
                           [wt.b[0]] + RT, [wt.b[0]])
                    wr = wt.t[:, 0, :, :]; wi = wt.t[:, 1, :, :]
                    dv(lambda e: e.tensor_tensor(out=ta.t[:], in0=wr, in1=cos2[:], op=ALU.mult), [wt.b[0]] + RT, [ta.b[0]], "pool")
                    dv(lambda e: e.tensor_tensor(out=tb_.t[:], in0=wi, in1=sin2[:], op=ALU.mult), [wt.b[0]] + RT, [tb_.b[0]], "pool")
                    dv(lambda e: e.tensor_tensor(out=zt.t[:, 0, :, :], in0=ta.t[:], in1=tb_.t[:], op=ALU.subtract), [ta.b[0], tb_.b[0]], [zt.b[0]], "pool")
                    dv(lambda e: e.tensor_tensor(out=ta.t[:], in0=wi, in1=cos2[:], op=ALU.mult), [wt.b[0]] + RT, [ta.b[0]], "pool")
                    dv(lambda e: e.tensor_tensor(out=tb_.t[:], in0=wr, in1=sin2[:], op=ALU.mult), [wt.b[0]] + RT, [tb_.b[0]], "pool")
                    dv(lambda e: e.tensor_tensor(out=zt.t[:, 1, :, :], in0=ta.t[:], in1=tb_.t[:], op=ALU.add), [ta.b[0], tb_.b[0]], [zt.b[0]], "pool")
                    if own:
                        S.op("pool", lambda e: e.tensor_copy(out=XB.t[:, :, :, 0], in_=carry.t[:]), reads=[carry.b[0]], writes=[XB.b[0]])
                        S.op("pool", lambda e: e.tensor_copy(out=XB.t[:, :, :, 1:65], in_=zt.t[:]), reads=[zt.b[0]], writes=[XB.b[0]])
                    S.op("pool", lambda e: e.tensor_copy(out=carry.t[:], in_=zt.t[:, :, :, 63]), reads=[zt.b[0]], writes=[carry.b[0]])
                return part2
            return None

        def ssm_out():
            dvec = sc("dvec")
            for c6 in range(6):
                npair = 3 if c6 < 5 else 1
                sl, slb = stream.get("s", QOFF + c6 * 1536, npair * 512)
                if S.collect:
                    continue
                q = sl[:, 0:npair * 512].rearrange("p (j c t h) -> p j c t h", j=npair, c=2, t=8)
                bk = palloc()
                nrw = 32 * npair
                for dl in range(8):
                    mm(bk, banks[bk][0:nrw, dl * 64:512], Wd[0:nrw, c6, dl, 0:nrw], ud.t[0:nrw, c6, 0:(8 - dl) * 64], dl == 0, False, [Wdb, ud.b[c6]])
                for j in range(npair):
                    G = 3 * c6 + j
                    for t in range(8):
                        ps = banks[bk][32 * j:32 * j + 32, t * 64:(t + 1) * 64]
                        for c in range(2):
                            mm(bk, ps, q[:, j, c, t, :], XB.t[:, c, G, 0:64], False, (t == 7 and c == 1 and j == npair - 1), [slb, XB.b[0]])
                nr_ = 32 * npair
                t1 = tmp()
                S.op("dve", lambda e, bk=bk, t1=t1, c6=c6, nr_=nr_: e.scalar_tensor_tensor(out=t1.t[0:nr_, :], in0=ud.t[0:nr_, c6, :], scalar=dvec[0:nr_, c6:c6 + 1], in1=banks[bk][0:nr_, :],
                                                                                           op0=ALU.mult, op1=ALU.add),
                     reads=[bankb[bk], ud.b[c6], smallb], writes=[t1.b[0]])
                pfree(bk)
                S.op("act", lambda e, t1=t1, c6=c6, nr_=nr_: e.activation(out=gbuf.t[0:nr_, 8 + c6, :].rearrange("p (n t) -> p t n", t=8), in_=t1.t[0:nr_, :].rearrange("p (t n) -> p t n", t=8),
                                                                         func=AF.Gelu_apprx_tanh),
                     reads=[t1.b[0]], writes=[gbuf.b[8 + c6]])


        def gate(b, c):
            res = []
            def ev(bk):
                t1 = tmp()
                S.op("act", lambda e: e.activation(out=t1.t[:], in_=banks[bk][:, :], func=AF.Sigmoid), reads=[bankb[bk]], writes=[t1.b[0]])
                res.append(t1)
            proj("pg%d_%d" % (b, c), ev)
            return res[0] if res else None

        def merge_and_out():
            for c in range(8):
                first = [True]
                def accumulate(prod_fn, reads):
                    pass
                nbr = 0
                if "ssm" in BR:
                    sg = gate(0, c)
                    sl, slb = wget("glu%d" % c)
                    if not S.collect:
                        w = sl[:, 0:1536].rearrange("p (a k c) -> p a k c", a=2, k=6)
                        ba = palloc(); bb = palloc()
                        for ab, bk in ((0, ba), (1, bb)):
                            for k6 in range(6):
                                nr = 96 if k6 < 5 else 32
                                mm(bk, banks[bk][:, :], w[0:nr, ab, k6, :], gbuf.t[0:nr, 8 + k6, :], k6 == 0, k6 == 5, [slb, gbuf.b[8 + k6]])
                        t1 = tmp()
                        S.op("act", lambda e, bb=bb, t1=t1: e.activation(out=t1.t[:], in_=banks[bb][:, :], func=AF.Sigmoid), reads=[bankb[bb]], writes=[t1.b[0]])
                        pfree(bb)
                        S.op("pool", lambda e, t1=t1, sg=sg: e.tensor_tensor(out=t1.t[:], in0=t1.t[:], in1=sg.t[:], op=ALU.mult), reads=[t1.b[0], sg.b[0]], writes=[t1.b[0]])
                        S.op("dve", lambda e, ba=ba, t1=t1: e.tensor_tensor(out=macc.t[:], in0=banks[ba][:, :], in1=t1.t[:], op=ALU.mult), reads=[bankb[ba], t1.b[0]], writes=[macc.b[0]])
                        pfree(ba)
                    nbr += 1
                for (bname, bi, wn, kk, src0) in (("attn", 1, "aup%d" % c, 64, 18), ("mem", 2, "xup%d" % c, 128, 14)):
                    if bname not in BR:
                        continue
                    sg = gate(bi, c)
                    sl, slb = wget(wn)
                    if not S.collect:
                        w = sl[:, 0:512].rearrange("p (k c) -> p k c", k=4)
                        bk = palloc()
                        for k4 in range(4):
                            mm(bk, banks[bk][:, :], w[0:kk, k4, :], gbuf.t[0:kk, src0 + k4, :], k4 == 0, k4 == 3, [slb, gbuf.b[src0 + k4]])
                        if nbr == 0:
                            S.op("dve", lambda e, bk=bk, sg=sg: e.tensor_tensor(out=macc.t[:], in0=banks[bk][:, :], in1=sg.t[:], op=ALU.mult), reads=[bankb[bk], sg.b[0]], writes=[macc.b[0]])
                        else:
                            S.op("dve", lambda e, bk=bk, sg=sg: e.tensor_tensor(out=sg.t[:], in0=banks[bk][:, :], in1=sg.t[:], op=ALU.mult), reads=[bankb[bk], sg.b[0]], writes=[sg.b[0]])
                            S.op("pool", lambda e, sg=sg: e.tensor_tensor(out=macc.t[:], in0=macc.t[:], in1=sg.t[:], op=ALU.add), reads=[macc.b[0], sg.b[0]], writes=[macc.b[0]])
                        pfree(bk)
                    nbr += 1
                if not S.collect:
                    S.op("act", lambda e, c=c: e.copy(out=gbuf.t[:, c, :], in_=macc.t[:]), reads=[macc.b[0]], writes=[gbuf.b[c]])
            for c2 in range(8):
                sl, slb = wget("wo%d" % c2)
                if S.collect:
                    continue
                bk = palloc()
                for k in range(8):
                    mm(bk, banks[bk][:, :], sl[:, k * 128:(k + 1) * 128], gbuf.t[:, k, :], k == 0, k == 7, [slb, gbuf.b[k]])
                S.op("dve", lambda e, bk=bk, c2=c2: e.tensor_tensor(out=xres.t[:, c2, :], in0=banks[bk][:, :], in1=xres.t[:, c2, :], op=ALU.add),
                     reads=[bankb[bk], xres.b[c2]], writes=[xres.b[c2]])
                pfree(bk)


        def final_out(to):
            if S.collect:
                return
            g = gain["final_norm"]
            for c in range(8):
                if c % 2 == 0:
                    S.op("act", lambda e, c=c: e.activation(out=xn.t[:, c, :], in_=xres.t[:, c, :], func=AF.Square), reads=[xres.b[c]], writes=[xn.b[c]])
                else:
                    S.op("dve", lambda e, c=c: e.tensor_tensor(out=xn.t[:, c, :], in0=xres.t[:, c, :], in1=xres.t[:, c, :], op=ALU.mult), reads=[xres.b[c]], writes=[xn.b[c]])
            bk = palloc()
            for c in range(8):
                mm(bk, banks[bk][:, :], ones_bf[:], xn.t[:, c, :], c == 0, c == 7, [onesb, xn.b[c]])
            t1 = tmp(); t2 = tmp()
            S.op("act", lambda e: e.activation(out=t1.t[:], in_=banks[bk][:, :], func=AF.Ln, scale=1.0 / D, bias=epsb[:]), reads=[bankb[bk], epsbb], writes=[t1.b[0]])
            pfree(bk)
            S.op("act", lambda e: e.activation(out=t2.t[:], in_=t1.t[:], func=AF.Exp, scale=-0.5), reads=[t1.b[0]], writes=[t2.b[0]])
            for c in range(8):
                S.op("dve", lambda e, c=c: e.scalar_tensor_tensor(out=xres.t[:, c, :], in0=xres.t[:, c, :], scalar=g[:, c:c + 1], in1=t2.t[:], op0=ALU.mult, op1=ALU.mult),
                     reads=[xres.b[c], t2.b[0], smallb], writes=[xres.b[c]])
            store_y(to, xres)

        def body():
            with ExitStack() as st2:
                if S.collect:
                    if "mem" in BR and stage != "ffn1":
                        setup_mem(st2)
                else:
                    if "ssm" in BR and stage != "ffn1":
                        setup_ssm(st2)
                    if "mem" in BR and stage != "ffn1":
                        setup_mem(st2)
                if not S.collect:
                    S.barrier()
            if not S.collect:
                alloc_main()
            early = False
            for ti in range(n_prev + n_own):
                own = ti >= n_prev
                if not S.collect and not early:
                    load_x(ti)
                    rmsnorm("ffn1_norm")
                early = False
                ffn(1)
                if stage == "ffn1":
                    if own and not S.collect:
                        store_y(ti - n_prev, xres)
                    continue
                if not S.collect:
                    rmsnorm("mix_norm")
                projections(ti, own)
                if (not own) and ti + 1 < n_prev + n_own and "ssm" in BR:
                    if not S.collect:
                        load_x(ti + 1)
                        rmsnorm("ffn1_norm")
                    early = True
                ssm_p2 = None
                if "ssm" in BR:
                    ssm_p2 = ssm_tile(ti, own)
                    if ssm_p2 is not None and not (own and "attn" in BR):
                        ssm_p2(); ssm_p2 = None
                if not S.collect and ti <= 5:
                    rem = ncb - ncb_first
                    conv_issue(ncb_first + ((ti + 1) * rem + 5) // 6)
                if not own:
                    continue
                if "attn" in BR:
                    attention(ti, ssm_p2)
                if "mem" in BR:
                    cross_attention()
                if "ssm" in BR:
                    ssm_out()
                merge_and_out()
                if not S.collect:
                    rmsnorm("ffn2_norm")
                ffn(2)
                if not S.collect:
                    issue_x(ti + 1)
                final_out(ti - n_prev)

        S.collect = True
        body()
        S.collect = False
        body()
        S.barrier(final=True)
        S.emit(nc, esem, dsem)
    return nc


_CACHE = {}


def kernel(**inp):
    inp = {k: np.asarray(v) for k, v in inp.items()}
    n_prev = CFG["n_prev"]; n_own = CFG["n_own"]
    W = build_wall(inp)
    wall = W.build(8192)
    small, scols, ssmp, pcols = build_small(inp)
    key = (n_prev, n_own, CFG["stage"], tuple(CFG.get("branches", ())))
    if key not in _CACHE:
        _CACHE[key] = build_program(W.off, W.total, scols, small.shape[1], pcols, ssmp.shape[1], W.first_end)
    nc = _CACHE[key]
    x = inp["x"]; mem = inp["mem"]
    in_maps = []
    for core in range(8):
        b = core // 2; half = core % 2
        own = x[b, half * HALF: half * HALF + n_own * T]
        if half == 1:
            prev = x[b, HALF - n_prev * T: HALF]
            flag = np.ones((128, 64), np.float32)
        else:
            prev = np.zeros((n_prev * T, D), np.float32)
            flag = np.zeros((128, 64), np.float32)
        xin = np.ascontiguousarray(np.concatenate([prev, own], axis=0))
        in_maps.append({"xin": xin, "wall": wall, "small": small, "ssmp": ssmp, "flag": flag,
                        "mem": np.ascontiguousarray(mem[b])})
    res = run_bass_kernel_spmd(nc, in_maps, core_ids=list(range(8)))
    out = np.zeros((4, SEQ, D), np.float32)
    for core in range(8):
        b = core // 2; half = core % 2
        out[b, half * HALF: half * HALF + n_own * T] = res.results[core]["y"]
    return out
```
